# Optimizing a Trainium2 kernel written in Bass

```python
import math
import jax, jax.numpy as jnp
from jax import lax
import numpy as np

D_MODEL = 1024
BATCH = 16
SEQ = 4096
DEPTH = 1
DEC_BATCH = 8
DEC_SEQ = 8192
PAST_LEN = 128

GRID_W = 64
D_ATTN = D_MODEL // 2
D_HYENA = D_MODEL - D_ATTN
D_IN = 3 * D_ATTN + 3 * D_HYENA
NA_HEADS = 8
NA_HEAD_DIM = D_ATTN // NA_HEADS
WIN_H_MAX = 8
WIN_W = 16
Q_COL_BLOCK = 16
K_COL_BLOCK = Q_COL_BLOCK + WIN_W
N_COL_BLOCKS = GRID_W // Q_COL_BLOCK
HY_ORDER = 2
HY_SHORT = 3
HY_BANDS = 8
HY_EMB = 1 + 2 * HY_BANDS
HY_FILTER_FFN = 64
HY_FAST_DECAY = 0.3
HY_SLOW_DECAY = 1.5
HY_TARGET = 1e-2
D_FF = ((8 * D_MODEL // 3 + 127) // 128) * 128
N_MOD = 9
EPS = 1e-6
NEG_INF = -1e30

kernel_name = "hybrid_natten_hyena_macaron_encoder"


def rmsnorm(x, g):
    xf = x.astype(jnp.float32)
    xf = xf * lax.rsqrt(jnp.mean(xf * xf, axis=-1, keepdims=True) + EPS)
    return xf.astype(x.dtype) * g


def swiglu(x, w_gate, w_up, w_down):
    return (jax.nn.silu(x @ w_gate) * (x @ w_up)) @ w_down


def neighbourhood_attention(q, k, v, rpb):
    b, seq_len, n_heads, head_dim = q.shape
    rows = seq_len // GRID_W
    win_h = min(WIN_H_MAX, rows)
    r = np.arange(rows)
    row_idx = np.clip(r - win_h // 2, 0, rows - win_h)[:, None] + np.arange(win_h)[None, :]
    qcol = np.arange(GRID_W).reshape(N_COL_BLOCKS, Q_COL_BLOCK)
    kcol_start = np.clip(qcol[:, 0] - WIN_W // 2, 0, GRID_W - K_COL_BLOCK)
    col_idx = kcol_start[:, None] + np.arange(K_COL_BLOCK)[None, :]
    qwin_start = np.clip(qcol - WIN_W // 2, 0, GRID_W - WIN_W)
    kc = col_idx[:, None, :]
    col_ok = (kc >= qwin_start[..., None]) & (kc < qwin_start[..., None] + WIN_W)
    d_row = row_idx - r[:, None] + WIN_H_MAX - 1
    d_col = np.clip(kc - qcol[..., None] + WIN_W - 1, 0, 2 * WIN_W - 2)
    bias = rpb[:, d_row[:, None, None, :, None], d_col[None, :, :, None, :]].astype(jnp.float32)
    bias = jnp.where(col_ok[None, None, :, :, None, :], bias, NEG_INF)
    scale = head_dim ** -0.5
    qg = q.reshape(b, rows, N_COL_BLOCKS, Q_COL_BLOCK, n_heads, head_dim)
    kg = k.reshape(b, rows, GRID_W, n_heads, head_dim)
    vg = v.reshape(b, rows, GRID_W, n_heads, head_dim)
    ri = row_idx[:, None, :, None]
    ci = col_idx[None, :, None, :]

    def one_sequence(args):
        qs, ks, vs = args
        kw = ks[ri, ci]
        vw = vs[ri, ci]
        s = jnp.einsum("rjqhd,rjakhd->hrjqak", qs, kw, preferred_element_type=jnp.float32) * scale + bias
        p = jax.nn.softmax(s.reshape(s.shape[:4] + (-1,)), axis=-1).reshape(s.shape)
        return jnp.einsum("hrjqak,rjakhd->rjqhd", p.astype(vs.dtype), vw)

    out = lax.map(one_sequence, (qg, kg, vg))
    return out.reshape(b, seq_len, n_heads * head_dim)


def short_conv(x, w, bias):
    seq_len = x.shape[1]
    pad = HY_SHORT // 2
    xp = jnp.pad(x, ((0, 0), (pad, pad), (0, 0)))
    out = bias
    for i in range(HY_SHORT):
        out = out + xp[:, i:i + seq_len] * w[i]
    return out


def hyena_filters(seq_len, w1, b1, w2, b2, w3, b3, wo, freq):
    t = jnp.linspace(0.0, 1.0, seq_len, dtype=jnp.float32)[:, None]
    w = 2.0 * math.pi * jnp.arange(seq_len, dtype=jnp.float32)[:, None] / seq_len
    f = jnp.linspace(1e-4, HY_BANDS - 1, HY_BANDS, dtype=jnp.float32)[None, :]
    z = jnp.concatenate([t, jnp.cos(f * w), -jnp.sin(f * w)], axis=-1)
    h = jnp.sin(freq * (z @ w1 + b1))
    h = jnp.sin(freq * (h @ w2 + b2))
    h = jnp.sin(freq * (h @ w3 + b3))
    h = (h @ wo).astype(jnp.float32).reshape(seq_len, 2, HY_ORDER, D_HYENA)
    deltas = jnp.abs(jnp.linspace(math.log(HY_TARGET) / HY_FAST_DECAY,
                                  math.log(HY_TARGET) / HY_SLOW_DECAY, D_HYENA, dtype=jnp.float32))
    h = h * jnp.exp(-t * deltas)[:, None, None, :]
    h_fwd = h[:, 0]
    h_bwd = h[:-1, 1]
    l1 = jnp.sum(jnp.abs(h_fwd), axis=0) + jnp.sum(jnp.abs(h_bwd), axis=0)
    return h_fwd / l1, h_bwd / l1


def two_sided_fftconv(u, h_fwd, h_bwd):
    seq_len, ch = h_fwd.shape
    k = jnp.concatenate([h_fwd, jnp.zeros((1, ch), jnp.float32), h_bwd[::-1]], axis=0)
    k_f = jnp.fft.rfft(k, axis=0)
    u_f = jnp.fft.rfft(u.astype(jnp.float32), n=2 * seq_len, axis=1)
    y = jnp.fft.irfft(u_f * k_f[None], n=2 * seq_len, axis=1)[:, :seq_len]
    return y.astype(u.dtype)


def hyena_mixer(u, conv_w, conv_b, w1, b1, w2, b2, w3, b3, wo, freq, skip):
    seq_len = u.shape[1]
    u = short_conv(u, conv_w, conv_b)
    parts = jnp.split(u, HY_ORDER + 1, axis=-1)
    h_fwd, h_bwd = hyena_filters(seq_len, w1, b1, w2, b2, w3, b3, wo, freq)
    z = parts[0]
    for n in range(HY_ORDER):
        z = parts[n + 1] * (two_sided_fftconv(z, h_fwd[:, n], h_bwd[:, n]) + skip[n] * z)
    return z


def encoder_layer(x, c, p):
    b, seq_len, _ = x.shape
    mod = (jax.nn.silu(c) @ p["w_ada"] + p["b_ada"]).reshape(b, N_MOD, 1, D_MODEL)
    sh1, sc1, g1, sh2, sc2, g2, sh3, sc3, g3 = (mod[:, i] for i in range(N_MOD))
    hn = rmsnorm(x, p["ffn1_norm"]) * (1.0 + sc1) + sh1
    x = x + 0.5 * g1 * swiglu(hn, p["ffn1_w_gate"], p["ffn1_w_up"], p["ffn1_w_down"])
    hn = rmsnorm(x, p["mix_norm"]) * (1.0 + sc2) + sh2
    proj = hn @ p["w_in"]
    q, k, v, hy_in = jnp.split(proj, [D_ATTN, 2 * D_ATTN, 3 * D_ATTN], axis=-1)
    shp = (b, seq_len, NA_HEADS, NA_HEAD_DIM)
    attn = neighbourhood_attention(q.reshape(shp), k.reshape(shp), v.reshape(shp), p["na_rpb"])
    hy = hyena_mixer(hy_in, p["hy_conv_w"], p["hy_conv_b"], p["hy_w1"], p["hy_b1"], p["hy_w2"],
                     p["hy_b2"], p["hy_w3"], p["hy_b3"], p["hy_wo"], p["hy_sin_freq"], p["hy_skip"])
    mixed = jnp.concatenate([rmsnorm(attn, p["attn_out_norm"]), rmsnorm(hy, p["hy_out_norm"])], axis=-1)
    x = x + g2 * (mixed @ p["w_out"])
    hn = rmsnorm(x, p["ffn2_norm"]) * (1.0 + sc3) + sh3
    x = x + 0.5 * g3 * swiglu(hn, p["ffn2_w_gate"], p["ffn2_w_up"], p["ffn2_w_down"])
    return x


def setup_inputs(seed: int = 0) -> dict:
    key = jax.random.key(seed)
    ks = iter(jax.random.split(key, 48))

    def normal(shape, scale):
        return jax.random.normal(next(ks), shape, jnp.float32) * scale

    def gain(shape):
        return 1.0 + 0.05 * jax.random.normal(next(ks), shape, jnp.float32)

    L = DEPTH
    return {
        "x_prompt": normal((BATCH, SEQ, D_MODEL), 1.0),
        "x_sample": normal((DEC_BATCH, DEC_SEQ, D_MODEL), 1.0),
        "c_prompt": normal((BATCH, D_MODEL), 1.0),
        "c_sample": normal((DEC_BATCH, D_MODEL), 1.0),
        "w_ada": normal((L, D_MODEL, N_MOD * D_MODEL), 0.5 * D_MODEL ** -0.5),
        "b_ada": normal((L, N_MOD * D_MODEL), 0.02),
        "ffn1_norm": gain((L, D_MODEL)),
        "ffn1_w_gate": normal((L, D_MODEL, D_FF), D_MODEL ** -0.5),
        "ffn1_w_up": normal((L, D_MODEL, D_FF), D_MODEL ** -0.5),
        "ffn1_w_down": normal((L, D_FF, D_MODEL), D_FF ** -0.5),
        "mix_norm": gain((L, D_MODEL)),
        "w_in": normal((L, D_MODEL, D_IN), D_MODEL ** -0.5),
        "na_rpb": normal((L, NA_HEADS, 2 * WIN_H_MAX - 1, 2 * WIN_W - 1), 0.1),
        "hy_conv_w": normal((L, HY_SHORT, 3 * D_HYENA), HY_SHORT ** -0.5),
        "hy_conv_b": normal((L, 3 * D_HYENA), 0.02),
        "hy_w1": normal((L, HY_EMB, HY_FILTER_FFN), HY_EMB ** -0.5),
        "hy_b1": normal((L, HY_FILTER_FFN), 0.1),
        "hy_w2": normal((L, HY_FILTER_FFN, HY_FILTER_FFN), HY_FILTER_FFN ** -0.5),
        "hy_b2": normal((L, HY_FILTER_FFN), 0.1),
        "hy_w3": normal((L, HY_FILTER_FFN, HY_FILTER_FFN), HY_FILTER_FFN ** -0.5),
        "hy_b3": normal((L, HY_FILTER_FFN), 0.1),
        "hy_wo": normal((L, HY_FILTER_FFN, 2 * HY_ORDER * D_HYENA), HY_FILTER_FFN ** -0.5),
        "hy_sin_freq": gain((L, HY_FILTER_FFN)),
        "hy_skip": normal((L, HY_ORDER, D_HYENA), 0.5),
        "attn_out_norm": gain((L, D_ATTN)),
        "hy_out_norm": gain((L, D_HYENA)),
        "w_out": normal((L, D_MODEL, D_MODEL), D_MODEL ** -0.5),
        "ffn2_norm": gain((L, D_MODEL)),
        "ffn2_w_gate": normal((L, D_MODEL, D_FF), D_MODEL ** -0.5),
        "ffn2_w_up": normal((L, D_MODEL, D_FF), D_MODEL ** -0.5),
        "ffn2_w_down": normal((L, D_FF, D_MODEL), D_FF ** -0.5),
        "final_norm": gain((D_MODEL,)),
    }


def reference(x_prompt, x_sample, c_prompt, c_sample, w_ada, b_ada, ffn1_norm, ffn1_w_gate,
              ffn1_w_up, ffn1_w_down, mix_norm, w_in, na_rpb, hy_conv_w, hy_conv_b, hy_w1, hy_b1,
              hy_w2, hy_b2, hy_w3, hy_b3, hy_wo, hy_sin_freq, hy_skip, attn_out_norm, hy_out_norm,
              w_out, ffn2_norm, ffn2_w_gate, ffn2_w_up, ffn2_w_down, final_norm):
    def trunk(x, c):
        for i in range(DEPTH):
            p = {
                "w_ada": w_ada[i], "b_ada": b_ada[i],
                "ffn1_norm": ffn1_norm[i], "ffn1_w_gate": ffn1_w_gate[i],
                "ffn1_w_up": ffn1_w_up[i], "ffn1_w_down": ffn1_w_down[i],
                "mix_norm": mix_norm[i], "w_in": w_in[i], "na_rpb": na_rpb[i],
                "hy_conv_w": hy_conv_w[i], "hy_conv_b": hy_conv_b[i],
                "hy_w1": hy_w1[i], "hy_b1": hy_b1[i], "hy_w2": hy_w2[i], "hy_b2": hy_b2[i],
                "hy_w3": hy_w3[i], "hy_b3": hy_b3[i], "hy_wo": hy_wo[i],
                "hy_sin_freq": hy_sin_freq[i], "hy_skip": hy_skip[i],
                "attn_out_norm": attn_out_norm[i], "hy_out_norm": hy_out_norm[i], "w_out": w_out[i],
                "ffn2_norm": ffn2_norm[i], "ffn2_w_gate": ffn2_w_gate[i],
                "ffn2_w_up": ffn2_w_up[i], "ffn2_w_down": ffn2_w_down[i],
            }
            x = encoder_layer(x, c, p)
        return rmsnorm(x, final_norm)

    y_prompt = trunk(x_prompt, c_prompt)
    y_sample = trunk(x_sample, c_sample)
    return (y_prompt, y_sample)
```

```python
import contextlib
import math
import numpy as np
import ml_dtypes
import concourse.bass as bass
import concourse.mybir as mybir
from concourse.bass_utils import run_bass_kernel_spmd

F32 = mybir.dt.float32
BF16 = mybir.dt.bfloat16
ALU = mybir.AluOpType
AF = mybir.ActivationFunctionType

ENGS = ("pe", "act", "dve", "pool", "sp")
D = 1024
DFF = 2816
NJF = DFF // 128
EPS = 1e-6
MAG = 12582912.0
TWO_PI = 2.0 * math.pi


class Counter:
    def __init__(self, name, step, epoch):
        self.name, self.step, self.epoch = name, step, epoch
        self.n = 0
        self.sems = []

    def nxt(self):
        self.n += 1
        return self.n

    def nsems(self):
        return max(1, (self.n + self.epoch - 1) // self.epoch)

    def loc(self, n):
        return (n - 1) // self.epoch, ((n - 1) % self.epoch + 1) * self.step


_CUR = [None]


class Buf:
    __slots__ = ("name", "lw", "rd", "_ctr", "ap")

    def __init__(self, name, ap=None):
        self.name = name
        self.lw = None
        self.rd = {}
        self._ctr = None
        self.ap = ap

    @property
    def ctr(self):
        if self._ctr is None:
            self._ctr = _CUR[0].dma_counter(self.name)
        return self._ctr

    @ctr.setter
    def ctr(self, v):
        self._ctr = v


class Sync:
    def __init__(self):
        self.ops = {e: [] for e in ENGS}
        self.cnt = {e: Counter("c_" + e, 1, 30000) for e in ENGS}
        self.known = {e: {} for e in ENGS}
        self.counters = list(self.cnt.values())
        self.nd = 0
        self.free = []
        self.phase_ctrs = []
        _CUR[0] = self

    def dma_counter(self, name):
        if self.free:
            c = self.free.pop()
        else:
            self.nd += 1
            c = Counter(f"d{self.nd}", 16, 1800)
            self.counters.append(c)
        self.phase_ctrs.append(c)
        return c

    def phase_begin(self):
        self.phase_ctrs = []

    def phase_end(self):
        self.barrier()
        self.free.extend(self.phase_ctrs)
        self.phase_ctrs = []

    def _wait(self, eng, ctr, n):
        k = self.known[eng]
        if k.get(ctr, 0) >= n:
            return
        k[ctr] = n
        self.ops[eng].append(("w", ctr, n))

    def _deps(self, eng, reads, writes):
        deps = {}
        for b in reads:
            if b.lw is not None:
                c, n = b.lw
                if deps.get(c, 0) < n:
                    deps[c] = n
        for b in writes:
            if b.lw is not None:
                c, n = b.lw
                if deps.get(c, 0) < n:
                    deps[c] = n
            for c, n in b.rd.items():
                if deps.get(c, 0) < n:
                    deps[c] = n
        mine = self.cnt[eng]
        for c, n in deps.items():
            if c is mine and eng == "pe":
                continue
            self._wait(eng, c, n)

    def _commit(self, ctr, n, reads, writes):
        for b in reads:
            if b.rd.get(ctr, 0) < n:
                b.rd[ctr] = n
        for b in writes:
            b.lw = (ctr, n)
            b.rd = {}

    def ins(self, eng, fn, reads=(), writes=()):
        self._deps(eng, reads, writes)
        ctr = self.cnt[eng]
        n = ctr.nxt()
        self.ops[eng].append(("i", fn, ctr, n))
        self._commit(ctr, n, reads, writes)

    def dma(self, eng, out_ap, in_ap, ctr, reads=(), writes=()):
        self._deps(eng, reads, writes)
        n = ctr.nxt()
        self.ops[eng].append(("i", lambda e: e.dma_start(out=out_ap, in_=in_ap), ctr, n))
        self._commit(ctr, n, reads, writes)

    def barrier(self):
        for e in ENGS:
            for c in self.counters:
                if c.n > 0:
                    self._wait(e, c, c.n)

    def emit(self, nc):
        for c in self.counters:
            if c.n > 0:
                self._wait("sp", c, c.n)
        with contextlib.ExitStack() as st:
            for c in self.counters:
                c.sems = [st.enter_context(nc.semaphore(f"{c.name}_{i}")) for i in range(c.nsems())]
            block = st.enter_context(nc.Block())

            def run(eng_name):
                def f(e):
                    for op in self.ops[eng_name]:
                        if op[0] == "w":
                            _, c, n = op
                            ei, v = c.loc(n)
                            e.wait_ge(c.sems[ei], v)
                        else:
                            _, fn, c, n = op
                            ei, _v = c.loc(n)
                            fn(e).then_inc(c.sems[ei], c.step)
                return f

            block.tensor(run("pe"))
            block.scalar(run("act"))
            block.vector(run("dve"))
            block.gpsimd(run("pool"))
            block.sync(run("sp"))


def MM(out, lhsT, rhs, start, stop):
    return lambda e: e.matmul(out, lhsT, rhs, start=start, stop=stop)


def TR(out, in_, ident):
    return lambda e: e.transpose(out, in_, ident)


def ACT(out, in_, func, bias=0.0, scale=1.0):
    return lambda e: e.activation(out, in_, func, bias=bias, scale=scale)


def TTO(out, in0, in1, op):
    return lambda e: e.tensor_tensor(out, in0, in1, op=op)


def TS(out, in0, s1, s2, op0, op1=None):
    if op1 is None:
        return lambda e: e.tensor_scalar(out, in0, s1, None, op0=op0)
    return lambda e: e.tensor_scalar(out, in0, s1, s2, op0=op0, op1=op1)


def STT(out, in0, scalar, in1, op0, op1, accum_out=None):
    if accum_out is None:
        return lambda e: e.scalar_tensor_tensor(out, in0, scalar, in1, op0=op0, op1=op1)
    return lambda e: e.scalar_tensor_tensor(out, in0, scalar, in1, op0=op0, op1=op1, accum_out=accum_out)


def CP(out, in_):
    return lambda e: e.tensor_copy(out, in_)


def ACP(out, in_):
    return lambda e: e.copy(out, in_)


def RECIP(out, in_):
    return lambda e: e.reciprocal(out, in_)


def MSET(ap, v):
    return lambda e: e.memset(ap, v)


def bcast_rows(ap, p=128):
    pairs = [list(x) for x in ap.ap]
    return bass.AP(ap.tensor, ap.offset, [[0, p]] + pairs[1:])


def signal_defs(LB):
    m = np.arange(LB)
    jm1 = np.maximum(m - 1, 0)
    sig = [
        (0, m, LB, False),
        (1, jm1, LB, True),
        (0, m, 2 * LB, False),
        (1, jm1, 2 * LB, True),
        (1, LB - 1 - m, 2 * LB, False),
        (1, LB - 1 + m, 2 * LB, True),
        (0, LB + m, 2 * LB, False),
        (0, np.where(m == 0, LB, LB - m), 2 * LB, True),
    ]
    return sig


def host_consts(LB):
    TC = LB // 128
    KC = LB // 128
    N = 2 * LB
    c = {}
    c["ident"] = np.eye(128, dtype=np.float32)
    c["jrev"] = np.ascontiguousarray(np.eye(64, dtype=np.float32)[::-1])
    qc = np.arange(64)[None, :]
    kc = np.arange(64)[:, None]
    st = np.clip(qc - 8, 0, 48)
    cm = ((kc >= st) & (kc < st + 16)).astype(np.float32)
    c["cmask"] = np.concatenate([cm, cm], 0)
    deltas = np.abs(np.linspace(math.log(1e-2) / 0.3, math.log(1e-2) / 1.5, 512, dtype=np.float32)).astype(np.float32)
    c["deltas"] = deltas[None, :]
    sigs = signal_defs(LB)
    zf = np.zeros((8, 17, LB), np.float32)
    negt = np.zeros((128, 8, TC), np.float32)
    f = np.linspace(1e-4, 7.0, 8, dtype=np.float32)[None, :].astype(np.float64)
    for i, (dr, pos, L, mk) in enumerate(sigs):
        t = pos.astype(np.float64) / (L - 1)
        w = 2.0 * math.pi * pos.astype(np.float64)[:, None] / L
        z = np.concatenate([t[:, None], np.cos(f * w), -np.sin(f * w)], -1)
        zf[i] = z.T.astype(np.float32)
        negt[:, i, :] = (-t).astype(np.float32).reshape(TC, 128).T
    c["zfeat"] = zf
    c["negt"] = negt
    N2 = LB // 64
    bf = ml_dtypes.bfloat16
    t1 = np.arange(64, dtype=np.float64)[:, None, None]
    t2 = np.arange(N2, dtype=np.float64)[None, :, None]
    k1 = np.arange(64, dtype=np.float64)[None, None, :]
    th = np.mod((2.0 * math.pi / N) * ((k1 + 0.5) * (N2 * t1 + t2)), 2.0 * math.pi)
    F1 = np.concatenate([np.cos(th), -np.sin(th)], -1)
    c["F1"] = np.ascontiguousarray(F1.reshape(64, N2 * 128)).astype(bf)
    psi = th.transpose(2, 1, 0)
    F2i = (2.0 / N) * np.concatenate([np.cos(psi), -np.sin(psi)], 0)
    c["F2i"] = np.ascontiguousarray(F2i.reshape(128, N2 * 64)).astype(bf)
    a2 = np.arange(N2, dtype=np.float64)
    ph = np.mod((2.0 * math.pi / N2) * np.outer(a2, a2), 2.0 * math.pi)
    C2, S2 = np.cos(ph), np.sin(ph)
    sm = np.stack([np.concatenate([C2, -S2], 1), np.concatenate([S2, C2], 1), np.concatenate([C2, C2], 1),
                   np.concatenate([S2, S2], 1), np.concatenate([-S2, -S2], 1), np.concatenate([-C2, -C2], 1)], 1)
    c["smalls"] = np.ascontiguousarray(sm.reshape(N2, 12 * N2)).astype(bf)
    CC = np.concatenate([C2, C2], 1)
    SS = np.concatenate([S2, S2], 1)
    sm2 = np.stack([np.concatenate([CC, CC], 0), np.concatenate([SS, SS], 0),
                    np.concatenate([-SS, SS], 0), np.concatenate([CC, -CC], 0)], 1)
    c["smalls2"] = np.ascontiguousarray(sm2.reshape(2 * N2, 8 * N2)).astype(bf)
    G1 = np.block([[C2, S2], [-S2, C2]])
    G2 = np.block([[-S2, C2], [-C2, -S2]])
    c["Gm"] = np.ascontiguousarray(np.stack([G1, G2], 1).reshape(2 * N2, 4 * N2)).astype(bf)
    negt2 = np.zeros((64, 8, N2), np.float32)
    for i, (dr, pos, L, mk) in enumerate(sigs):
        tt = pos.astype(np.float64) / (L - 1)
        negt2[:, i, :] = (-tt).astype(np.float32).reshape(64, N2)
    c["negt2"] = negt2
    del c["negt"]
    return c


CONST_SHAPES = None


def build_program(LB, stages=("ffn", "attn", "hyena"), debug=False):
    nc = bass.Bass("TRN2", target_bir_lowering=False)
    S = Sync()
    TC = LB // 128
    KC = LB // 128
    NT = 4 * LB
    TT = 512
    NTILES = NT // TT
    seqs = [(0, 1), (LB, 1), (2 * LB, 2)]
    NSEQ = 3

    def tile_seq(tok):
        return 0 if tok < LB else (1 if tok < 2 * LB else 2)

    def din(name, shape, dt=F32):
        return nc.dram_tensor(name, list(shape), dt, kind="ExternalInput").ap()

    def dscr(name, shape, dt):
        if debug and name in ("s_qk", "s_v", "s_mixa", "s_hyp", "s_hyc", "s_v0", "s_z1", "s_z2", "s_hs2"):
            return nc.dram_tensor(name, list(shape), dt, kind="ExternalOutput").ap()
        return nc.dram_tensor(name, list(shape), dt).ap()

    xin = din("xin", [NT, D])
    cT = din("cT", [128, 8, NSEQ])
    w_ada = din("w_ada", [D, 9 * D])
    b_adaT = din("b_adaT", [128, 72])
    nrm = din("nrm", [128, 4, 8])
    wsrc = {
        "f1g": din("f1g", [D, DFF]), "f1u": din("f1u", [D, DFF]), "f1d": din("f1d", [DFF, D]),
        "f2g": din("f2g", [D, DFF]), "f2u": din("f2u", [D, DFF]), "f2d": din("f2d", [DFF, D]),
        "win": din("win", [D, 3072]), "wout": din("wout", [D, D]),
    }
    rpbp = din("rpbp", [8, 15, 128])
    convw = din("convw", [1, 3 * 1536])
    convb = din("convb", [1, 1536])
    hw1 = din("hw1", [17, 64])
    hw2 = din("hw2", [64, 64])
    hw3 = din("hw3", [64, 64])
    hwo = din("hwo", [64, 2048])
    hb = din("hb", [64, 4])
    skipv = din("skipv", [1, 1024])
    anorm = din("anorm", [1, 512])
    hnorm = din("hnorm", [1, 512])
    ident_d = din("ident", [128, 128])
    jrev_d = din("jrev", [64, 64])
    cmask_d = din("cmask", [128, 64])
    deltas_d = din("deltas", [1, 512])
    zfeat_d = din("zfeat", [8, 17, LB])
    NN2 = LB // 64
    negt2_d = din("negt2", [64, 8, NN2])
    F1_d = din("F1", [64, NN2 * 128], BF16)
    F2i_d = din("F2i", [128, NN2 * 64], BF16)
    smalls_d = din("smalls", [NN2, 12 * NN2], BF16)
    smalls2_d = din("smalls2", [2 * NN2, 8 * NN2], BF16)
    G_d = din("Gm", [2 * NN2, 4 * NN2], BF16)
    yout = nc.dram_tensor("yout", [NT, D], F32, kind="ExternalOutput").ap()

    Wl = {
        "f1g": dscr("s_f1g", [NJF, 128, 8 * 128], BF16), "f1u": dscr("s_f1u", [NJF, 128, 8 * 128], BF16),
        "f1d": dscr("s_f1d", [8, 128, NJF * 128], BF16),
        "f2g": dscr("s_f2g", [NJF, 128, 8 * 128], BF16), "f2u": dscr("s_f2u", [NJF, 128, 8 * 128], BF16),
        "f2d": dscr("s_f2d", [8, 128, NJF * 128], BF16),
        "winl": dscr("s_winl", [8, 128, 8 * 128], BF16), "winr": dscr("s_winr", [4, 128, 8 * 512], BF16),
        "wout": dscr("s_wout", [8, 128, 8 * 128], BF16),
    }
    X1 = dscr("s_x1", [8, 128, NT], F32)
    QK = dscr("s_qk", [8, 128, NT], BF16)
    Vd = dscr("s_v", [NT, 640], BF16)
    HYP = dscr("s_hyp", [NT, 1536], F32)
    HYC = dscr("s_hyc", [NT, 1536], F32)
    V0 = dscr("s_v0", [NT, 512], BF16)
    Z1 = dscr("s_z1", [NT, 512], BF16)
    Z2 = dscr("s_z2", [NT, 512], F32)
    MIXA = dscr("s_mixa", [NT, 512], BF16)
    HS2 = dscr("s_hs2", [4, 2, 2, 2 * NN2, 64, 512], BF16)
    RL1 = dscr("s_rl1", [2, 2, 64, 512], F32)
    B1 = dscr("s_b1", [2, 128, NN2, 512], BF16)
    B2 = dscr("s_b2", [2, 2 * NN2, 64, 512], BF16)

    es_all = contextlib.ExitStack()

    uid = [0]

    def alloc(es, name, shape, dt, dmac=True):
        uid[0] += 1
        name = f"sb{uid[0]}_{name}"
        t = es.enter_context(nc.sbuf_tensor(name, list(shape), dt))
        b = Buf(name, t.ap())
        return b

    P = es_all
    ident = alloc(P, "ident", [128, 128], F32)
    identb = alloc(P, "identb", [128, 128], BF16, False)
    ones = alloc(P, "ones", [128, 128], BF16, False)
    epst = alloc(P, "epst", [128, 1], F32, False)
    par = alloc(P, "par", [128, NSEQ, 9, 8], F32, False)
    gfin = alloc(P, "gfin", [128, 8], F32)
    banks = []
    for i in range(8):
        t = nc.alloc_psum_tensor(f"bank{i}", [128, 512], F32)
        banks.append(Buf(f"bank{i}", t.ap()))
    bank_i = [0]

    def nextbank():
        b = banks[bank_i[0] % 7]
        bank_i[0] += 1
        return b

    ev_i = [0]

    def evac(out_ap, in_ap, reads, writes):
        ev_i[0] += 1
        if ev_i[0] % 2:
            S.ins("act", ACP(out_ap, in_ap), reads, writes)
        else:
            S.ins("dve", CP(out_ap, in_ap), reads, writes)

    S.dma("sp", ident.ap, ident_d, ident.ctr, writes=[ident])
    S.ins("dve", CP(identb.ap, ident.ap), [ident], [identb])
    S.ins("pool", MSET(ones.ap, 1.0), (), [ones])
    S.ins("pool", MSET(epst.ap, EPS), (), [epst])
    S.dma("sp", gfin.ap, nrm[:, 3, :], gfin.ctr, writes=[gfin])

    with contextlib.ExitStack() as es:
        S.phase_begin()
        c_sb = alloc(es, "c_sb", [128, 8, NSEQ], F32)
        sc_sb = alloc(es, "sc_sb", [128, 8, NSEQ], F32, False)
        bada = alloc(es, "bada", [128, 72], F32)
        nrm_sb = alloc(es, "nrm_sb", [128, 4, 8], F32)
        modsb = alloc(es, "modsb", [128, 72, NSEQ], F32, False)
        wslab = [alloc(es, f"wslab{i}", [128, 8, 512], F32) for i in range(2)]
        S.dma("sp", c_sb.ap, cT, c_sb.ctr, writes=[c_sb])
        S.dma("sp", bada.ap, b_adaT, bada.ctr, writes=[bada])
        S.dma("sp", nrm_sb.ap, nrm, nrm_sb.ctr, writes=[nrm_sb])
        S.ins("act", ACT(sc_sb.ap, c_sb.ap, AF.Silu), [c_sb], [sc_sb])
        pm = banks[7]
        for sl in range(18):
            ws = wslab[sl % 2]
            S.dma("sp", ws.ap, w_ada[:, sl * 512:(sl + 1) * 512].rearrange("(k p) c -> p k c", p=128), ws.ctr, writes=[ws])
            for jj in range(4):
                j = sl * 4 + jj
                for kc in range(8):
                    S.ins("pe", MM(pm.ap[:, j * NSEQ:(j + 1) * NSEQ], ws.ap[:, kc, jj * 128:(jj + 1) * 128],
                                   sc_sb.ap[:, kc, :], kc == 0, kc == 7), [ws, sc_sb], [pm])
        pmv = pm.ap[:, 0:72 * NSEQ].rearrange("p (j s) -> p j s", s=NSEQ)
        for s in range(NSEQ):
            S.ins("dve", TTO(modsb.ap[:, :, s], pmv[:, :, s], bada.ap, ALU.add), [pm, bada], [modsb])
        for s in range(NSEQ):
            for sub in range(3):
                sh = modsb.ap[:, (sub * 3 + 0) * 8:(sub * 3 + 0) * 8 + 8, s]
                scl = modsb.ap[:, (sub * 3 + 1) * 8:(sub * 3 + 1) * 8 + 8, s]
                g = modsb.ap[:, (sub * 3 + 2) * 8:(sub * 3 + 2) * 8 + 8, s]
                S.ins("dve", STT(par.ap[:, s, sub * 3 + 0, :], scl, 1.0, nrm_sb.ap[:, sub, :], ALU.add, ALU.mult),
                      [modsb, nrm_sb], [par])
                S.ins("dve", CP(par.ap[:, s, sub * 3 + 1, :], sh), [modsb], [par])
                S.ins("dve", TS(par.ap[:, s, sub * 3 + 2, :], g, 1.0 if sub == 1 else 0.5, None, ALU.mult), [modsb], [par])
        S.phase_end()

    with contextlib.ExitStack() as es:
        S.phase_begin()
        f32s = [alloc(es, f"wc32_{i}", [128, 2816], F32) for i in range(2)]
        b16s = [alloc(es, f"wc16_{i}", [128, 2816], BF16) for i in range(2)]
        it = [0]

        def cast_w(src, K, col0, ncols, dst, TW):
            KCn = K // 128
            CW = 256 if KCn == 8 else 128
            for cs in range(col0, col0 + ncols, CW):
                a = f32s[it[0] % 2]
                b = b16s[it[0] % 2]
                it[0] += 1
                av = a.ap[:, 0:KCn * CW].rearrange("p (k c) -> p k c", c=CW)
                bv = b.ap[:, 0:KCn * CW].rearrange("p (k c) -> p k c", c=CW)
                S.dma("sp", av, src[:, cs:cs + CW].rearrange("(k p) c -> p k c", p=128), a.ctr, writes=[a])
                if it[0] % 2:
                    S.ins("act", ACP(bv, av), [a], [b])
                else:
                    S.ins("dve", CP(bv, av), [a], [b])
                rel = cs - col0
                if TW == 128:
                    for g in range(CW // 128):
                        j = (rel + g * 128) // 128
                        dv = dst[j].rearrange("p (k c) -> p k c", c=128)
                        S.dma("pool", dv, bv[:, :, g * 128:(g + 1) * 128], b.ctr, reads=[b])
                else:
                    g = rel // 512
                    off = rel % 512
                    dv = dst[g].rearrange("p (k c) -> p k c", c=512)
                    S.dma("pool", dv[:, :, off:off + CW], bv, b.ctr, reads=[b])

        if "ffn" in stages:
            for nm in ("f1g", "f1u", "f2g", "f2u"):
                cast_w(wsrc[nm], D, 0, DFF, Wl[nm], 128)
            for nm in ("f1d", "f2d"):
                cast_w(wsrc[nm], DFF, 0, D, Wl[nm], 128)
            cast_w(wsrc["win"], D, 0, 1024, Wl["winl"], 128)
            cast_w(wsrc["win"], D, 1024, 2048, Wl["winr"], 512)
            cast_w(wsrc["wout"], D, 0, D, Wl["wout"], 128)
        S.phase_end()

    N2 = LB // 64
    TCH = min(8, N2)
    KCH = 4
    if "hyena" in stages:
        with contextlib.ExitStack() as es:
            S.phase_begin()
            w1s = alloc(es, "w1s", [17, 64], F32)
            w2s = alloc(es, "w2s", [64, 64], F32)
            w3s = alloc(es, "w3s", [64, 64], F32)
            hbs = alloc(es, "hbs", [64, 4], F32)
            wob = alloc(es, "wob", [64, 2048], BF16, False)
            with contextlib.ExitStack() as esw:
                wo32 = alloc(esw, "wo32", [64, 2048], F32)
                S.dma("sp", wo32.ap, hwo, wo32.ctr, writes=[wo32])
                S.ins("dve", CP(wob.ap, wo32.ap), [wo32], [wob])
                S.barrier()
            dlt = alloc(es, "dlt", [64, 512], F32)
            ngt = alloc(es, "ngt", [64, 8, N2], F32)
            F1s = alloc(es, "F1s", [64, N2 * 128], BF16)
            sm = alloc(es, "sm", [N2, 12 * N2], BF16)
            S.dma("sp", w1s.ap, hw1, w1s.ctr, writes=[w1s])
            S.dma("sp", w2s.ap, hw2, w2s.ctr, writes=[w2s])
            S.dma("sp", w3s.ap, hw3, w3s.ctr, writes=[w3s])
            S.dma("sp", hbs.ap, hb, hbs.ctr, writes=[hbs])
            S.dma("sp", dlt.ap, bcast_rows(deltas_d, 64), dlt.ctr, writes=[dlt])
            S.dma("sp", ngt.ap, negt2_d, ngt.ctr, writes=[ngt])
            S.dma("sp", F1s.ap, F1_d, F1s.ctr, writes=[F1s])
            S.dma("sp", sm.ap, smalls_d, sm.ctr, writes=[sm])
            sm2 = alloc(es, "sm2", [2 * N2, 8 * N2], BF16)
            S.dma("sp", sm2.ap, smalls2_d, sm2.ctr, writes=[sm2])
            sm2v = sm2.ap.rearrange("p (s c) -> p s c", s=4)
            F1v = F1s.ap.rearrange("p (t c) -> p t c", c=128)
            smv = sm.ap.rearrange("p (s c) -> p s c", s=6)
            hbf = alloc(es, "hbf", [64, 3], F32, False)
            for li_ in range(3):
                S.ins("dve", TTO(hbf.ap[:, li_:li_ + 1], hbs.ap[:, li_:li_ + 1], hbs.ap[:, 3:4], ALU.mult), [hbs], [hbf])
            h3T = [alloc(es, f"h3T{i}", [64, LB], BF16, False) for i in range(6)]
            NCH = LB // 512
            GRP = min(4, NCH)
            zt = [alloc(es, f"zt{i}", [17, 512], F32) for i in range(GRP)]
            ta = [alloc(es, f"hta{i}", [64, 512], F32, False) for i in range(GRP)]
            tb = [alloc(es, f"htb{i}", [64, 512], F32, False) for i in range(GRP)]
            lay = [alloc(es, f"lay{i}", [64, GRP * 512], F32, False) for i in range(2)]
            layb = [[Buf(f"lay{i}_{g}") for g in range(GRP)] for i in range(2)]
            dec = [alloc(es, f"dec{i}", [64, 512], F32, False) for i in range(3)]
            hb16 = [alloc(es, f"hb16_{i}", [64, 512], BF16, False) for i in range(8)]
            rl1 = alloc(es, "rl1", [128, 512], F32, False)
            habs = [alloc(es, f"habs{i}", [64, 512], BF16, False) for i in range(4)]
            cnt_ha = 0
            stg = [alloc(es, f"fstg{i}", [128, TCH, 512], BF16) for i in range(2)]
            Ach = [alloc(es, f"fA{i}", [2 * N2, KCH, 512], BF16) for i in range(4)]
            Achb = [[Buf(f"fAh{i}_{k}") for k in range(2)] for i in range(4)]
            hst = [alloc(es, f"fhst{i}", [2 * N2, KCH, 512], BF16) for i in range(4)]
            sigs = signal_defs(LB)
            cnt = {"z": 0, "d": 0, "h": 0, "s": 0, "a": 0, "t": 0, "ha": 0}

            def hidden(sig, slot):
                for g0 in range(0, NCH, GRP):
                    chunks = list(range(g0, min(NCH, g0 + GRP)))
                    for gi, c in enumerate(chunks):
                        S.dma("sp", zt[gi].ap, zfeat_d[sig, :, c * 512:(c + 1) * 512], zt[gi].ctr, writes=[zt[gi]])
                    for li, wsb in enumerate((w1s, w2s, w3s)):
                        pbs = []
                        for gi, c in enumerate(chunks):
                            pb = nextbank()
                            if li == 0:
                                cur, curb = zt[gi].ap, zt[gi]
                            else:
                                cur, curb = lay[li - 1].ap[:, gi * 512:(gi + 1) * 512], layb[li - 1][gi]
                            S.ins("pe", MM(pb.ap[0:64, :], wsb.ap, cur, True, True), [wsb, curb], [pb])
                            pbs.append(pb)
                        for gi, c in enumerate(chunks):
                            S.ins("act", ACT(ta[gi].ap, pbs[gi].ap[0:64, :], AF.Identity, bias=hbf.ap[:, li:li + 1], scale=hbs.ap[:, 3:4]),
                                  [pbs[gi], hbs, hbf], [ta[gi]])
                        for gi, c in enumerate(chunks):
                            S.ins("dve", TS(tb[gi].ap, ta[gi].ap, 1.0 / TWO_PI, MAG, ALU.mult, ALU.add), [ta[gi]], [tb[gi]])
                        for gi, c in enumerate(chunks):
                            S.ins("dve", TS(tb[gi].ap, tb[gi].ap, MAG, None, ALU.subtract), [tb[gi]], [tb[gi]])
                        for gi, c in enumerate(chunks):
                            S.ins("dve", STT(ta[gi].ap, tb[gi].ap, -TWO_PI, ta[gi].ap, ALU.mult, ALU.add), [ta[gi], tb[gi]], [ta[gi]])
                        for gi, c in enumerate(chunks):
                            S.ins("dve", TS(ta[gi].ap, ta[gi].ap, math.pi, -math.pi, ALU.min, ALU.max), [ta[gi]], [ta[gi]])
                        for gi, c in enumerate(chunks):
                            if li < 2:
                                S.ins("act", ACT(lay[li].ap[:, gi * 512:(gi + 1) * 512], ta[gi].ap, AF.Sin), [ta[gi]], [layb[li][gi]])
                            else:
                                S.ins("act", ACT(h3T[slot].ap[:, c * 512:(c + 1) * 512], ta[gi].ap, AF.Sin), [ta[gi]], [h3T[slot]])

            def genA(sig, slot, order, t2, mask_after):
                dr = sigs[sig][0]
                off = (dr * 2 + order) * 512
                pb = nextbank()
                hv = h3T[slot].ap.rearrange("p (a b) -> p a b", b=N2)[:, :, t2]
                S.ins("pe", MM(pb.ap[0:64, :], hv, wob.ap[:, off:off + 512], True, True), [h3T[slot], wob], [pb])
                d = dec[cnt["d"] % 3]
                cnt["d"] += 1
                S.ins("act", ACT(d.ap, dlt.ap, AF.Exp, scale=ngt.ap[:, sig, t2:t2 + 1]), [dlt, ngt], [d])
                h = hb16[cnt["s"] % 8]
                cnt["s"] += 1
                S.ins("dve", TTO(h.ap, pb.ap[0:64, :], d.ap, ALU.mult), [pb, d], [h])
                return h

            def pipeline(n, stages, lag):
                st = [dict() for _ in range(n)]
                ns = len(stages)
                for step in range(n + (ns - 1) * lag):
                    for si in reversed(range(ns)):
                        k = step - si * lag
                        if 0 <= k < n:
                            stages[si](k, st[k])

            LAG = 2
            for fs in range(2):
                if fs == 0:
                    sl = {0: 0, 1: 1}
                    l1sigs = [(0, False), (1, True)]
                    pieces = [(0, 0, 1)]
                else:
                    sl = {2: 0, 3: 1, 4: 2, 5: 3, 6: 4, 7: 5}
                    l1sigs = [(2, False), (6, False), (3, True), (5, False)]
                    pieces = [(1, 2, 3), (2, 4, 5), (3, 6, 7)]
                for sg, slot in sl.items():
                    hidden(sg, slot)
                for order in range(2):
                    l1b = banks[7]
                    l1set = {sg: mk for (sg, mk) in l1sigs}
                    l1order = [sg for (pidx_, sa_, sb_) in pieces for sg in (sa_, sb_) if sg in l1set]
                    l1first, l1last = l1order[0], l1order[-1]
                    for (pidx, sa, sbb) in pieces:
                        units = [(ab, sg, msk, t2) for ab, (sg, msk) in enumerate(((sa, sigs[sa][3]), (sbb, True))) for t2 in range(N2)]

                        def pA(k, stt, units=units, order=order):
                            ab, sg, msk, t2 = units[k]
                            h = genA(sg, sl[sg], order, t2, False)
                            late_mask = (sg in l1set) and (not l1set[sg]) and msk
                            if msk and t2 == 0 and not late_mask:
                                S.ins("dve", MSET(h.ap[0:1, :], 0.0), (), [h])
                            stt["h"] = h
                            stt["late"] = late_mask and t2 == 0

                        def pB(k, stt, units=units):
                            ab, sg, msk, t2 = units[k]
                            xb = stt["h"]
                            if sg in l1set:
                                ha = habs[cnt["ha"] % 4]
                                cnt["ha"] += 1
                                S.ins("dve", STT(ha.ap, xb.ap, -1.0, xb.ap, ALU.mult, ALU.max), [xb], [ha])
                                stt["ha"] = ha
                            if stt["late"]:
                                S.ins("dve", MSET(xb.ap[0:1, :], 0.0), (), [xb])
                            stt["xb"] = xb

                        def pC(k, stt, units=units):
                            ab, sg, msk, t2 = units[k]
                            j = t2 % TCH
                            if j == 0:
                                stt["st"] = stg[cnt["t"] % 2]
                                cnt["t"] += 1
                                cnt["cur_st"] = stt["st"]
                            st_ = cnt["cur_st"]
                            pb = nextbank()
                            S.ins("pe", MM(pb.ap, F1v[:, t2, :], stt["xb"].ap, True, True), [F1s, stt["xb"]], [pb])
                            S.ins("act", ACP(st_.ap[:, j, :], pb.ap), [pb], [st_])
                            if j == TCH - 1:
                                t2a = t2 - j
                                S.dma("pool", B1[ab, :, t2a:t2a + TCH, :], st_.ap, st_.ctr, reads=[st_])
                            if sg in l1set:
                                S.ins("pe", MM(l1b.ap, ones.ap[0:64, :], stt["ha"].ap, sg == l1first and t2 == 0, sg == l1last and t2 == N2 - 1),
                                      [ones, stt["ha"]], [l1b])

                        pipeline(len(units), [pA, pB, pC], LAG)
                        S.barrier()
                        for k1a in range(0, 64, KCH):
                            bufs = {}
                            for c in range(2):
                                ai = cnt["a"] % 4
                                cnt["a"] += 1
                                bf = Ach[ai]
                                for ab in range(2):
                                    hbuf = Achb[ai][ab]
                                    S.dma("sp", bf.ap[ab * N2:(ab + 1) * N2, :, :],
                                          B1[ab, c * 64 + k1a: c * 64 + k1a + KCH, :, :].rearrange("k t c -> t k c"),
                                          hbuf.ctr, writes=[hbuf])
                                bufs[c] = (bf, Achb[ai])
                            hA = hst[(k1a // KCH) % 2 * 2]
                            hB = hst[(k1a // KCH) % 2 * 2 + 1]
                            for j in range(KCH):
                                pA_ = nextbank()
                                for qi, (si, c) in enumerate(((0, 0), (1, 1))):
                                    bf, hbs_ = bufs[c]
                                    S.ins("pe", MM(pA_.ap[0:2 * N2, :], sm2v[:, si, :], bf.ap[:, j, :], qi == 0, qi == 1), [sm2] + hbs_, [pA_])
                                S.ins("act", ACP(hA.ap[:, j, :], pA_.ap[0:2 * N2, :]), [pA_], [hA])
                                pB_ = nextbank()
                                for qi, (si, c) in enumerate(((2, 0), (3, 1))):
                                    bf, hbs_ = bufs[c]
                                    S.ins("pe", MM(pB_.ap[0:2 * N2, :], sm2v[:, si, :], bf.ap[:, j, :], qi == 0, qi == 1), [sm2] + hbs_, [pB_])
                                S.ins("act", ACP(hB.ap[:, j, :], pB_.ap[0:2 * N2, :]), [pB_], [hB])
                            S.dma("pool", HS2[pidx, order, 0, :, k1a:k1a + KCH, :], hA.ap, hA.ctr, reads=[hA])
                            S.dma("pool", HS2[pidx, order, 1, :, k1a:k1a + KCH, :], hB.ap, hB.ctr, reads=[hB])
                        S.barrier()
                    S.ins("dve", RECIP(rl1.ap, l1b.ap), [l1b], [rl1])
                    S.dma("sp", RL1[fs, order], rl1.ap[0:64, :], rl1.ctr, reads=[rl1])
                    S.barrier()
            S.phase_end()

    def norm_to(es_b, xT, xTb, s, aidx, bidx, outT, outTb):
        sq, rstd, tmp, tmpb = es_b["sq"], es_b["rstd"], es_b["tmp"], es_b["tmpb"]
        for kc in range(8):
            S.ins("dve", TTO(sq.ap[:, kc, :], xT.ap[:, kc, :], xT.ap[:, kc, :], ALU.mult), [xTb[kc]], [es_b["sqb"][kc]])
        pb = nextbank()
        for kc in range(8):
            S.ins("pe", MM(pb.ap, ones.ap, sq.ap[:, kc, :], kc == 0, kc == 7), [ones, es_b["sqb"][kc]], [pb])
        S.ins("act", ACT(rstd.ap, pb.ap, AF.Sqrt, bias=epst.ap[:, 0:1], scale=1.0 / D), [pb, epst], [rstd])
        S.ins("dve", RECIP(rstd.ap, rstd.ap), [rstd], [rstd])
        for kc in range(8):
            if aidx is None:
                sc_ap = gfin.ap[:, kc:kc + 1]
                S.ins("dve", STT(outT.ap[:, kc, :], xT.ap[:, kc, :], sc_ap, rstd.ap, ALU.mult, ALU.mult),
                      [xTb[kc], rstd, gfin], [outTb[kc]])
            else:
                S.ins("dve", STT(tmp.ap[:, kc, :], xT.ap[:, kc, :], par.ap[:, s, aidx, kc:kc + 1], rstd.ap, ALU.mult, ALU.mult),
                      [xTb[kc], rstd, par], [tmpb[kc]])
                S.ins("act", ACT(outT.ap[:, kc, :], tmp.ap[:, kc, :], AF.Identity, bias=par.ap[:, s, bidx, kc:kc + 1]),
                      [tmpb[kc], par], [outTb[kc]])

    ring_i = [0]

    def ring_load(ring, src2d, width):
        sl = ring[ring_i[0] % len(ring)]
        ring_i[0] += 1
        S.dma("sp", sl.ap[:, 0:width], src2d, sl.ctr, writes=[sl])
        return sl

    def ffn(es_b, ring, hnT, hnTb, xT, xTb, s, gidx, wg, wu, wd):
        aT, aTb, sgt = es_b["aT"], es_b["aTb"], es_b["sgt"]
        for j in range(NJF):
            sg = ring_load(ring, Wl[wg][j], 1024)
            su = ring_load(ring, Wl[wu][j], 1024)
            sgv = sg.ap[:, 0:1024].rearrange("p (k c) -> p k c", c=128)
            suv = su.ap[:, 0:1024].rearrange("p (k c) -> p k c", c=128)
            pg = nextbank()
            pu = nextbank()
            for kc in range(8):
                S.ins("pe", MM(pg.ap, sgv[:, kc, :], hnT.ap[:, kc, :], kc == 0, kc == 7), [sg, hnTb[kc]], [pg])
            for kc in range(8):
                S.ins("pe", MM(pu.ap, suv[:, kc, :], hnT.ap[:, kc, :], kc == 0, kc == 7), [su, hnTb[kc]], [pu])
            sl = sgt[j % 2]
            S.ins("act", ACT(sl.ap, pg.ap, AF.Silu), [pg], [sl])
            S.ins("dve", TTO(aT.ap[:, j, :], sl.ap, pu.ap, ALU.mult), [sl, pu], [aTb[j]])
        for oc in range(8):
            sd = ring_load(ring, Wl[wd][oc], NJF * 128)
            sdv = sd.ap[:, 0:NJF * 128].rearrange("p (k c) -> p k c", c=128)
            py = nextbank()
            for j in range(NJF):
                S.ins("pe", MM(py.ap, sdv[:, j, :], aT.ap[:, j, :], j == 0, j == NJF - 1), [sd, aTb[j]], [py])
            S.ins("dve", STT(xT.ap[:, oc, :], py.ap, par.ap[:, s, gidx, oc:oc + 1], xT.ap[:, oc, :], ALU.mult, ALU.add),
                  [py, par, xTb[oc]], [xTb[oc]])

    def ffn_bufs(es):
        b = {}
        b["sq"] = alloc(es, "sq", [128, 8, TT], BF16, False)
        b["sqb"] = [Buf(f"sq{k}") for k in range(8)]
        b["rstd"] = alloc(es, "rstd", [128, TT], F32, False)
        b["tmp"] = alloc(es, "tmp", [128, 8, TT], F32, False)
        b["tmpb"] = [Buf(f"tmp{k}") for k in range(8)]
        b["aT"] = alloc(es, "aT", [128, NJF, TT], BF16, False)
        b["aTb"] = [Buf(f"aT{k}") for k in range(NJF)]
        b["sgt"] = [alloc(es, f"sgt{i}", [128, TT], F32, False) for i in range(2)]
        return b

    if "ffn" in stages:
        with contextlib.ExitStack() as es:
            S.phase_begin()
            fb = ffn_bufs(es)
            ring = [alloc(es, f"ring{i}", [128, 4096], BF16) for i in range(6)]
            xtok = [alloc(es, f"xtok{i}", [128, D], F32) for i in range(4)]
            xTs = [(alloc(es, f"xT_{i}", [128, 8, TT], F32), [Buf(f"xT{i}_{k}") for k in range(8)]) for i in range(2)]
            hnT = alloc(es, "hnT", [128, 8, TT], BF16, False)
            hnTb = [Buf(f"hnT{k}") for k in range(8)]
            qk = alloc(es, "qk", [128, 8, TT], BF16)
            qkb = [Buf(f"qk{k}") for k in range(8)]
            vst = [alloc(es, f"vst{i}", [128, 8, 80], BF16) for i in range(2)]
            hstg = [alloc(es, f"hstg{i}", [128, 512], F32) for i in range(3)]
            for v in vst:
                S.ins("pool", MSET(v.ap, 1.0), (), [v])
            hq = 0
            for ti in range(NTILES):
                tok0 = ti * TT
                s = tile_seq(tok0)
                xT, xTb = xTs[ti % 2]
                for st in range(4):
                    S.dma("sp", xtok[st].ap, xin[tok0 + st * 128: tok0 + (st + 1) * 128, :], xtok[st].ctr, writes=[xtok[st]])
                for kc in range(8):
                    pb = nextbank()
                    for st in range(4):
                        S.ins("pe", TR(pb.ap[:, st * 128:(st + 1) * 128], xtok[st].ap[:, kc * 128:(kc + 1) * 128], ident.ap),
                              [xtok[st], ident], [pb])
                    evac(xT.ap[:, kc, :], pb.ap, [pb], [xTb[kc]])
                norm_to(fb, xT, xTb, s, 0, 1, hnT, hnTb)
                ffn(fb, ring, hnT, hnTb, xT, xTb, s, 2, "f1g", "f1u", "f1d")
                S.dma("pool", X1[:, :, tok0:tok0 + TT].rearrange("k p t -> p k t"), xT.ap, xT.ctr, reads=xTb)
                norm_to(fb, xT, xTb, s, 3, 4, hnT, hnTb)
                for j in range(8):
                    sw = ring_load(ring, Wl["winl"][j], 1024)
                    swv = sw.ap[:, 0:1024].rearrange("p (k c) -> p k c", c=128)
                    pb = nextbank()
                    for kc in range(8):
                        S.ins("pe", MM(pb.ap, swv[:, kc, :], hnT.ap[:, kc, :], kc == 0, kc == 7), [sw, hnTb[kc]], [pb])
                    evac(qk.ap[:, j, :], pb.ap, [pb], [qkb[j]])
                S.dma("pool", QK[:, :, tok0:tok0 + TT].rearrange("k p t -> p k t"), qk.ap, qk.ctr, reads=qkb)
                for g in range(4):
                    sw = ring_load(ring, Wl["winr"][g], 4096)
                    swv = sw.ap.rearrange("p (k c) -> p k c", c=512)
                    for st in range(4):
                        pb = nextbank()
                        for kc in range(8):
                            S.ins("pe", MM(pb.ap, hnT.ap[:, kc, st * 128:(st + 1) * 128], swv[:, kc, :], kc == 0, kc == 7),
                                  [sw, hnTb[kc]], [pb])
                        tk = tok0 + st * 128
                        if g == 0:
                            v = vst[st % 2]
                            evac(v.ap[:, :, 0:64], pb.ap.rearrange("p (h d) -> p h d", d=64), [pb], [v])
                            S.dma("pool", Vd[tk:tk + 128, :], v.ap.rearrange("p h d -> p (h d)"), v.ctr, reads=[v])
                        else:
                            h = hstg[hq % 3]
                            hq += 1
                            evac(h.ap, pb.ap, [pb], [h])
                            S.dma("pool", HYP[tk:tk + 128, (g - 1) * 512:g * 512], h.ap, h.ctr, reads=[h])
            S.phase_end()

    if "attn" in stages:
        with contextlib.ExitStack() as es:
            S.phase_begin()
            Tpp = alloc(es, "Tpp", [64, 8, 15, 64], F32)
            jrev = alloc(es, "jrev", [64, 64], F32)
            cmask = alloc(es, "cmask", [128, 64], F32)
            T2 = alloc(es, "T2", [128, 8 * 14, 64], F32, False)
            gat = alloc(es, "gat", [128, 512], F32)
            S.dma("sp", jrev.ap, jrev_d, jrev.ctr, writes=[jrev])
            S.dma("sp", cmask.ap, cmask_d, cmask.ctr, writes=[cmask])
            S.dma("sp", gat.ap, bcast_rows(anorm), gat.ctr, writes=[gat])
            for h in range(8):
                src = bass.AP(rpbp.tensor, rpbp.offset + h * 15 * 128, [[1, 64], [128, 15], [1, 64]])
                S.dma("sp", Tpp.ap[:, h, :, :], src, Tpp.ctr, writes=[Tpp])
            import os
            ACUT = int(os.environ.get("ACUT", "9"))
            if ACUT >= -1:
                S.ins("act", ACT(Tpp.ap, Tpp.ap, AF.Exp), [Tpp], [Tpp])
            for bk in (range(14) if ACUT >= 0 else []):
                pb = nextbank()
                for sl in range(8):
                    fl = bk * 8 + sl
                    h, d = fl // 14, fl % 14
                    for half in range(2):
                        S.ins("pe", MM(pb.ap[64 * half:64 * half + 64, sl * 64:(sl + 1) * 64], Tpp.ap[:, h, d + half, :], jrev.ap, True, True),
                              [Tpp, jrev], [pb])
                if ACUT >= 1:
                    S.ins("dve", TTO(T2.ap[:, bk * 8:(bk + 1) * 8, :], pb.ap.rearrange("p (s c) -> p s c", c=64),
                                cmask.ap.unsqueeze(1).to_broadcast([128, 8, 64]), ALU.mult), [pb, cmask], [T2])
            T2v = T2.ap.rearrange("p (h d) c -> p h d c", d=14)
            if debug:
                dbgT2 = nc.dram_tensor("dbgT2", [128, 112 * 64], F32, kind="ExternalOutput").ap()
                dbgE = nc.dram_tensor("dbgE", [128, 512], F32, kind="ExternalOutput").ap()
                dbgP = nc.dram_tensor("dbgP", [128, 512], BF16, kind="ExternalOutput").ap()
                dbgO = nc.dram_tensor("dbgO", [128, 512], F32, kind="ExternalOutput").ap()
                dbgA = nc.dram_tensor("dbgA", [128, 512], F32, kind="ExternalOutput").ap()
                dctr = S.dma_counter("dbg")
                S.dma("sp", dbgT2, T2.ap.rearrange("p a c -> p (a c)"), dctr, reads=[T2])
            QB = min(32, LB // 64)
            KMAX = QB + 8
            NE = (KMAX + 2) // 2
            qT = alloc(es, "qT", [128, 4, QB * 64], BF16)
            kT = alloc(es, "kT", [128, 4, (KMAX + 1) * 64], BF16)
            Ve = alloc(es, "Ve", [128, NE, 640], BF16)
            Vo = alloc(es, "Vo", [128, NE, 640], BF16)
            Est = [alloc(es, f"Est{i}", [128, 512], F32, False) for i in range(4)]
            Pst = [alloc(es, f"Pst{i}", [128, 512], BF16, False) for i in range(4)]
            rec = alloc(es, "rec", [128, 8], F32, False)
            att = alloc(es, "att", [128, 512], F32, False)
            junk = alloc(es, "junk", [128, 512], F32, False)
            ssq = alloc(es, "ssq", [128, 1], F32, False)
            ans = [alloc(es, f"ans{i}", [128, 512], BF16) for i in range(2)]
            pq = 0
            for (sb, nb) in (seqs if ACUT >= 2 else []):
                rows = nb * LB // 64
                for qb0 in range(0, rows, QB):
                    r_lo, r_hi = qb0, qb0 + QB
                    klo = min(max(r_lo - 4, 0), rows - 8)
                    khi = min(max(r_hi - 1 - 4, 0), rows - 8) + 8
                    klo_e = klo - (klo % 2)
                    nk = khi - klo_e
                    ne = (nk + 1) // 2
                    no = (nk - 1) // 2
                    S.dma("sp", qT.ap, QK[0:4, :, sb + r_lo * 64: sb + r_hi * 64].rearrange("k p t -> p k t"), qT.ctr, writes=[qT])
                    S.dma("sp", kT.ap[:, :, 0:nk * 64], QK[4:8, :, sb + klo_e * 64: sb + (klo_e + nk) * 64].rearrange("k p t -> p k t"),
                          kT.ctr, writes=[kT])
                    t0 = sb + klo_e * 64
                    S.dma("sp", Ve.ap[:, 0:ne, :], Vd[t0:t0 + ne * 128, :].rearrange("(m p) c -> p m c", p=128), Ve.ctr, writes=[Ve])
                    if no > 0:
                        S.dma("sp", Vo.ap[:, 0:no, :], Vd[t0 + 64:t0 + 64 + no * 128, :].rearrange("(m p) c -> p m c", p=128),
                              Vo.ctr, writes=[Vo])
                    Sset = [(banks[0], banks[1]), (banks[2], banks[3])]
                    Oset = [(banks[4], banks[5]), (banks[6], banks[7])]
                    units = [(rr, hq) for rr in range(r_lo, r_hi) for hq in range(2)]

                    def rowinfo(rr):
                        r0 = min(max(rr - 4, 0), rows - 8)
                        d0 = r0 - rr + 7
                        pr = (r0 - klo_e) % 2
                        m0 = (r0 - klo_e - pr) // 2
                        return r0, d0, (Vo if pr else Ve), m0, (r0 - klo_e) * 64, (rr - r_lo) * 64

                    def attA(u):
                        rr, hq = units[u]
                        r0, d0, Vt, m0, kofs, qofs = rowinfo(rr)
                        sbs = Sset[u % 2]
                        for hpi in range(2):
                            hp = 2 * hq + hpi
                            for i in range(4):
                                for hh in range(2):
                                    p0 = 64 * hh
                                    S.ins("pe", MM(sbs[hh].ap[:, (hpi * 4 + i) * 64:(hpi * 4 + i + 1) * 64],
                                                   kT.ap[p0:p0 + 64, hp, kofs + 128 * i: kofs + 128 * (i + 1)],
                                                   qT.ap[p0:p0 + 64, hp, qofs:qofs + 64], True, True), [kT, qT], [sbs[hh]])

                    def attB(u):
                        nonlocal pq
                        rr, hq = units[u]
                        r0, d0, Vt, m0, kofs, qofs = rowinfo(rr)
                        sbs = Sset[u % 2]
                        ob = Oset[rr % 2]
                        for hh in range(2):
                            E = Est[pq % 4]
                            Pb = Pst[pq % 4]
                            pq += 1
                            h0 = 4 * hq + hh
                            S.ins("act", ACT(E.ap, sbs[hh].ap, AF.Exp, scale=0.125), [sbs[hh]], [E])
                            S.ins("dve", TTO(Pb.ap.rearrange("p (h i c) -> p h i c", h=2, i=4),
                                            E.ap.rearrange("p (h i c) -> p h i c", h=2, i=4),
                                            T2v[:, h0:h0 + 3:2, d0:d0 + 7:2, :], ALU.mult), [E, T2], [Pb])
                            for hpi in range(2):
                                h = 4 * hq + 2 * hpi + hh
                                for i in range(4):
                                    S.ins("pe", MM(ob[hq].ap[0:64, (h % 4) * 80:(h % 4) * 80 + 66],
                                                   Pb.ap[:, (hpi * 4 + i) * 64:(hpi * 4 + i + 1) * 64],
                                                   Vt.ap[:, m0 + i, h * 80:h * 80 + 66], i == 0, i == 3), [Pb, Vt], [ob[hq]])
                        if hq == 1:
                            r = rr
                            for bk in range(2):
                                obv = ob[bk].ap[0:64, 0:320].rearrange("p (h d) -> p h d", d=80)
                                S.ins("dve", RECIP(rec.ap[0:64, bk * 4:bk * 4 + 4], obv[:, :, 64]), [ob[bk]], [rec])
                                S.ins("dve", TTO(att.ap[0:64, bk * 256:(bk + 1) * 256].rearrange("p (h d) -> p h d", d=64), obv[:, :, 0:64],
                                                rec.ap[0:64, bk * 4:bk * 4 + 4].unsqueeze(2).to_broadcast([64, 4, 64]), ALU.mult),
                                      [ob[bk], rec], [att])
                            S.ins("dve", STT(junk.ap[0:64, :], att.ap[0:64, :], 1.0, att.ap[0:64, :], ALU.mult, ALU.mult, accum_out=ssq.ap[0:64, :]), [att], [junk, ssq])
                            S.ins("act", ACT(ssq.ap[0:64, :], ssq.ap[0:64, :], AF.Sqrt, bias=epst.ap[0:64, 0:1], scale=1.0 / 512), [ssq, epst], [ssq])
                            S.ins("dve", RECIP(ssq.ap[0:64, :], ssq.ap[0:64, :]), [ssq], [ssq])
                            an = ans[r % 2]
                            S.ins("dve", STT(an.ap[0:64, :], att.ap[0:64, :], ssq.ap[0:64, 0:1], gat.ap[0:64, :], ALU.mult, ALU.mult), [att, ssq, gat], [an])
                            S.dma("pool", MIXA[sb + r * 64: sb + r * 64 + 64, :], an.ap[0:64, :], an.ctr, reads=[an])

                    attA(0)
                    for u in range(len(units)):
                        if u + 1 < len(units):
                            attA(u + 1)
                        attB(u)

            S.phase_end()

    if "hyena" in stages:
        with contextlib.ExitStack() as es:
            S.phase_begin()
            cw = alloc(es, "cw", [128, 3, 1536], F32)
            cb = alloc(es, "cb", [128, 1536], F32)
            S.dma("sp", cw.ap, bcast_rows(convw).rearrange("p (i c) -> p i c", i=3), cw.ctr, writes=[cw])
            S.dma("sp", cb.ap, bcast_rows(convb), cb.ctr, writes=[cb])
            G = 4
            NB3 = 2
            At = [alloc(es, f"cA{i}", [128, G, 512], F32) for i in range(NB3)]
            Bt = [alloc(es, f"cB{i}", [128, G, 512], F32) for i in range(NB3)]
            Ct = [alloc(es, f"cC{i}", [128, G, 512], F32) for i in range(NB3)]
            Ot = [alloc(es, f"cO{i}", [128, G, 512], F32) for i in range(NB3)]
            Ob = [alloc(es, f"cOb{i}", [128, G, 512], BF16) for i in range(NB3)]
            dq = ["sp", "act"]
            q = 0
            for (sb, nb) in seqs:
                ntl = nb * TC
                for part in range(3):
                    c0 = part * 512
                    for tl in range(0, ntl, G):
                        tok = sb + tl * 128
                        a, b, c, o, ob_ = At[q % NB3], Bt[q % NB3], Ct[q % NB3], Ot[q % NB3], Ob[q % NB3]
                        eng = "pool" if q % 3 == 2 else "dve"
                        dqe = "sp"
                        q += 1

                        def rows(r0, n):
                            return HYP[r0:r0 + n * 128, c0:c0 + 512].rearrange("(m p) c -> p m c", p=128)
                        if tl == 0:
                            S.ins(eng, MSET(a.ap[0:1, 0, :], 0.0), (), [a])
                            S.dma(dqe, a.ap[1:128, 0, :], HYP[tok:tok + 127, c0:c0 + 512], a.ctr, writes=[a])
                            S.dma(dqe, a.ap[:, 1:G, :], rows(tok + 127, G - 1), a.ctr, writes=[a])
                        else:
                            S.dma(dqe, a.ap, rows(tok - 1, G), a.ctr, writes=[a])
                        S.dma(dqe, b.ap, rows(tok, G), b.ctr, writes=[b])
                        if tl + G >= ntl:
                            S.ins(eng, MSET(c.ap[:, G - 1, :], 0.0), (), [c])
                            S.dma(dqe, c.ap[:, 0:G - 1, :], rows(tok + 1, G - 1), c.ctr, writes=[c])
                            lt = tok + (G - 1) * 128
                            S.dma(dqe, c.ap[0:127, G - 1, :], HYP[lt + 1:lt + 128, c0:c0 + 512], c.ctr, writes=[c])
                        else:
                            S.dma(dqe, c.ap, rows(tok + 1, G), c.ctr, writes=[c])
                        w0 = cw.ap[:, 0, c0:c0 + 512].unsqueeze(1).to_broadcast([128, G, 512])
                        w1 = cw.ap[:, 1, c0:c0 + 512].unsqueeze(1).to_broadcast([128, G, 512])
                        w2 = cw.ap[:, 2, c0:c0 + 512].unsqueeze(1).to_broadcast([128, G, 512])
                        bb = cb.ap[:, c0:c0 + 512].unsqueeze(1).to_broadcast([128, G, 512])
                        S.ins(eng, TTO(a.ap, a.ap, w0, ALU.mult), [a, cw], [a])
                        S.ins(eng, TTO(b.ap, b.ap, w1, ALU.mult), [b, cw], [b])
                        S.ins(eng, TTO(c.ap, c.ap, w2, ALU.mult), [c, cw], [c])
                        S.ins(eng, TTO(a.ap, a.ap, b.ap, ALU.add), [a, b], [a])
                        S.ins(eng, TTO(c.ap, c.ap, bb, ALU.add), [c, cb], [c])
                        if part == 0:
                            S.ins(eng, TTO(ob_.ap, a.ap, c.ap, ALU.add), [a, c], [ob_])
                            S.dma("act", V0[tok:tok + G * 128, :].rearrange("(m p) c -> p m c", p=128), ob_.ap, ob_.ctr, reads=[ob_])
                        else:
                            S.ins(eng, TTO(o.ap, a.ap, c.ap, ALU.add), [a, c], [o])
                            S.dma("act", HYC[tok:tok + G * 128, c0:c0 + 512].rearrange("(m p) c -> p m c", p=128), o.ap, o.ctr, reads=[o])
            S.phase_end()

        with contextlib.ExitStack() as es:
            S.phase_begin()
            skb = alloc(es, "skb", [64, 1024], F32)
            F1s = alloc(es, "F1s3", [64, N2 * 128], BF16)
            F2s = alloc(es, "F2s3", [128, N2 * 64], BF16)
            sm = alloc(es, "sm3", [N2, 12 * N2], BF16)
            Gs = alloc(es, "Gs3", [2 * N2, 4 * N2], BF16)
            S.dma("sp", skb.ap, bcast_rows(skipv, 64), skb.ctr, writes=[skb])
            S.dma("sp", F1s.ap, F1_d, F1s.ctr, writes=[F1s])
            S.dma("sp", F2s.ap, F2i_d, F2s.ctr, writes=[F2s])
            S.dma("sp", sm.ap, smalls_d, sm.ctr, writes=[sm])
            S.dma("sp", Gs.ap, G_d, Gs.ctr, writes=[Gs])
            F1v = F1s.ap.rearrange("p (t c) -> p t c", c=128)
            F2v = F2s.ap.rearrange("p (t c) -> p t c", c=64)
            smv = sm.ap.rearrange("p (s c) -> p s c", s=6)
            Gv = Gs.ap.rearrange("p (s c) -> p s c", s=2)
            S.barrier()

            def rows3(dr, sb, b, c0, c1):
                return dr[sb + b * LB: sb + (b + 1) * LB, c0:c1].rearrange("(a t) c -> a t c", t=N2)

            for (sb, nb) in seqs:
                for order in range(2):
                    src = V0 if order == 0 else Z1
                    with contextlib.ExitStack() as e1:
                        S.phase_begin()
                        uch = [alloc(e1, f"uch{i}", [64, TCH, 512], BF16) for i in range(3)]
                        stg = [alloc(e1, f"stg{i}", [128, TCH, 512], BF16) for i in range(2)]
                        q = 0
                        for b in range(nb):
                            for t2a in range(0, N2, TCH):
                                u = uch[q % 3]
                                st_ = stg[q % 2]
                                q += 1
                                S.dma("sp", u.ap, rows3(src, sb, b, 0, 512)[:, t2a:t2a + TCH, :], u.ctr, writes=[u])
                                for j in range(TCH):
                                    pb = nextbank()
                                    S.ins("pe", MM(pb.ap, F1v[:, t2a + j, :], u.ap[:, j, :], True, True), [F1s, u], [pb])
                                    evac(st_.ap[:, j, :], pb.ap, [pb], [st_])
                                S.dma("pool", B1[b, :, t2a:t2a + TCH, :], st_.ap, st_.ctr, reads=[st_])
                        S.phase_end()
                    with contextlib.ExitStack() as e2:
                        S.phase_begin()
                        Ach = [alloc(e2, f"A{i}", [N2, KCH, 512], BF16) for i in range(8)]
                        Hch = [alloc(e2, f"H{i}", [2 * N2, KCH, 512], BF16) for i in range(12)]
                        T12 = [alloc(e2, f"T12_{i}", [2 * N2, 512], BF16, False) for i in range(16)]
                        tmpf = [alloc(e2, f"tmpf{i}", [2 * N2, 512], F32, False) for i in range(8)]
                        stgB = [alloc(e2, f"stgB{i}", [2 * N2, KCH, 512], BF16) for i in range(4)]
                        cq = {"a": 0, "h": 0, "t": 0, "f": 0, "s": 0}
                        pids = [0] if nb == 1 else [1, 2, 3]
                        if nb == 1:
                            terms = [[(0, 0)]]
                        else:
                            terms = [[(0, 1), (1, 2)], [(0, 3), (1, 1)]]
                        Uset = [(banks[0], banks[1]), (banks[2], banks[3])]
                        Cbanks = [banks[4], banks[5], banks[6]]
                        chst = {}
                        ust = [dict() for _ in range(64)]

                        def s2A(k):
                            ci, j = k // KCH, k % KCH
                            k1a = ci * KCH
                            if j == 0:
                                Ab = {}
                                for b in range(nb):
                                    for c in range(2):
                                        bf = Ach[cq["a"] % 8]
                                        cq["a"] += 1
                                        S.dma("sp", bf.ap, B1[b, c * 64 + k1a: c * 64 + k1a + KCH, :, :].rearrange("k t c -> t k c"),
                                              bf.ctr, writes=[bf])
                                        Ab[b, c] = bf
                                Hb = {}
                                for pidx in pids:
                                    for ab in range(2):
                                        bf = Hch[cq["h"] % 12]
                                        cq["h"] += 1
                                        S.dma("pool", bf.ap, HS2[pidx, order, ab, :, k1a:k1a + KCH, :], bf.ctr, writes=[bf])
                                        Hb[pidx, ab] = bf
                                sB = []
                                for ob in range(nb):
                                    sB.append(stgB[cq["s"] % 4])
                                    cq["s"] += 1
                                chst[ci] = (Ab, Hb, sB)
                            Ab, Hb, sB = chst[ci]
                            pU = []
                            for b in range(nb):
                                pu = Uset[k % 2][b]
                                S.ins("pe", MM(pu.ap[0:2 * N2, :], smv[:, 0, :], Ab[b, 0].ap[:, j, :], True, False), [sm, Ab[b, 0]], [pu])
                                S.ins("pe", MM(pu.ap[0:2 * N2, :], smv[:, 1, :], Ab[b, 1].ap[:, j, :], False, True), [sm, Ab[b, 1]], [pu])
                                pU.append(pu)
                            ust[k]["pU"] = pU

                        def s2B(k):
                            ci, j = k // KCH, k % KCH
                            Ab, Hb, sB = chst[ci]
                            pU = ust[k]["pU"]
                            allT = []
                            for ob in range(nb):
                                Ts = []
                                for ab in range(2):
                                    T = T12[cq["t"] % 16]
                                    cq["t"] += 1
                                    tl = terms[ob]
                                    if len(tl) == 1:
                                        (ub, pidx) = tl[0]
                                        S.ins("dve", TTO(T.ap, pU[ub].ap[0:2 * N2, :], Hb[pidx, ab].ap[:, j, :], ALU.mult),
                                              [pU[ub], Hb[pidx, ab]], [T])
                                    else:
                                        f0 = tmpf[cq["f"] % 8]
                                        f1 = tmpf[(cq["f"] + 1) % 8]
                                        cq["f"] += 2
                                        (u0, p0_), (u1, p1_) = tl
                                        S.ins("dve", TTO(f0.ap, pU[u0].ap[0:2 * N2, :], Hb[p0_, ab].ap[:, j, :], ALU.mult),
                                              [pU[u0], Hb[p0_, ab]], [f0])
                                        S.ins("dve", TTO(f1.ap, pU[u1].ap[0:2 * N2, :], Hb[p1_, ab].ap[:, j, :], ALU.mult),
                                              [pU[u1], Hb[p1_, ab]], [f1])
                                        S.ins("dve", TTO(T.ap, f0.ap, f1.ap, ALU.add), [f0, f1], [T])
                                    Ts.append(T)
                                allT.append(Ts)
                            ust[k]["T"] = allT

                        def s2C(k):
                            ci, j = k // KCH, k % KCH
                            k1a = ci * KCH
                            Ab, Hb, sB = chst[ci]
                            for ob in range(nb):
                                Ts = ust[k]["T"][ob]
                                pbk = Cbanks[cq["c"] % 3]
                                cq["c"] += 1
                                S.ins("pe", MM(pbk.ap[0:2 * N2, :], Gv[:, 0, :], Ts[0].ap, True, False), [Gs, Ts[0]], [pbk])
                                S.ins("pe", MM(pbk.ap[0:2 * N2, :], Gv[:, 1, :], Ts[1].ap, False, True), [Gs, Ts[1]], [pbk])
                                evac(sB[ob].ap[:, j, :], pbk.ap[0:2 * N2, :], [pbk], [sB[ob]])
                            if j == KCH - 1:
                                for ob in range(nb):
                                    S.dma("pool", B2[ob, :, k1a:k1a + KCH, :], sB[ob].ap, sB[ob].ctr, reads=[sB[ob]])

                        cq["c"] = 0
                        for step in range(64 + 2):
                            if 0 <= step - 2 < 64:
                                s2C(step - 2)
                            if 0 <= step - 1 < 64:
                                s2B(step - 1)
                            if step < 64:
                                s2A(step)

                        S.phase_end()
                    with contextlib.ExitStack() as e3:
                        S.phase_begin()
                        Bc = [alloc(e3, f"Bc{i}", [128, TCH, 512], BF16) for i in range(2)]
                        Bh = [[Buf(f"bch{i}_{k}") for k in range(2)] for i in range(2)]
                        vch = [alloc(e3, f"vch{i}", [64, TCH, 512], BF16) for i in range(2)]
                        gch = [alloc(e3, f"gch{i}", [64, TCH, 512], F32) for i in range(2)]
                        rlt = alloc(e3, "rlt", [64, 512], F32)
                        S.dma("sp", rlt.ap, RL1[0 if nb == 1 else 1, order], rlt.ctr, writes=[rlt])
                        vsk = [alloc(e3, f"vsk{i}", [64, TCH, 512], F32, False) for i in range(2)]
                        zof = [alloc(e3, f"zof{i}", [64, TCH, 512], F32) for i in range(2)]
                        zob = [alloc(e3, f"zob{i}", [64, TCH, 512], BF16) for i in range(2)] if order == 0 else None
                        q = 0
                        zq = 0
                        gc0 = 512 * (1 + order)
                        for b in range(nb):
                            for t2a in range(0, N2, TCH):
                                bc = Bc[q % 2]
                                hb0, hb1 = Bh[q % 2]
                                vv_ = vch[q % 2]
                                gg = gch[q % 2]
                                zo = (zob if order == 0 else zof)[q % 2]
                                ztmp = zof[q % 2]
                                vs_ = vsk[q % 2]
                                q += 1
                                S.dma("sp", bc.ap[0:64, :, :], B2[b, t2a:t2a + TCH, :, :].rearrange("t k c -> k t c"), hb0.ctr,
                                      reads=(), writes=[hb0])
                                S.dma("sp", bc.ap[64:128, :, :], B2[b, N2 + t2a:N2 + t2a + TCH, :, :].rearrange("t k c -> k t c"), hb1.ctr,
                                      reads=(), writes=[hb1])
                                S.dma("sp", vv_.ap, rows3(src, sb, b, 0, 512)[:, t2a:t2a + TCH, :], vv_.ctr, writes=[vv_])
                                S.dma("pool", gg.ap, rows3(HYC, sb, b, gc0, gc0 + 512)[:, t2a:t2a + TCH, :], gg.ctr, writes=[gg])
                                skbc = skb.ap[:, order * 512:(order + 1) * 512].unsqueeze(1).to_broadcast([64, TCH, 512])
                                rlbc = rlt.ap.unsqueeze(1).to_broadcast([64, TCH, 512])
                                S.ins("dve", TTO(vs_.ap, vv_.ap, skbc, ALU.mult), [vv_, skb], [vs_])
                                S.ins("dve", TTO(vs_.ap, vs_.ap, gg.ap, ALU.mult), [vs_, gg], [vs_])
                                S.ins("dve", TTO(gg.ap, gg.ap, rlbc, ALU.mult), [gg, rlt], [gg])
                                for j in range(TCH):
                                    py = nextbank()
                                    S.ins("pe", MM(py.ap[0:64, :], F2v[:, t2a + j, :], bc.ap[:, j, :], True, True), [F2s, hb0, hb1], [py])
                                    S.ins("dve", TTO(ztmp.ap[:, j, :], py.ap[0:64, :], gg.ap[:, j, :], ALU.mult), [py, gg], [ztmp])
                                S.ins("dve", TTO(zo.ap, ztmp.ap, vs_.ap, ALU.add), [ztmp, vs_], [zo])
                                dstz = Z1 if order == 0 else Z2
                                S.dma("sp", rows3(dstz, sb, b, 0, 512)[:, t2a:t2a + TCH, :], zo.ap, zo.ctr, reads=[zo])
                        S.phase_end()
            S.phase_end()

    if "ffn" in stages:
        with contextlib.ExitStack() as es:
            S.phase_begin()
            fb = ffn_bufs(es)
            ring = [alloc(es, f"ringb{i}", [128, 4096], BF16) for i in range(6)]
            xTs = [(alloc(es, f"xT4_{i}", [128, 8, TT], F32), [Buf(f"xT4{i}_{k}") for k in range(8)]) for i in range(2)]
            hnT = alloc(es, "hnT4", [128, 8, TT], BF16, False)
            hnTb = [Buf(f"hnT4_{k}") for k in range(8)]
            mixA = alloc(es, "mixA", [128, 4, 512], BF16)
            z2t = alloc(es, "z2t", [128, 4, 512], F32)
            z2n = alloc(es, "z2n", [128, 4, 512], BF16, False)
            z2nb = [Buf(f"z2n{k}") for k in range(4)]
            ghy = alloc(es, "ghy", [128, 512], F32)
            ss4 = alloc(es, "ss4", [128, 4], F32, False)
            junk4 = alloc(es, "junk4", [128, 512], F32, False)
            mixT = alloc(es, "mixT", [128, 8, TT], BF16, False)
            mixTb = [Buf(f"mixT{k}") for k in range(8)]
            outT = fb["tmp"]
            outTb = fb["tmpb"]
            ytok = [alloc(es, f"ytok{i}", [128, D], F32) for i in range(2)]
            S.dma("sp", ghy.ap, bcast_rows(hnorm), ghy.ctr, writes=[ghy])
            if "attn" not in stages or "hyena" not in stages:
                S.ins("pool", MSET(z2t.ap, 0.0), (), [z2t])
                S.ins("pool", MSET(mixA.ap, 0.0), (), [mixA])
                for ti in range(NTILES):
                    tok0 = ti * TT
                    if "attn" not in stages:
                        S.dma("sp", MIXA[tok0:tok0 + TT, :].rearrange("(m p) c -> p m c", p=128), mixA.ap, mixA.ctr, reads=[mixA])
                    if "hyena" not in stages:
                        S.dma("sp", Z2[tok0:tok0 + TT, :].rearrange("(m p) c -> p m c", p=128), z2t.ap, z2t.ctr, reads=[z2t])
                S.barrier()
            for ti in range(NTILES):
                tok0 = ti * TT
                s = tile_seq(tok0)
                xT, xTb = xTs[ti % 2]
                S.dma("sp", xT.ap, X1[:, :, tok0:tok0 + TT].rearrange("k p t -> p k t"), xT.ctr, writes=xTb)
                S.dma("sp", mixA.ap, MIXA[tok0:tok0 + TT, :].rearrange("(m p) c -> p m c", p=128), mixA.ctr, writes=[mixA])
                S.dma("sp", z2t.ap, Z2[tok0:tok0 + TT, :].rearrange("(m p) c -> p m c", p=128), z2t.ctr, writes=[z2t])
                for st in range(4):
                    S.ins("dve", STT(junk4.ap, z2t.ap[:, st, :], 1.0, z2t.ap[:, st, :], ALU.mult, ALU.mult, accum_out=ss4.ap[:, st:st + 1]),
                          [z2t], [junk4, ss4])
                S.ins("act", ACT(ss4.ap, ss4.ap, AF.Sqrt, bias=epst.ap[:, 0:1], scale=1.0 / 512), [ss4, epst], [ss4])
                S.ins("dve", RECIP(ss4.ap, ss4.ap), [ss4], [ss4])
                for st in range(4):
                    S.ins("dve", STT(z2n.ap[:, st, :], z2t.ap[:, st, :], ss4.ap[:, st:st + 1], ghy.ap, ALU.mult, ALU.mult),
                          [z2t, ss4, ghy], [z2nb[st]])
                for fc in range(8):
                    pb = nextbank()
                    pbv = pb.ap.bitcast(BF16)
                    for st in range(4):
                        if fc < 4:
                            S.ins("pe", TR(pbv[:, st * 128:(st + 1) * 128], mixA.ap[:, st, fc * 128:(fc + 1) * 128], identb.ap),
                                  [mixA, identb], [pb])
                        else:
                            S.ins("pe", TR(pbv[:, st * 128:(st + 1) * 128], z2n.ap[:, st, (fc - 4) * 128:(fc - 3) * 128], identb.ap),
                                  [z2nb[st], identb], [pb])
                    evac(mixT.ap[:, fc, :], pbv[:, 0:TT], [pb], [mixTb[fc]])
                for oc in range(8):
                    sw = ring_load(ring, Wl["wout"][oc], 1024)
                    swv = sw.ap[:, 0:1024].rearrange("p (k c) -> p k c", c=128)
                    py = nextbank()
                    for kc in range(8):
                        S.ins("pe", MM(py.ap, swv[:, kc, :], mixT.ap[:, kc, :], kc == 0, kc == 7), [sw, mixTb[kc]], [py])
                    S.ins("dve", STT(xT.ap[:, oc, :], py.ap, par.ap[:, s, 5, oc:oc + 1], xT.ap[:, oc, :], ALU.mult, ALU.add),
                          [py, par, xTb[oc]], [xTb[oc]])
                norm_to(fb, xT, xTb, s, 6, 7, hnT, hnTb)
                ffn(fb, ring, hnT, hnTb, xT, xTb, s, 8, "f2g", "f2u", "f2d")
                norm_to(fb, xT, xTb, s, None, None, outT, outTb)
                for st in range(4):
                    yt = ytok[st % 2]
                    for hf in range(2):
                        pb = nextbank()
                        for k4 in range(4):
                            kc = hf * 4 + k4
                            S.ins("pe", TR(pb.ap[:, k4 * 128:(k4 + 1) * 128], outT.ap[:, kc, st * 128:(st + 1) * 128], ident.ap),
                                  [outTb[kc], ident], [pb])
                        evac(yt.ap[:, hf * 512:(hf + 1) * 512], pb.ap, [pb], [yt])
                    S.dma("sp", yout[tok0 + st * 128: tok0 + (st + 1) * 128, :], yt.ap, yt.ctr, reads=[yt])
            S.phase_end()

    S.emit(nc)
    es_all.close()
    return nc


def prep_shared(inp, LB):
    f = lambda a: np.ascontiguousarray(np.asarray(a, dtype=np.float32))
    sh = {}
    sh["w_ada"] = f(inp["w_ada"][0])
    sh["b_adaT"] = f(inp["b_ada"][0].reshape(72, 128).T)
    nrm = np.stack([inp["ffn1_norm"][0], inp["mix_norm"][0], inp["ffn2_norm"][0], inp["final_norm"]], 0)
    sh["nrm"] = f(nrm.reshape(4, 8, 128).transpose(2, 0, 1))
    sh["f1g"] = f(inp["ffn1_w_gate"][0]); sh["f1u"] = f(inp["ffn1_w_up"][0]); sh["f1d"] = f(inp["ffn1_w_down"][0])
    sh["f2g"] = f(inp["ffn2_w_gate"][0]); sh["f2u"] = f(inp["ffn2_w_up"][0]); sh["f2d"] = f(inp["ffn2_w_down"][0])
    sh["win"] = f(inp["w_in"][0]); sh["wout"] = f(inp["w_out"][0])
    rp = np.zeros((8, 15, 128), np.float32)
    rp[:, :, 48:79] = inp["na_rpb"][0]
    sh["rpbp"] = rp
    sh["convw"] = f(inp["hy_conv_w"][0].reshape(1, 3 * 1536))
    sh["convb"] = f(inp["hy_conv_b"][0].reshape(1, 1536))
    sh["hw1"] = f(inp["hy_w1"][0]); sh["hw2"] = f(inp["hy_w2"][0]); sh["hw3"] = f(inp["hy_w3"][0])
    sh["hwo"] = f(inp["hy_wo"][0])
    sh["hb"] = f(np.stack([inp["hy_b1"][0], inp["hy_b2"][0], inp["hy_b3"][0], inp["hy_sin_freq"][0]], 1))
    sh["skipv"] = f(inp["hy_skip"][0].reshape(1, 1024))
    sh["anorm"] = f(inp["attn_out_norm"][0].reshape(1, 512))
    sh["hnorm"] = f(inp["hy_out_norm"][0].reshape(1, 512))
    sh.update(host_consts(LB))
    return sh


DEBUG_OUT = {}


def run_cores(inp, LB, ncores, stages=("ffn", "attn", "hyena"), debug=False):
    xp = np.asarray(inp["x_prompt"], np.float32)
    xs = np.asarray(inp["x_sample"], np.float32)
    cp = np.asarray(inp["c_prompt"], np.float32)
    cs = np.asarray(inp["c_sample"], np.float32)
    sh = prep_shared(inp, LB)
    in_maps = []
    for i in range(ncores):
        m = dict(sh)
        m["xin"] = np.ascontiguousarray(np.concatenate([xp[2 * i], xp[2 * i + 1], xs[i]], 0))
        c3 = np.stack([cp[2 * i], cp[2 * i + 1], cs[i]], 0)
        m["cT"] = np.ascontiguousarray(c3.reshape(3, 8, 128).transpose(2, 1, 0))
        in_maps.append(m)
    nc = build_program(LB, stages, debug)
    res = run_bass_kernel_spmd(nc, in_maps, core_ids=list(range(ncores)))
    if debug:
        DEBUG_OUT.update(res.results[0])
    yp = np.zeros_like(xp)
    ys = np.zeros_like(xs)
    for i in range(ncores):
        y = res.results[i]["yout"]
        yp[2 * i] = y[0:LB]
        yp[2 * i + 1] = y[LB:2 * LB]
        ys[i] = y[2 * LB:4 * LB]
    return yp, ys


def kernel(**inputs):
    yp, ys = run_cores(inputs, 4096, 8)
    return (yp, ys)
```

```python
import contextlib
import math
import numpy as np
import ml_dtypes
import concourse.bass as bass
import concourse.mybir as mybir
from concourse.bass_utils import run_bass_kernel_spmd

F32 = mybir.dt.float32
BF16 = mybir.dt.bfloat16
ALU = mybir.AluOpType
AF = mybir.ActivationFunctionType

ENGS = ("pe", "act", "dve", "pool", "sp")
D = 1024
DFF = 2816
NJF = DFF // 128
EPS = 1e-6
MAG = 12582912.0
TWO_PI = 2.0 * math.pi


class Counter:
    def __init__(self, name, step, epoch):
        self.name, self.step, self.epoch = name, step, epoch
        self.n = 0
        self.sems = []

    def nxt(self):
        self.n += 1
        return self.n

    def nsems(self):
        return max(1, (self.n + self.epoch - 1) // self.epoch)

    def loc(self, n):
        return (n - 1) // self.epoch, ((n - 1) % self.epoch + 1) * self.step


_CUR = [None]


class Buf:
    __slots__ = ("name", "lw", "rd", "_ctr", "ap")

    def __init__(self, name, ap=None):
        self.name = name
        self.lw = None
        self.rd = {}
        self._ctr = None
        self.ap = ap

    @property
    def ctr(self):
        if self._ctr is None:
            self._ctr = _CUR[0].dma_counter(self.name)
        return self._ctr

    @ctr.setter
    def ctr(self, v):
        self._ctr = v


class Sync:
    def __init__(self):
        self.ops = {e: [] for e in ENGS}
        self.cnt = {e: Counter("c_" + e, 1, 30000) for e in ENGS}
        self.known = {e: {} for e in ENGS}
        self.counters = list(self.cnt.values())
        self.nd = 0
        self.free = []
        self.phase_ctrs = []
        _CUR[0] = self

    def dma_counter(self, name):
        if self.free:
            c = self.free.pop()
        else:
            self.nd += 1
            c = Counter(f"d{self.nd}", 16, 1800)
            self.counters.append(c)
        self.phase_ctrs.append(c)
        return c

    def phase_begin(self):
        self.phase_ctrs = []

    def phase_end(self):
        self.barrier()
        self.free.extend(self.phase_ctrs)
        self.phase_ctrs = []

    def _wait(self, eng, ctr, n):
        k = self.known[eng]
        if k.get(ctr, 0) >= n:
            return
        k[ctr] = n
        self.ops[eng].append(("w", ctr, n))

    def _deps(self, eng, reads, writes):
        deps = {}
        for b in reads:
            if b.lw is not None:
                c, n = b.lw
                if deps.get(c, 0) < n:
                    deps[c] = n
        for b in writes:
            if b.lw is not None:
                c, n = b.lw
                if deps.get(c, 0) < n:
                    deps[c] = n
            for c, n in b.rd.items():
                if deps.get(c, 0) < n:
                    deps[c] = n
        mine = self.cnt[eng]
        for c, n in deps.items():
            if c is mine and eng == "pe":
                continue
            self._wait(eng, c, n)

    def _commit(self, ctr, n, reads, writes):
        for b in reads:
            if b.rd.get(ctr, 0) < n:
                b.rd[ctr] = n
        for b in writes:
            b.lw = (ctr, n)
            b.rd = {}

    def ins(self, eng, fn, reads=(), writes=()):
        self._deps(eng, reads, writes)
        ctr = self.cnt[eng]
        n = ctr.nxt()
        self.ops[eng].append(("i", fn, ctr, n))
        self._commit(ctr, n, reads, writes)

    def dma(self, eng, out_ap, in_ap, ctr, reads=(), writes=()):
        self._deps(eng, reads, writes)
        n = ctr.nxt()
        self.ops[eng].append(("i", lambda e: e.dma_start(out=out_ap, in_=in_ap), ctr, n))
        self._commit(ctr, n, reads, writes)

    def barrier(self):
        for e in ENGS:
            for c in self.counters:
                if c.n > 0:
                    self._wait(e, c, c.n)

    def emit(self, nc):
        for c in self.counters:
            if c.n > 0:
                self._wait("sp", c, c.n)
        with contextlib.ExitStack() as st:
            for c in self.counters:
                c.sems = [st.enter_context(nc.semaphore(f"{c.name}_{i}")) for i in range(c.nsems())]
            block = st.enter_context(nc.Block())

            def run(eng_name):
                def f(e):
                    for op in self.ops[eng_name]:
                        if op[0] == "w":
                            _, c, n = op
                            ei, v = c.loc(n)
                            e.wait_ge(c.sems[ei], v)
                        else:
                            _, fn, c, n = op
                            ei, _v = c.loc(n)
                            fn(e).then_inc(c.sems[ei], c.step)
                return f

            block.tensor(run("pe"))
            block.scalar(run("act"))
            block.vector(run("dve"))
            block.gpsimd(run("pool"))
            block.sync(run("sp"))


def MM(out, lhsT, rhs, start, stop):
    return lambda e: e.matmul(out, lhsT, rhs, start=start, stop=stop)


def TR(out, in_, ident):
    return lambda e: e.transpose(out, in_, ident)


def ACT(out, in_, func, bias=0.0, scale=1.0):
    return lambda e: e.activation(out, in_, func, bias=bias, scale=scale)


def TTO(out, in0, in1, op):
    return lambda e: e.tensor_tensor(out, in0, in1, op=op)


def TS(out, in0, s1, s2, op0, op1=None):
    if op1 is None:
        return lambda e: e.tensor_scalar(out, in0, s1, None, op0=op0)
    return lambda e: e.tensor_scalar(out, in0, s1, s2, op0=op0, op1=op1)


def STT(out, in0, scalar, in1, op0, op1, accum_out=None):
    if accum_out is None:
        return lambda e: e.scalar_tensor_tensor(out, in0, scalar, in1, op0=op0, op1=op1)
    return lambda e: e.scalar_tensor_tensor(out, in0, scalar, in1, op0=op0, op1=op1, accum_out=accum_out)


def CP(out, in_):
    return lambda e: e.tensor_copy(out, in_)


def ACP(out, in_):
    return lambda e: e.copy(out, in_)


def RECIP(out, in_):
    return lambda e: e.reciprocal(out, in_)


def MSET(ap, v):
    return lambda e: e.memset(ap, v)


def bcast_rows(ap, p=128):
    pairs = [list(x) for x in ap.ap]
    return bass.AP(ap.tensor, ap.offset, [[0, p]] + pairs[1:])


def signal_defs(LB):
    m = np.arange(LB)
    jm1 = np.maximum(m - 1, 0)
    sig = [
        (0, m, LB, False),
        (1, jm1, LB, True),
        (0, m, 2 * LB, False),
        (1, jm1, 2 * LB, True),
        (1, LB - 1 - m, 2 * LB, False),
        (1, LB - 1 + m, 2 * LB, True),
        (0, LB + m, 2 * LB, False),
        (0, np.where(m == 0, LB, LB - m), 2 * LB, True),
    ]
    return sig


def host_consts(LB):
    TC = LB // 128
    KC = LB // 128
    N = 2 * LB
    c = {}
    c["ident"] = np.eye(128, dtype=np.float32)
    c["jrev"] = np.ascontiguousarray(np.eye(64, dtype=np.float32)[::-1])
    qc = np.arange(64)[None, :]
    kc = np.arange(64)[:, None]
    st = np.clip(qc - 8, 0, 48)
    cm = ((kc >= st) & (kc < st + 16)).astype(np.float32)
    c["cmask"] = np.concatenate([cm, cm], 0)
    deltas = np.abs(np.linspace(math.log(1e-2) / 0.3, math.log(1e-2) / 1.5, 512, dtype=np.float32)).astype(np.float32)
    c["deltas"] = deltas[None, :]
    sigs = signal_defs(LB)
    zf = np.zeros((8, 17, LB), np.float32)
    negt = np.zeros((128, 8, TC), np.float32)
    f = np.linspace(1e-4, 7.0, 8, dtype=np.float32)[None, :].astype(np.float64)
    for i, (dr, pos, L, mk) in enumerate(sigs):
        t = pos.astype(np.float64) / (L - 1)
        w = 2.0 * math.pi * pos.astype(np.float64)[:, None] / L
        z = np.concatenate([t[:, None], np.cos(f * w), -np.sin(f * w)], -1)
        zf[i] = z.T.astype(np.float32)
        negt[:, i, :] = (-t).astype(np.float32).reshape(TC, 128).T
    c["zfeat"] = zf
    c["negt"] = negt
    N2 = LB // 64
    bf = ml_dtypes.bfloat16
    t1 = np.arange(64, dtype=np.float64)[:, None, None]
    t2 = np.arange(N2, dtype=np.float64)[None, :, None]
    k1 = np.arange(64, dtype=np.float64)[None, None, :]
    th = np.mod((2.0 * math.pi / N) * ((k1 + 0.5) * (N2 * t1 + t2)), 2.0 * math.pi)
    F1 = np.concatenate([np.cos(th), -np.sin(th)], -1)
    c["F1"] = np.ascontiguousarray(F1.reshape(64, N2 * 128)).astype(bf)
    psi = th.transpose(2, 1, 0)
    F2i = (2.0 / N) * np.concatenate([np.cos(psi), -np.sin(psi)], 0)
    c["F2i"] = np.ascontiguousarray(F2i.reshape(128, N2 * 64)).astype(bf)
    a2 = np.arange(N2, dtype=np.float64)
    ph = np.mod((2.0 * math.pi / N2) * np.outer(a2, a2), 2.0 * math.pi)
    C2, S2 = np.cos(ph), np.sin(ph)
    sm = np.stack([np.concatenate([C2, -S2], 1), np.concatenate([S2, C2], 1), np.concatenate([C2, C2], 1),
                   np.concatenate([S2, S2], 1), np.concatenate([-S2, -S2], 1), np.concatenate([-C2, -C2], 1)], 1)
    c["smalls"] = np.ascontiguousarray(sm.reshape(N2, 12 * N2)).astype(bf)
    CC = np.concatenate([C2, C2], 1)
    SS = np.concatenate([S2, S2], 1)
    sm2 = np.stack([np.concatenate([CC, CC], 0), np.concatenate([SS, SS], 0),
                    np.concatenate([-SS, SS], 0), np.concatenate([CC, -CC], 0)], 1)
    c["smalls2"] = np.ascontiguousarray(sm2.reshape(2 * N2, 8 * N2)).astype(bf)
    G1 = np.block([[C2, S2], [-S2, C2]])
    G2 = np.block([[-S2, C2], [-C2, -S2]])
    c["Gm"] = np.ascontiguousarray(np.stack([G1, G2], 1).reshape(2 * N2, 4 * N2)).astype(bf)
    negt2 = np.zeros((64, 8, N2), np.float32)
    for i, (dr, pos, L, mk) in enumerate(sigs):
        tt = pos.astype(np.float64) / (L - 1)
        negt2[:, i, :] = (-tt).astype(np.float32).reshape(64, N2)
    c["negt2"] = negt2
    del c["negt"]
    return c


CONST_SHAPES = None


def build_program(LB, stages=("ffn", "attn", "hyena"), debug=False):
    nc = bass.Bass("TRN2", target_bir_lowering=False)
    S = Sync()
    TC = LB // 128
    KC = LB // 128
    NT = 4 * LB
    TT = 512
    NTILES = NT // TT
    seqs = [(0, 1), (LB, 1), (2 * LB, 2)]
    NSEQ = 3

    def tile_seq(tok):
        return 0 if tok < LB else (1 if tok < 2 * LB else 2)

    def din(name, shape, dt=F32):
        return nc.dram_tensor(name, list(shape), dt, kind="ExternalInput").ap()

    def dscr(name, shape, dt):
        if debug and name in ("s_qk", "s_v", "s_mixa", "s_hyp", "s_hyc", "s_v0", "s_z1", "s_z2", "s_hs2"):
            return nc.dram_tensor(name, list(shape), dt, kind="ExternalOutput").ap()
        return nc.dram_tensor(name, list(shape), dt).ap()

    xin = din("xin", [NT, D])
    cT = din("cT", [128, 8, NSEQ])
    w_ada = din("w_ada", [D, 9 * D])
    b_adaT = din("b_adaT", [128, 72])
    nrm = din("nrm", [128, 4, 8])
    wsrc = {
        "f1g": din("f1g", [D, DFF]), "f1u": din("f1u", [D, DFF]), "f1d": din("f1d", [DFF, D]),
        "f2g": din("f2g", [D, DFF]), "f2u": din("f2u", [D, DFF]), "f2d": din("f2d", [DFF, D]),
        "win": din("win", [D, 3072]), "wout": din("wout", [D, D]),
    }
    rpbp = din("rpbp", [8, 15, 128])
    convw = din("convw", [1, 3 * 1536])
    convb = din("convb", [1, 1536])
    hw1 = din("hw1", [17, 64])
    hw2 = din("hw2", [64, 64])
    hw3 = din("hw3", [64, 64])
    hwo = din("hwo", [64, 2048])
    hb = din("hb", [64, 4])
    skipv = din("skipv", [1, 1024])
    anorm = din("anorm", [1, 512])
    hnorm = din("hnorm", [1, 512])
    ident_d = din("ident", [128, 128])
    jrev_d = din("jrev", [64, 64])
    cmask_d = din("cmask", [128, 64])
    deltas_d = din("deltas", [1, 512])
    zfeat_d = din("zfeat", [8, 17, LB])
    NN2 = LB // 64
    negt2_d = din("negt2", [64, 8, NN2])
    F1_d = din("F1", [64, NN2 * 128], BF16)
    F2i_d = din("F2i", [128, NN2 * 64], BF16)
    smalls_d = din("smalls", [NN2, 12 * NN2], BF16)
    smalls2_d = din("smalls2", [2 * NN2, 8 * NN2], BF16)
    G_d = din("Gm", [2 * NN2, 4 * NN2], BF16)
    yout = nc.dram_tensor("yout", [NT, D], F32, kind="ExternalOutput").ap()

    Wl = {
        "f1g": dscr("s_f1g", [NJF, 128, 8 * 128], BF16), "f1u": dscr("s_f1u", [NJF, 128, 8 * 128], BF16),
        "f1d": dscr("s_f1d", [8, 128, NJF * 128], BF16),
        "f2g": dscr("s_f2g", [NJF, 128, 8 * 128], BF16), "f2u": dscr("s_f2u", [NJF, 128, 8 * 128], BF16),
        "f2d": dscr("s_f2d", [8, 128, NJF * 128], BF16),
        "winl": dscr("s_winl", [8, 128, 8 * 128], BF16), "winr": dscr("s_winr", [4, 128, 8 * 512], BF16),
        "wout": dscr("s_wout", [8, 128, 8 * 128], BF16),
    }
    X1 = dscr("s_x1", [8, 128, NT], F32)
    QK = dscr("s_qk", [8, 128, NT], BF16)
    Vd = dscr("s_v", [NT, 640], BF16)
    HYP = dscr("s_hyp", [NT, 1536], F32)
    HYC = dscr("s_hyc", [NT, 1536], F32)
    V0 = dscr("s_v0", [NT, 512], BF16)
    Z1 = dscr("s_z1", [NT, 512], BF16)
    Z2 = dscr("s_z2", [NT, 512], F32)
    MIXA = dscr("s_mixa", [NT, 512], BF16)
    HS2 = dscr("s_hs2", [4, 2, 2, 2 * NN2, 64, 512], BF16)
    RL1 = dscr("s_rl1", [2, 2, 64, 512], F32)
    B1 = dscr("s_b1", [2, 128, NN2, 512], BF16)
    B2 = dscr("s_b2", [2, 2 * NN2, 64, 512], BF16)

    es_all = contextlib.ExitStack()

    uid = [0]

    def alloc(es, name, shape, dt, dmac=True):
        uid[0] += 1
        name = f"sb{uid[0]}_{name}"
        t = es.enter_context(nc.sbuf_tensor(name, list(shape), dt))
        b = Buf(name, t.ap())
        return b

    P = es_all
    ident = alloc(P, "ident", [128, 128], F32)
    identb = alloc(P, "identb", [128, 128], BF16, False)
    ones = alloc(P, "ones", [128, 128], BF16, False)
    epst = alloc(P, "epst", [128, 1], F32, False)
    par = alloc(P, "par", [128, NSEQ, 9, 8], F32, False)
    gfin = alloc(P, "gfin", [128, 8], F32)
    banks = []
    for i in range(8):
        t = nc.alloc_psum_tensor(f"bank{i}", [128, 512], F32)
        banks.append(Buf(f"bank{i}", t.ap()))
    bank_i = [0]

    def nextbank():
        b = banks[bank_i[0] % 7]
        bank_i[0] += 1
        return b

    ev_i = [0]

    def evac(out_ap, in_ap, reads, writes):
        ev_i[0] += 1
        if ev_i[0] % 2:
            S.ins("act", ACP(out_ap, in_ap), reads, writes)
        else:
            S.ins("dve", CP(out_ap, in_ap), reads, writes)

    S.dma("sp", ident.ap, ident_d, ident.ctr, writes=[ident])
    S.ins("dve", CP(identb.ap, ident.ap), [ident], [identb])
    S.ins("pool", MSET(ones.ap, 1.0), (), [ones])
    S.ins("pool", MSET(epst.ap, EPS), (), [epst])
    S.dma("sp", gfin.ap, nrm[:, 3, :], gfin.ctr, writes=[gfin])

    with contextlib.ExitStack() as es:
        S.phase_begin()
        c_sb = alloc(es, "c_sb", [128, 8, NSEQ], F32)
        sc_sb = alloc(es, "sc_sb", [128, 8, NSEQ], F32, False)
        bada = alloc(es, "bada", [128, 72], F32)
        nrm_sb = alloc(es, "nrm_sb", [128, 4, 8], F32)
        modsb = alloc(es, "modsb", [128, 72, NSEQ], F32, False)
        wslab = [alloc(es, f"wslab{i}", [128, 8, 512], F32) for i in range(2)]
        S.dma("sp", c_sb.ap, cT, c_sb.ctr, writes=[c_sb])
        S.dma("sp", bada.ap, b_adaT, bada.ctr, writes=[bada])
        S.dma("sp", nrm_sb.ap, nrm, nrm_sb.ctr, writes=[nrm_sb])
        S.ins("act", ACT(sc_sb.ap, c_sb.ap, AF.Silu), [c_sb], [sc_sb])
        pm = banks[7]
        for sl in range(18):
            ws = wslab[sl % 2]
            S.dma("sp", ws.ap, w_ada[:, sl * 512:(sl + 1) * 512].rearrange("(k p) c -> p k c", p=128), ws.ctr, writes=[ws])
            for jj in range(4):
                j = sl * 4 + jj
                for kc in range(8):
                    S.ins("pe", MM(pm.ap[:, j * NSEQ:(j + 1) * NSEQ], ws.ap[:, kc, jj * 128:(jj + 1) * 128],
                                   sc_sb.ap[:, kc, :], kc == 0, kc == 7), [ws, sc_sb], [pm])
        pmv = pm.ap[:, 0:72 * NSEQ].rearrange("p (j s) -> p j s", s=NSEQ)
        for s in range(NSEQ):
            S.ins("dve", TTO(modsb.ap[:, :, s], pmv[:, :, s], bada.ap, ALU.add), [pm, bada], [modsb])
        for s in range(NSEQ):
            for sub in range(3):
                sh = modsb.ap[:, (sub * 3 + 0) * 8:(sub * 3 + 0) * 8 + 8, s]
                scl = modsb.ap[:, (sub * 3 + 1) * 8:(sub * 3 + 1) * 8 + 8, s]
                g = modsb.ap[:, (sub * 3 + 2) * 8:(sub * 3 + 2) * 8 + 8, s]
                S.ins("dve", STT(par.ap[:, s, sub * 3 + 0, :], scl, 1.0, nrm_sb.ap[:, sub, :], ALU.add, ALU.mult),
                      [modsb, nrm_sb], [par])
                S.ins("dve", CP(par.ap[:, s, sub * 3 + 1, :], sh), [modsb], [par])
                S.ins("dve", TS(par.ap[:, s, sub * 3 + 2, :], g, 1.0 if sub == 1 else 0.5, None, ALU.mult), [modsb], [par])
        S.phase_end()

    with contextlib.ExitStack() as es:
        S.phase_begin()
        f32s = [alloc(es, f"wc32_{i}", [128, 2816], F32) for i in range(2)]
        b16s = [alloc(es, f"wc16_{i}", [128, 2816], BF16) for i in range(2)]
        it = [0]

        def cast_w(src, K, col0, ncols, dst, TW):
            KCn = K // 128
            CW = 256 if KCn == 8 else 128
            for cs in range(col0, col0 + ncols, CW):
                a = f32s[it[0] % 2]
                b = b16s[it[0] % 2]
                it[0] += 1
                av = a.ap[:, 0:KCn * CW].rearrange("p (k c) -> p k c", c=CW)
                bv = b.ap[:, 0:KCn * CW].rearrange("p (k c) -> p k c", c=CW)
                S.dma("sp", av, src[:, cs:cs + CW].rearrange("(k p) c -> p k c", p=128), a.ctr, writes=[a])
                if it[0] % 2:
                    S.ins("act", ACP(bv, av), [a], [b])
                else:
                    S.ins("dve", CP(bv, av), [a], [b])
                rel = cs - col0
                if TW == 128:
                    for g in range(CW // 128):
                        j = (rel + g * 128) // 128
                        dv = dst[j].rearrange("p (k c) -> p k c", c=128)
                        S.dma("pool", dv, bv[:, :, g * 128:(g + 1) * 128], b.ctr, reads=[b])
                else:
                    g = rel // 512
                    off = rel % 512
                    dv = dst[g].rearrange("p (k c) -> p k c", c=512)
                    S.dma("pool", dv[:, :, off:off + CW], bv, b.ctr, reads=[b])

        if "ffn" in stages:
            for nm in ("f1g", "f1u", "f2g", "f2u"):
                cast_w(wsrc[nm], D, 0, DFF, Wl[nm], 128)
            for nm in ("f1d", "f2d"):
                cast_w(wsrc[nm], DFF, 0, D, Wl[nm], 128)
            cast_w(wsrc["win"], D, 0, 1024, Wl["winl"], 128)
            cast_w(wsrc["win"], D, 1024, 2048, Wl["winr"], 512)
            cast_w(wsrc["wout"], D, 0, D, Wl["wout"], 128)
        S.phase_end()

    N2 = LB // 64
    TCH = min(8, N2)
    KCH = 4
    if "hyena" in stages:
        with contextlib.ExitStack() as es:
            S.phase_begin()
            w1s = alloc(es, "w1s", [17, 64], F32)
            w2s = alloc(es, "w2s", [64, 64], F32)
            w3s = alloc(es, "w3s", [64, 64], F32)
            hbs = alloc(es, "hbs", [64, 4], F32)
            wob = alloc(es, "wob", [64, 2048], BF16, False)
            with contextlib.ExitStack() as esw:
                wo32 = alloc(esw, "wo32", [64, 2048], F32)
                S.dma("sp", wo32.ap, hwo, wo32.ctr, writes=[wo32])
                S.ins("dve", CP(wob.ap, wo32.ap), [wo32], [wob])
                S.barrier()
            dlt = alloc(es, "dlt", [64, 512], F32)
            ngt = alloc(es, "ngt", [64, 8, N2], F32)
            F1s = alloc(es, "F1s", [64, N2 * 128], BF16)
            sm = alloc(es, "sm", [N2, 12 * N2], BF16)
            S.dma("sp", w1s.ap, hw1, w1s.ctr, writes=[w1s])
            S.dma("sp", w2s.ap, hw2, w2s.ctr, writes=[w2s])
            S.dma("sp", w3s.ap, hw3, w3s.ctr, writes=[w3s])
            S.dma("sp", hbs.ap, hb, hbs.ctr, writes=[hbs])
            S.dma("sp", dlt.ap, bcast_rows(deltas_d, 64), dlt.ctr, writes=[dlt])
            S.dma("sp", ngt.ap, negt2_d, ngt.ctr, writes=[ngt])
            S.dma("sp", F1s.ap, F1_d, F1s.ctr, writes=[F1s])
            S.dma("sp", sm.ap, smalls_d, sm.ctr, writes=[sm])
            sm2 = alloc(es, "sm2", [2 * N2, 8 * N2], BF16)
            S.dma("sp", sm2.ap, smalls2_d, sm2.ctr, writes=[sm2])
            sm2v = sm2.ap.rearrange("p (s c) -> p s c", s=4)
            F1v = F1s.ap.rearrange("p (t c) -> p t c", c=128)
            smv = sm.ap.rearrange("p (s c) -> p s c", s=6)
            hbf = alloc(es, "hbf", [64, 3], F32, False)
            for li_ in range(3):
                S.ins("dve", TTO(hbf.ap[:, li_:li_ + 1], hbs.ap[:, li_:li_ + 1], hbs.ap[:, 3:4], ALU.mult), [hbs], [hbf])
            h3T = [alloc(es, f"h3T{i}", [64, LB], BF16, False) for i in range(6)]
            NCH = LB // 512
            GRP = min(4, NCH)
            zt = [alloc(es, f"zt{i}", [17, 512], F32) for i in range(GRP)]
            ta = [alloc(es, f"hta{i}", [64, 512], F32, False) for i in range(GRP)]
            tb = [alloc(es, f"htb{i}", [64, 512], F32, False) for i in range(GRP)]
            lay = [alloc(es, f"lay{i}", [64, GRP * 512], F32, False) for i in range(2)]
            layb = [[Buf(f"lay{i}_{g}") for g in range(GRP)] for i in range(2)]
            dec = [alloc(es, f"dec{i}", [64, 512], F32, False) for i in range(3)]
            hb16 = [alloc(es, f"hb16_{i}", [64, 512], BF16, False) for i in range(8)]
            rl1 = alloc(es, "rl1", [128, 512], F32, False)
            habs = [alloc(es, f"habs{i}", [64, 512], BF16, False) for i in range(4)]
            cnt_ha = 0
            stg = [alloc(es, f"fstg{i}", [128, TCH, 512], BF16) for i in range(2)]
            Ach = [alloc(es, f"fA{i}", [2 * N2, KCH, 512], BF16) for i in range(4)]
            Achb = [[Buf(f"fAh{i}_{k}") for k in range(2)] for i in range(4)]
            hst = [alloc(es, f"fhst{i}", [2 * N2, KCH, 512], BF16) for i in range(4)]
            sigs = signal_defs(LB)
            cnt = {"z": 0, "d": 0, "h": 0, "s": 0, "a": 0, "t": 0, "ha": 0}

            def hidden(sig, slot):
                for g0 in range(0, NCH, GRP):
                    chunks = list(range(g0, min(NCH, g0 + GRP)))
                    for gi, c in enumerate(chunks):
                        S.dma("sp", zt[gi].ap, zfeat_d[sig, :, c * 512:(c + 1) * 512], zt[gi].ctr, writes=[zt[gi]])
                    for li, wsb in enumerate((w1s, w2s, w3s)):
                        pbs = []
                        for gi, c in enumerate(chunks):
                            pb = nextbank()
                            if li == 0:
                                cur, curb = zt[gi].ap, zt[gi]
                            else:
                                cur, curb = lay[li - 1].ap[:, gi * 512:(gi + 1) * 512], layb[li - 1][gi]
                            S.ins("pe", MM(pb.ap[0:64, :], wsb.ap, cur, True, True), [wsb, curb], [pb])
                            pbs.append(pb)
                        for gi, c in enumerate(chunks):
                            S.ins("act", ACT(ta[gi].ap, pbs[gi].ap[0:64, :], AF.Identity, bias=hbf.ap[:, li:li + 1], scale=hbs.ap[:, 3:4]),
                                  [pbs[gi], hbs, hbf], [ta[gi]])
                        for gi, c in enumerate(chunks):
                            S.ins("dve", TS(tb[gi].ap, ta[gi].ap, 1.0 / TWO_PI, MAG, ALU.mult, ALU.add), [ta[gi]], [tb[gi]])
                        for gi, c in enumerate(chunks):
                            S.ins("dve", TS(tb[gi].ap, tb[gi].ap, MAG, None, ALU.subtract), [tb[gi]], [tb[gi]])
                        for gi, c in enumerate(chunks):
                            S.ins("dve", STT(ta[gi].ap, tb[gi].ap, -TWO_PI, ta[gi].ap, ALU.mult, ALU.add), [ta[gi], tb[gi]], [ta[gi]])
                        for gi, c in enumerate(chunks):
                            S.ins("dve", TS(ta[gi].ap, ta[gi].ap, math.pi, -math.pi, ALU.min, ALU.max), [ta[gi]], [ta[gi]])
                        for gi, c in enumerate(chunks):
                            if li < 2:
                                S.ins("act", ACT(lay[li].ap[:, gi * 512:(gi + 1) * 512], ta[gi].ap, AF.Sin), [ta[gi]], [layb[li][gi]])
                            else:
                                S.ins("act", ACT(h3T[slot].ap[:, c * 512:(c + 1) * 512], ta[gi].ap, AF.Sin), [ta[gi]], [h3T[slot]])

            def genA(sig, slot, order, t2, mask_after):
                dr = sigs[sig][0]
                off = (dr * 2 + order) * 512
                pb = nextbank()
                hv = h3T[slot].ap.rearrange("p (a b) -> p a b", b=N2)[:, :, t2]
                S.ins("pe", MM(pb.ap[0:64, :], hv, wob.ap[:, off:off + 512], True, True), [h3T[slot], wob], [pb])
                d = dec[cnt["d"] % 3]
                cnt["d"] += 1
                S.ins("act", ACT(d.ap, dlt.ap, AF.Exp, scale=ngt.ap[:, sig, t2:t2 + 1]), [dlt, ngt], [d])
                h = hb16[cnt["s"] % 8]
                cnt["s"] += 1
                S.ins("dve", TTO(h.ap, pb.ap[0:64, :], d.ap, ALU.mult), [pb, d], [h])
                return h

            def pipeline(n, stages, lag):
                st = [dict() for _ in range(n)]
                ns = len(stages)
                for step in range(n + (ns - 1) * lag):
                    for si in reversed(range(ns)):
                        k = step - si * lag
                        if 0 <= k < n:
                            stages[si](k, st[k])

            LAG = 2
            for fs in range(2):
                if fs == 0:
                    sl = {0: 0, 1: 1}
                    l1sigs = [(0, False), (1, True)]
                    pieces = [(0, 0, 1)]
                else:
                    sl = {2: 0, 3: 1, 4: 2, 5: 3, 6: 4, 7: 5}
                    l1sigs = [(2, False), (6, False), (3, True), (5, False)]
                    pieces = [(1, 2, 3), (2, 4, 5), (3, 6, 7)]
                for sg, slot in sl.items():
                    hidden(sg, slot)
                for order in range(2):
                    l1b = banks[7]
                    l1set = {sg: mk for (sg, mk) in l1sigs}
                    l1order = [sg for (pidx_, sa_, sb_) in pieces for sg in (sa_, sb_) if sg in l1set]
                    l1first, l1last = l1order[0], l1order[-1]
                    for (pidx, sa, sbb) in pieces:
                        units = [(ab, sg, msk, t2) for ab, (sg, msk) in enumerate(((sa, sigs[sa][3]), (sbb, True))) for t2 in range(N2)]

                        def pA(k, stt, units=units, order=order):
                            ab, sg, msk, t2 = units[k]
                            h = genA(sg, sl[sg], order, t2, False)
                            late_mask = (sg in l1set) and (not l1set[sg]) and msk
                            if msk and t2 == 0 and not late_mask:
                                S.ins("dve", MSET(h.ap[0:1, :], 0.0), (), [h])
                            stt["h"] = h
                            stt["late"] = late_mask and t2 == 0

                        def pB(k, stt, units=units):
                            ab, sg, msk, t2 = units[k]
                            xb = stt["h"]
                            if sg in l1set:
                                ha = habs[cnt["ha"] % 4]
                                cnt["ha"] += 1
                                S.ins("dve", STT(ha.ap, xb.ap, -1.0, xb.ap, ALU.mult, ALU.max), [xb], [ha])
                                stt["ha"] = ha
                            if stt["late"]:
                                S.ins("dve", MSET(xb.ap[0:1, :], 0.0), (), [xb])
                            stt["xb"] = xb

                        def pC(k, stt, units=units):
                            ab, sg, msk, t2 = units[k]
                            j = t2 % TCH
                            if j == 0:
                                stt["st"] = stg[cnt["t"] % 2]
                                cnt["t"] += 1
                                cnt["cur_st"] = stt["st"]
                            st_ = cnt["cur_st"]
                            pb = nextbank()
                            S.ins("pe", MM(pb.ap, F1v[:, t2, :], stt["xb"].ap, True, True), [F1s, stt["xb"]], [pb])
                            S.ins("act", ACP(st_.ap[:, j, :], pb.ap), [pb], [st_])
                            if j == TCH - 1:
                                t2a = t2 - j
                                S.dma("pool", B1[ab, :, t2a:t2a + TCH, :], st_.ap, st_.ctr, reads=[st_])
                            if sg in l1set:
                                S.ins("pe", MM(l1b.ap, ones.ap[0:64, :], stt["ha"].ap, sg == l1first and t2 == 0, sg == l1last and t2 == N2 - 1),
                                      [ones, stt["ha"]], [l1b])

                        pipeline(len(units), [pA, pB, pC], LAG)
                        S.barrier()
                        for k1a in range(0, 64, KCH):
                            bufs = {}
                            for c in range(2):
                                ai = cnt["a"] % 4
                                cnt["a"] += 1
                                bf = Ach[ai]
                                for ab in range(2):
                                    hbuf = Achb[ai][ab]
                                    S.dma("sp", bf.ap[ab * N2:(ab + 1) * N2, :, :],
                                          B1[ab, c * 64 + k1a: c * 64 + k1a + KCH, :, :].rearrange("k t c -> t k c"),
                                          hbuf.ctr, writes=[hbuf])
                                bufs[c] = (bf, Achb[ai])
                            hA = hst[(k1a // KCH) % 2 * 2]
                            hB = hst[(k1a // KCH) % 2 * 2 + 1]
                            for j in range(KCH):
                                pA_ = nextbank()
                                for qi, (si, c) in enumerate(((0, 0), (1, 1))):
                                    bf, hbs_ = bufs[c]
                                    S.ins("pe", MM(pA_.ap[0:2 * N2, :], sm2v[:, si, :], bf.ap[:, j, :], qi == 0, qi == 1), [sm2] + hbs_, [pA_])
                                S.ins("act", ACP(hA.ap[:, j, :], pA_.ap[0:2 * N2, :]), [pA_], [hA])
                                pB_ = nextbank()
                                for qi, (si, c) in enumerate(((2, 0), (3, 1))):
                                    bf, hbs_ = bufs[c]
                                    S.ins("pe", MM(pB_.ap[0:2 * N2, :], sm2v[:, si, :], bf.ap[:, j, :], qi == 0, qi == 1), [sm2] + hbs_, [pB_])
                                S.ins("act", ACP(hB.ap[:, j, :], pB_.ap[0:2 * N2, :]), [pB_], [hB])
                            S.dma("pool", HS2[pidx, order, 0, :, k1a:k1a + KCH, :], hA.ap, hA.ctr, reads=[hA])
                            S.dma("pool", HS2[pidx, order, 1, :, k1a:k1a + KCH, :], hB.ap, hB.ctr, reads=[hB])
                        S.barrier()
                    S.ins("dve", RECIP(rl1.ap, l1b.ap), [l1b], [rl1])
                    S.dma("sp", RL1[fs, order], rl1.ap[0:64, :], rl1.ctr, reads=[rl1])
                    S.barrier()
            S.phase_end()

    def norm_to(es_b, xT, xTb, s, aidx, bidx, outT, outTb):
        sq, rstd, tmp, tmpb = es_b["sq"], es_b["rstd"], es_b["tmp"], es_b["tmpb"]
        for kc in range(8):
            S.ins("dve", TTO(sq.ap[:, kc, :], xT.ap[:, kc, :], xT.ap[:, kc, :], ALU.mult), [xTb[kc]], [es_b["sqb"][kc]])
        pb = nextbank()
        for kc in range(8):
            S.ins("pe", MM(pb.ap, ones.ap, sq.ap[:, kc, :], kc == 0, kc == 7), [ones, es_b["sqb"][kc]], [pb])
        S.ins("act", ACT(rstd.ap, pb.ap, AF.Sqrt, bias=epst.ap[:, 0:1], scale=1.0 / D), [pb, epst], [rstd])
        S.ins("dve", RECIP(rstd.ap, rstd.ap), [rstd], [rstd])
        for kc in range(8):
            if aidx is None:
                sc_ap = gfin.ap[:, kc:kc + 1]
                S.ins("dve", STT(outT.ap[:, kc, :], xT.ap[:, kc, :], sc_ap, rstd.ap, ALU.mult, ALU.mult),
                      [xTb[kc], rstd, gfin], [outTb[kc]])
            else:
                S.ins("dve", STT(tmp.ap[:, kc, :], xT.ap[:, kc, :], par.ap[:, s, aidx, kc:kc + 1], rstd.ap, ALU.mult, ALU.mult),
                      [xTb[kc], rstd, par], [tmpb[kc]])
                S.ins("act", ACT(outT.ap[:, kc, :], tmp.ap[:, kc, :], AF.Identity, bias=par.ap[:, s, bidx, kc:kc + 1]),
                      [tmpb[kc], par], [outTb[kc]])

    ring_i = [0]

    def ring_load(ring, src2d, width):
        sl = ring[ring_i[0] % len(ring)]
        ring_i[0] += 1
        S.dma("sp", sl.ap[:, 0:width], src2d, sl.ctr, writes=[sl])
        return sl

    def ffn(es_b, ring, hnT, hnTb, xT, xTb, s, gidx, wg, wu, wd):
        aT, aTb, sgt = es_b["aT"], es_b["aTb"], es_b["sgt"]
        for j in range(NJF):
            sg = ring_load(ring, Wl[wg][j], 1024)
            su = ring_load(ring, Wl[wu][j], 1024)
            sgv = sg.ap[:, 0:1024].rearrange("p (k c) -> p k c", c=128)
            suv = su.ap[:, 0:1024].rearrange("p (k c) -> p k c", c=128)
            pg = nextbank()
            pu = nextbank()
            for kc in range(8):
                S.ins("pe", MM(pg.ap, sgv[:, kc, :], hnT.ap[:, kc, :], kc == 0, kc == 7), [sg, hnTb[kc]], [pg])
            for kc in range(8):
                S.ins("pe", MM(pu.ap, suv[:, kc, :], hnT.ap[:, kc, :], kc == 0, kc == 7), [su, hnTb[kc]], [pu])
            sl = sgt[j % 2]
            S.ins("act", ACT(sl.ap, pg.ap, AF.Silu), [pg], [sl])
            S.ins("dve", TTO(aT.ap[:, j, :], sl.ap, pu.ap, ALU.mult), [sl, pu], [aTb[j]])
        for oc in range(8):
            sd = ring_load(ring, Wl[wd][oc], NJF * 128)
            sdv = sd.ap[:, 0:NJF * 128].rearrange("p (k c) -> p k c", c=128)
            py = nextbank()
            for j in range(NJF):
                S.ins("pe", MM(py.ap, sdv[:, j, :], aT.ap[:, j, :], j == 0, j == NJF - 1), [sd, aTb[j]], [py])
            S.ins("dve", STT(xT.ap[:, oc, :], py.ap, par.ap[:, s, gidx, oc:oc + 1], xT.ap[:, oc, :], ALU.mult, ALU.add),
                  [py, par, xTb[oc]], [xTb[oc]])

    def ffn_bufs(es):
        b = {}
        b["sq"] = alloc(es, "sq", [128, 8, TT], BF16, False)
        b["sqb"] = [Buf(f"sq{k}") for k in range(8)]
        b["rstd"] = alloc(es, "rstd", [128, TT], F32, False)
        b["tmp"] = alloc(es, "tmp", [128, 8, TT], F32, False)
        b["tmpb"] = [Buf(f"tmp{k}") for k in range(8)]
        b["aT"] = alloc(es, "aT", [128, NJF, TT], BF16, False)
        b["aTb"] = [Buf(f"aT{k}") for k in range(NJF)]
        b["sgt"] = [alloc(es, f"sgt{i}", [128, TT], F32, False) for i in range(2)]
        return b

    if "ffn" in stages:
        with contextlib.ExitStack() as es:
            S.phase_begin()
            fb = ffn_bufs(es)
            ring = [alloc(es, f"ring{i}", [128, 4096], BF16) for i in range(6)]
            xtok = [alloc(es, f"xtok{i}", [128, D], F32) for i in range(4)]
            xTs = [(alloc(es, f"xT_{i}", [128, 8, TT], F32), [Buf(f"xT{i}_{k}") for k in range(8)]) for i in range(2)]
            hnT = alloc(es, "hnT", [128, 8, TT], BF16, False)
            hnTb = [Buf(f"hnT{k}") for k in range(8)]
            qk = alloc(es, "qk", [128, 8, TT], BF16)
            qkb = [Buf(f"qk{k}") for k in range(8)]
            vst = [alloc(es, f"vst{i}", [128, 8, 80], BF16) for i in range(2)]
            hstg = [alloc(es, f"hstg{i}", [128, 512], F32) for i in range(3)]
            for v in vst:
                S.ins("pool", MSET(v.ap, 1.0), (), [v])
            hq = 0
            for ti in range(NTILES):
                tok0 = ti * TT
                s = tile_seq(tok0)
                xT, xTb = xTs[ti % 2]
                for st in range(4):
                    S.dma("sp", xtok[st].ap, xin[tok0 + st * 128: tok0 + (st + 1) * 128, :], xtok[st].ctr, writes=[xtok[st]])
                for kc in range(8):
                    pb = nextbank()
                    for st in range(4):
                        S.ins("pe", TR(pb.ap[:, st * 128:(st + 1) * 128], xtok[st].ap[:, kc * 128:(kc + 1) * 128], ident.ap),
                              [xtok[st], ident], [pb])
                    evac(xT.ap[:, kc, :], pb.ap, [pb], [xTb[kc]])
                norm_to(fb, xT, xTb, s, 0, 1, hnT, hnTb)
                ffn(fb, ring, hnT, hnTb, xT, xTb, s, 2, "f1g", "f1u", "f1d")
                S.dma("pool", X1[:, :, tok0:tok0 + TT].rearrange("k p t -> p k t"), xT.ap, xT.ctr, reads=xTb)
                norm_to(fb, xT, xTb, s, 3, 4, hnT, hnTb)
                for j in range(8):
                    sw = ring_load(ring, Wl["winl"][j], 1024)
                    swv = sw.ap[:, 0:1024].rearrange("p (k c) -> p k c", c=128)
                    pb = nextbank()
                    for kc in range(8):
                        S.ins("pe", MM(pb.ap, swv[:, kc, :], hnT.ap[:, kc, :], kc == 0, kc == 7), [sw, hnTb[kc]], [pb])
                    evac(qk.ap[:, j, :], pb.ap, [pb], [qkb[j]])
                S.dma("pool", QK[:, :, tok0:tok0 + TT].rearrange("k p t -> p k t"), qk.ap, qk.ctr, reads=qkb)
                for g in range(4):
                    sw = ring_load(ring, Wl["winr"][g], 4096)
                    swv = sw.ap.rearrange("p (k c) -> p k c", c=512)
                    for st in range(4):
                        pb = nextbank()
                        for kc in range(8):
                            S.ins("pe", MM(pb.ap, hnT.ap[:, kc, st * 128:(st + 1) * 128], swv[:, kc, :], kc == 0, kc == 7),
                                  [sw, hnTb[kc]], [pb])
                        tk = tok0 + st * 128
                        if g == 0:
                            v = vst[st % 2]
                            evac(v.ap[:, :, 0:64], pb.ap.rearrange("p (h d) -> p h d", d=64), [pb], [v])
                            S.dma("pool", Vd[tk:tk + 128, :], v.ap.rearrange("p h d -> p (h d)"), v.ctr, reads=[v])
                        else:
                            h = hstg[hq % 3]
                            hq += 1
                            evac(h.ap, pb.ap, [pb], [h])
                            S.dma("pool", HYP[tk:tk + 128, (g - 1) * 512:g * 512], h.ap, h.ctr, reads=[h])
            S.phase_end()

    if "attn" in stages:
        with contextlib.ExitStack() as es:
            S.phase_begin()
            Tpp = alloc(es, "Tpp", [64, 8, 15, 64], F32)
            jrev = alloc(es, "jrev", [64, 64], F32)
            cmask = alloc(es, "cmask", [128, 64], F32)
            T2 = alloc(es, "T2", [128, 8 * 14, 64], F32, False)
            gat = alloc(es, "gat", [128, 512], F32)
            S.dma("sp", jrev.ap, jrev_d, jrev.ctr, writes=[jrev])
            S.dma("sp", cmask.ap, cmask_d, cmask.ctr, writes=[cmask])
            S.dma("sp", gat.ap, bcast_rows(anorm), gat.ctr, writes=[gat])
            for h in range(8):
                src = bass.AP(rpbp.tensor, rpbp.offset + h * 15 * 128, [[1, 64], [128, 15], [1, 64]])
                S.dma("sp", Tpp.ap[:, h, :, :], src, Tpp.ctr, writes=[Tpp])
            import os
            ACUT = int(os.environ.get("ACUT", "9"))
            if ACUT >= -1:
                S.ins("act", ACT(Tpp.ap, Tpp.ap, AF.Exp), [Tpp], [Tpp])
            for bk in (range(14) if ACUT >= 0 else []):
                pb = nextbank()
                for sl in range(8):
                    fl = bk * 8 + sl
                    h, d = fl // 14, fl % 14
                    for half in range(2):
                        S.ins("pe", MM(pb.ap[64 * half:64 * half + 64, sl * 64:(sl + 1) * 64], Tpp.ap[:, h, d + half, :], jrev.ap, True, True),
                              [Tpp, jrev], [pb])
                if ACUT >= 1:
                    S.ins("dve", TTO(T2.ap[:, bk * 8:(bk + 1) * 8, :], pb.ap.rearrange("p (s c) -> p s c", c=64),
                                cmask.ap.unsqueeze(1).to_broadcast([128, 8, 64]), ALU.mult), [pb, cmask], [T2])
            T2v = T2.ap.rearrange("p (h d) c -> p h d c", d=14)
            if debug:
                dbgT2 = nc.dram_tensor("dbgT2", [128, 112 * 64], F32, kind="ExternalOutput").ap()
                dbgE = nc.dram_tensor("dbgE", [128, 512], F32, kind="ExternalOutput").ap()
                dbgP = nc.dram_tensor("dbgP", [128, 512], BF16, kind="ExternalOutput").ap()
                dbgO = nc.dram_tensor("dbgO", [128, 512], F32, kind="ExternalOutput").ap()
                dbgA = nc.dram_tensor("dbgA", [128, 512], F32, kind="ExternalOutput").ap()
                dctr = S.dma_counter("dbg")
                S.dma("sp", dbgT2, T2.ap.rearrange("p a c -> p (a c)"), dctr, reads=[T2])
            QB = min(32, LB // 64)
            KMAX = QB + 8
            NE = (KMAX + 2) // 2
            qT = alloc(es, "qT", [128, 4, QB * 64], BF16)
            kT = alloc(es, "kT", [128, 4, (KMAX + 1) * 64], BF16)
            Ve = alloc(es, "Ve", [128, NE, 640], BF16)
            Vo = alloc(es, "Vo", [128, NE, 640], BF16)
            Est = [alloc(es, f"Est{i}", [128, 512], F32, False) for i in range(4)]
            Pst = [alloc(es, f"Pst{i}", [128, 512], BF16, False) for i in range(4)]
            rec = alloc(es, "rec", [128, 8], F32, False)
            att = alloc(es, "att", [128, 512], F32, False)
            junk = alloc(es, "junk", [128, 512], F32, False)
            ssq = alloc(es, "ssq", [128, 1], F32, False)
            ans = [alloc(es, f"ans{i}", [128, 512], BF16) for i in range(2)]
            pq = 0
            for (sb, nb) in (seqs if ACUT >= 2 else []):
                rows = nb * LB // 64
                for qb0 in range(0, rows, QB):
                    r_lo, r_hi = qb0, qb0 + QB
                    klo = min(max(r_lo - 4, 0), rows - 8)
                    khi = min(max(r_hi - 1 - 4, 0), rows - 8) + 8
                    klo_e = klo - (klo % 2)
                    nk = khi - klo_e
                    ne = (nk + 1) // 2
                    no = (nk - 1) // 2
                    S.dma("sp", qT.ap, QK[0:4, :, sb + r_lo * 64: sb + r_hi * 64].rearrange("k p t -> p k t"), qT.ctr, writes=[qT])
                    S.dma("sp", kT.ap[:, :, 0:nk * 64], QK[4:8, :, sb + klo_e * 64: sb + (klo_e + nk) * 64].rearrange("k p t -> p k t"),
                          kT.ctr, writes=[kT])
                    t0 = sb + klo_e * 64
                    S.dma("sp", Ve.ap[:, 0:ne, :], Vd[t0:t0 + ne * 128, :].rearrange("(m p) c -> p m c", p=128), Ve.ctr, writes=[Ve])
                    if no > 0:
                        S.dma("sp", Vo.ap[:, 0:no, :], Vd[t0 + 64:t0 + 64 + no * 128, :].rearrange("(m p) c -> p m c", p=128),
                              Vo.ctr, writes=[Vo])
                    Sset = [(banks[0], banks[1]), (banks[2], banks[3])]
                    Oset = [(banks[4], banks[5]), (banks[6], banks[7])]
                    units = [(rr, hq) for rr in range(r_lo, r_hi) for hq in range(2)]

                    def rowinfo(rr):
                        r0 = min(max(rr - 4, 0), rows - 8)
                        d0 = r0 - rr + 7
                        pr = (r0 - klo_e) % 2
                        m0 = (r0 - klo_e - pr) // 2
                        return r0, d0, (Vo if pr else Ve), m0, (r0 - klo_e) * 64, (rr - r_lo) * 64

                    def attA(u):
                        rr, hq = units[u]
                        r0, d0, Vt, m0, kofs, qofs = rowinfo(rr)
                        sbs = Sset[u % 2]
                        for hpi in range(2):
                            hp = 2 * hq + hpi
                            for i in range(4):
                                for hh in range(2):
                                    p0 = 64 * hh
                                    S.ins("pe", MM(sbs[hh].ap[:, (hpi * 4 + i) * 64:(hpi * 4 + i + 1) * 64],
                                                   kT.ap[p0:p0 + 64, hp, kofs + 128 * i: kofs + 128 * (i + 1)],
                                                   qT.ap[p0:p0 + 64, hp, qofs:qofs + 64], True, True), [kT, qT], [sbs[hh]])

                    def attB(u):
                        nonlocal pq
                        rr, hq = units[u]
                        r0, d0, Vt, m0, kofs, qofs = rowinfo(rr)
                        sbs = Sset[u % 2]
                        ob = Oset[rr % 2]
                        for hh in range(2):
                            E = Est[pq % 4]
                            Pb = Pst[pq % 4]
                            pq += 1
                            h0 = 4 * hq + hh
                            S.ins("act", ACT(E.ap, sbs[hh].ap, AF.Exp, scale=0.125), [sbs[hh]], [E])
                            S.ins("dve", TTO(Pb.ap.rearrange("p (h i c) -> p h i c", h=2, i=4),
                                            E.ap.rearrange("p (h i c) -> p h i c", h=2, i=4),
                                            T2v[:, h0:h0 + 3:2, d0:d0 + 7:2, :], ALU.mult), [E, T2], [Pb])
                            for hpi in range(2):
                                h = 4 * hq + 2 * hpi + hh
                                for i in range(4):
                                    S.ins("pe", MM(ob[hq].ap[0:64, (h % 4) * 80:(h % 4) * 80 + 66],
                                                   Pb.ap[:, (hpi * 4 + i) * 64:(hpi * 4 + i + 1) * 64],
                                                   Vt.ap[:, m0 + i, h * 80:h * 80 + 66], i == 0, i == 3), [Pb, Vt], [ob[hq]])
                        if hq == 1:
                            r = rr
                            for bk in range(2):
                                obv = ob[bk].ap[0:64, 0:320].rearrange("p (h d) -> p h d", d=80)
                                S.ins("dve", RECIP(rec.ap[0:64, bk * 4:bk * 4 + 4], obv[:, :, 64]), [ob[bk]], [rec])
                                S.ins("dve", TTO(att.ap[0:64, bk * 256:(bk + 1) * 256].rearrange("p (h d) -> p h d", d=64), obv[:, :, 0:64],
                                                rec.ap[0:64, bk * 4:bk * 4 + 4].unsqueeze(2).to_broadcast([64, 4, 64]), ALU.mult),
                                      [ob[bk], rec], [att])
                            S.ins("dve", STT(junk.ap[0:64, :], att.ap[0:64, :], 1.0, att.ap[0:64, :], ALU.mult, ALU.mult, accum_out=ssq.ap[0:64, :]), [att], [junk, ssq])
                            S.ins("act", ACT(ssq.ap[0:64, :], ssq.ap[0:64, :], AF.Sqrt, bias=epst.ap[0:64, 0:1], scale=1.0 / 512), [ssq, epst], [ssq])
                            S.ins("dve", RECIP(ssq.ap[0:64, :], ssq.ap[0:64, :]), [ssq], [ssq])
                            an = ans[r % 2]
                            S.ins("dve", STT(an.ap[0:64, :], att.ap[0:64, :], ssq.ap[0:64, 0:1], gat.ap[0:64, :], ALU.mult, ALU.mult), [att, ssq, gat], [an])
                            S.dma("pool", MIXA[sb + r * 64: sb + r * 64 + 64, :], an.ap[0:64, :], an.ctr, reads=[an])

                    attA(0)
                    for u in range(len(units)):
                        if u + 1 < len(units):
                            attA(u + 1)
                        attB(u)

            S.phase_end()

    if "hyena" in stages:
        with contextlib.ExitStack() as es:
            S.phase_begin()
            cw = alloc(es, "cw", [128, 3, 1536], F32)
            cb = alloc(es, "cb", [128, 1536], F32)
            S.dma("sp", cw.ap, bcast_rows(convw).rearrange("p (i c) -> p i c", i=3), cw.ctr, writes=[cw])
            S.dma("sp", cb.ap, bcast_rows(convb), cb.ctr, writes=[cb])
            G = 4
            NB3 = 2
            At = [alloc(es, f"cA{i}", [128, G, 512], F32) for i in range(NB3)]
            Bt = [alloc(es, f"cB{i}", [128, G, 512], F32) for i in range(NB3)]
            Ct = [alloc(es, f"cC{i}", [128, G, 512], F32) for i in range(NB3)]
            Ot = [alloc(es, f"cO{i}", [128, G, 512], F32) for i in range(NB3)]
            Ob = [alloc(es, f"cOb{i}", [128, G, 512], BF16) for i in range(NB3)]
            dq = ["sp", "act"]
            q = 0
            for (sb, nb) in seqs:
                ntl = nb * TC
                for part in range(3):
                    c0 = part * 512
                    for tl in range(0, ntl, G):
                        tok = sb + tl * 128
                        a, b, c, o, ob_ = At[q % NB3], Bt[q % NB3], Ct[q % NB3], Ot[q % NB3], Ob[q % NB3]
                        eng = "pool" if q % 3 == 2 else "dve"
                        dqe = "sp"
                        q += 1

                        def rows(r0, n):
                            return HYP[r0:r0 + n * 128, c0:c0 + 512].rearrange("(m p) c -> p m c", p=128)
                        if tl == 0:
                            S.ins(eng, MSET(a.ap[0:1, 0, :], 0.0), (), [a])
                            S.dma(dqe, a.ap[1:128, 0, :], HYP[tok:tok + 127, c0:c0 + 512], a.ctr, writes=[a])
                            S.dma(dqe, a.ap[:, 1:G, :], rows(tok + 127, G - 1), a.ctr, writes=[a])
                        else:
                            S.dma(dqe, a.ap, rows(tok - 1, G), a.ctr, writes=[a])
                        S.dma(dqe, b.ap, rows(tok, G), b.ctr, writes=[b])
                        if tl + G >= ntl:
                            S.ins(eng, MSET(c.ap[:, G - 1, :], 0.0), (), [c])
                            S.dma(dqe, c.ap[:, 0:G - 1, :], rows(tok + 1, G - 1), c.ctr, writes=[c])
                            lt = tok + (G - 1) * 128
                            S.dma(dqe, c.ap[0:127, G - 1, :], HYP[lt + 1:lt + 128, c0:c0 + 512], c.ctr, writes=[c])
                        else:
                            S.dma(dqe, c.ap, rows(tok + 1, G), c.ctr, writes=[c])
                        w0 = cw.ap[:, 0, c0:c0 + 512].unsqueeze(1).to_broadcast([128, G, 512])
                        w1 = cw.ap[:, 1, c0:c0 + 512].unsqueeze(1).to_broadcast([128, G, 512])
                        w2 = cw.ap[:, 2, c0:c0 + 512].unsqueeze(1).to_broadcast([128, G, 512])
                        bb = cb.ap[:, c0:c0 + 512].unsqueeze(1).to_broadcast([128, G, 512])
                        S.ins(eng, TTO(a.ap, a.ap, w0, ALU.mult), [a, cw], [a])
                        S.ins(eng, TTO(b.ap, b.ap, w1, ALU.mult), [b, cw], [b])
                        S.ins(eng, TTO(c.ap, c.ap, w2, ALU.mult), [c, cw], [c])
                        S.ins(eng, TTO(a.ap, a.ap, b.ap, ALU.add), [a, b], [a])
                        S.ins(eng, TTO(c.ap, c.ap, bb, ALU.add), [c, cb], [c])
                        if part == 0:
                            S.ins(eng, TTO(ob_.ap, a.ap, c.ap, ALU.add), [a, c], [ob_])
                            S.dma("act", V0[tok:tok + G * 128, :].rearrange("(m p) c -> p m c", p=128), ob_.ap, ob_.ctr, reads=[ob_])
                        else:
                            S.ins(eng, TTO(o.ap, a.ap, c.ap, ALU.add), [a, c], [o])
                            S.dma("act", HYC[tok:tok + G * 128, c0:c0 + 512].rearrange("(m p) c -> p m c", p=128), o.ap, o.ctr, reads=[o])
            S.phase_end()

        with contextlib.ExitStack() as es:
            S.phase_begin()
            skb = alloc(es, "skb", [64, 1024], F32)
            F1s = alloc(es, "F1s3", [64, N2 * 128], BF16)
            F2s = alloc(es, "F2s3", [128, N2 * 64], BF16)
            sm = alloc(es, "sm3", [N2, 12 * N2], BF16)
            Gs = alloc(es, "Gs3", [2 * N2, 4 * N2], BF16)
            S.dma("sp", skb.ap, bcast_rows(skipv, 64), skb.ctr, writes=[skb])
            S.dma("sp", F1s.ap, F1_d, F1s.ctr, writes=[F1s])
            S.dma("sp", F2s.ap, F2i_d, F2s.ctr, writes=[F2s])
            S.dma("sp", sm.ap, smalls_d, sm.ctr, writes=[sm])
            S.dma("sp", Gs.ap, G_d, Gs.ctr, writes=[Gs])
            F1v = F1s.ap.rearrange("p (t c) -> p t c", c=128)
            F2v = F2s.ap.rearrange("p (t c) -> p t c", c=64)
            smv = sm.ap.rearrange("p (s c) -> p s c", s=6)
            Gv = Gs.ap.rearrange("p (s c) -> p s c", s=2)
            S.barrier()

            def rows3(dr, sb, b, c0, c1):
                return dr[sb + b * LB: sb + (b + 1) * LB, c0:c1].rearrange("(a t) c -> a t c", t=N2)

            for (sb, nb) in seqs:
                for order in range(2):
                    src = V0 if order == 0 else Z1
                    with contextlib.ExitStack() as e1:
                        S.phase_begin()
                        uch = [alloc(e1, f"uch{i}", [64, TCH, 512], BF16) for i in range(3)]
                        stg = [alloc(e1, f"stg{i}", [128, TCH, 512], BF16) for i in range(2)]
                        q = 0
                        for b in range(nb):
                            for t2a in range(0, N2, TCH):
                                u = uch[q % 3]
                                st_ = stg[q % 2]
                                q += 1
                                S.dma("sp", u.ap, rows3(src, sb, b, 0, 512)[:, t2a:t2a + TCH, :], u.ctr, writes=[u])
                                for j in range(TCH):
                                    pb = nextbank()
                                    S.ins("pe", MM(pb.ap, F1v[:, t2a + j, :], u.ap[:, j, :], True, True), [F1s, u], [pb])
                                    S.ins("act", ACP(st_.ap[:, j, :], pb.ap), [pb], [st_])
                                S.dma("pool", B1[b, :, t2a:t2a + TCH, :], st_.ap, st_.ctr, reads=[st_])
                        S.phase_end()
                    with contextlib.ExitStack() as e2:
                        S.phase_begin()
                        Ach = [alloc(e2, f"A{i}", [N2, KCH, 512], BF16) for i in range(8)]
                        Hch = [alloc(e2, f"H{i}", [2 * N2, KCH, 512], BF16) for i in range(12)]
                        T12 = [alloc(e2, f"T12_{i}", [2 * N2, 512], BF16, False) for i in range(16)]
                        tmpf = [alloc(e2, f"tmpf{i}", [2 * N2, 512], F32, False) for i in range(8)]
                        stgB = [alloc(e2, f"stgB{i}", [2 * N2, KCH, 512], BF16) for i in range(4)]
                        cq = {"a": 0, "h": 0, "t": 0, "f": 0, "s": 0}
                        pids = [0] if nb == 1 else [1, 2, 3]
                        if nb == 1:
                            terms = [[(0, 0)]]
                        else:
                            terms = [[(0, 1), (1, 2)], [(0, 3), (1, 1)]]
                        Uset = [(banks[0], banks[1]), (banks[2], banks[3])]
                        Cbanks = [banks[4], banks[5], banks[6]]
                        chst = {}
                        ust = [dict() for _ in range(64)]

                        def s2A(k):
                            ci, j = k // KCH, k % KCH
                            k1a = ci * KCH
                            if j == 0:
                                Ab = {}
                                for b in range(nb):
                                    for c in range(2):
                                        bf = Ach[cq["a"] % 8]
                                        cq["a"] += 1
                                        S.dma("sp", bf.ap, B1[b, c * 64 + k1a: c * 64 + k1a + KCH, :, :].rearrange("k t c -> t k c"),
                                              bf.ctr, writes=[bf])
                                        Ab[b, c] = bf
                                Hb = {}
                                for pidx in pids:
                                    for ab in range(2):
                                        bf = Hch[cq["h"] % 12]
                                        cq["h"] += 1
                                        S.dma("pool", bf.ap, HS2[pidx, order, ab, :, k1a:k1a + KCH, :], bf.ctr, writes=[bf])
                                        Hb[pidx, ab] = bf
                                sB = []
                                for ob in range(nb):
                                    sB.append(stgB[cq["s"] % 4])
                                    cq["s"] += 1
                                chst[ci] = (Ab, Hb, sB)
                            Ab, Hb, sB = chst[ci]
                            pU = []
                            for b in range(nb):
                                pu = Uset[k % 2][b]
                                S.ins("pe", MM(pu.ap[0:2 * N2, :], smv[:, 0, :], Ab[b, 0].ap[:, j, :], True, False), [sm, Ab[b, 0]], [pu])
                                S.ins("pe", MM(pu.ap[0:2 * N2, :], smv[:, 1, :], Ab[b, 1].ap[:, j, :], False, True), [sm, Ab[b, 1]], [pu])
                                pU.append(pu)
                            ust[k]["pU"] = pU

                        def s2B(k):
                            ci, j = k // KCH, k % KCH
                            Ab, Hb, sB = chst[ci]
                            pU = ust[k]["pU"]
                            allT = []
                            for ob in range(nb):
                                Ts = []
                                for ab in range(2):
                                    T = T12[cq["t"] % 16]
                                    cq["t"] += 1
                                    tl = terms[ob]
                                    if len(tl) == 1:
                                        (ub, pidx) = tl[0]
                                        S.ins("dve", TTO(T.ap, pU[ub].ap[0:2 * N2, :], Hb[pidx, ab].ap[:, j, :], ALU.mult),
                                              [pU[ub], Hb[pidx, ab]], [T])
                                    else:
                                        f0 = tmpf[cq["f"] % 8]
                                        f1 = tmpf[(cq["f"] + 1) % 8]
                                        cq["f"] += 2
                                        (u0, p0_), (u1, p1_) = tl
                                        S.ins("dve", TTO(f0.ap, pU[u0].ap[0:2 * N2, :], Hb[p0_, ab].ap[:, j, :], ALU.mult),
                                              [pU[u0], Hb[p0_, ab]], [f0])
                                        S.ins("dve", TTO(f1.ap, pU[u1].ap[0:2 * N2, :], Hb[p1_, ab].ap[:, j, :], ALU.mult),
                                              [pU[u1], Hb[p1_, ab]], [f1])
                                        S.ins("dve", TTO(T.ap, f0.ap, f1.ap, ALU.add), [f0, f1], [T])
                                    Ts.append(T)
                                allT.append(Ts)
                            ust[k]["T"] = allT

                        def s2C(k):
                            ci, j = k // KCH, k % KCH
                            k1a = ci * KCH
                            Ab, Hb, sB = chst[ci]
                            for ob in range(nb):
                                Ts = ust[k]["T"][ob]
                                pbk = Cbanks[cq["c"] % 3]
                                cq["c"] += 1
                                S.ins("pe", MM(pbk.ap[0:2 * N2, :], Gv[:, 0, :], Ts[0].ap, True, False), [Gs, Ts[0]], [pbk])
                                S.ins("pe", MM(pbk.ap[0:2 * N2, :], Gv[:, 1, :], Ts[1].ap, False, True), [Gs, Ts[1]], [pbk])
                                S.ins("act", ACP(sB[ob].ap[:, j, :], pbk.ap[0:2 * N2, :]), [pbk], [sB[ob]])
                            if j == KCH - 1:
                                for ob in range(nb):
                                    S.dma("pool", B2[ob, :, k1a:k1a + KCH, :], sB[ob].ap, sB[ob].ctr, reads=[sB[ob]])

                        cq["c"] = 0
                        for step in range(64 + 2):
                            if 0 <= step - 2 < 64:
                                s2C(step - 2)
                            if 0 <= step - 1 < 64:
                                s2B(step - 1)
                            if step < 64:
                                s2A(step)

                        S.phase_end()
                    with contextlib.ExitStack() as e3:
                        S.phase_begin()
                        Bc = [alloc(e3, f"Bc{i}", [128, TCH, 512], BF16) for i in range(2)]
                        Bh = [[Buf(f"bch{i}_{k}") for k in range(2)] for i in range(2)]
                        vch = [alloc(e3, f"vch{i}", [64, TCH, 512], BF16) for i in range(2)]
                        gch = [alloc(e3, f"gch{i}", [64, TCH, 512], F32) for i in range(2)]
                        rlt = alloc(e3, "rlt", [64, 512], F32)
                        srt = alloc(e3, "srt", [64, 512], F32, False)
                        S.dma("sp", rlt.ap, RL1[0 if nb == 1 else 1, order], rlt.ctr, writes=[rlt])
                        S.ins("dve", RECIP(srt.ap, rlt.ap), [rlt], [srt])
                        S.ins("dve", TTO(srt.ap, srt.ap, skb.ap[:, order * 512:(order + 1) * 512], ALU.mult), [srt, skb], [srt])
                        vsk = [alloc(e3, f"vsk{i}", [64, TCH, 512], F32, False) for i in range(2)]
                        zof = [alloc(e3, f"zof{i}", [64, TCH, 512], F32) for i in range(2)]
                        zob = [alloc(e3, f"zob{i}", [64, TCH, 512], BF16) for i in range(2)] if order == 0 else None
                        q = 0
                        zq = 0
                        gc0 = 512 * (1 + order)
                        for b in range(nb):
                            for t2a in range(0, N2, TCH):
                                bc = Bc[q % 2]
                                hb0, hb1 = Bh[q % 2]
                                vv_ = vch[q % 2]
                                gg = gch[q % 2]
                                zo = (zob if order == 0 else zof)[q % 2]
                                ztmp = zof[q % 2]
                                vs_ = vsk[q % 2]
                                q += 1
                                S.dma("sp", bc.ap[0:64, :, :], B2[b, t2a:t2a + TCH, :, :].rearrange("t k c -> k t c"), hb0.ctr,
                                      reads=(), writes=[hb0])
                                S.dma("sp", bc.ap[64:128, :, :], B2[b, N2 + t2a:N2 + t2a + TCH, :, :].rearrange("t k c -> k t c"), hb1.ctr,
                                      reads=(), writes=[hb1])
                                S.dma("sp", vv_.ap, rows3(src, sb, b, 0, 512)[:, t2a:t2a + TCH, :], vv_.ctr, writes=[vv_])
                                S.dma("pool", gg.ap, rows3(HYC, sb, b, gc0, gc0 + 512)[:, t2a:t2a + TCH, :], gg.ctr, writes=[gg])
                                srbc = srt.ap.unsqueeze(1).to_broadcast([64, TCH, 512])
                                rlbc = rlt.ap.unsqueeze(1).to_broadcast([64, TCH, 512])
                                S.ins("pool", TTO(vs_.ap, vv_.ap, srbc, ALU.mult), [vv_, srt], [vs_])
                                S.ins("dve", TTO(gg.ap, gg.ap, rlbc, ALU.mult), [gg, rlt], [gg])
                                for j in range(TCH):
                                    py = nextbank()
                                    S.ins("pe", MM(py.ap[0:64, :], F2v[:, t2a + j, :], bc.ap[:, j, :], True, True), [F2s, hb0, hb1], [py])
                                    S.ins("dve", TTO(ztmp.ap[:, j, :], py.ap[0:64, :], vs_.ap[:, j, :], ALU.add), [py, vs_], [ztmp])
                                S.ins("dve", TTO(zo.ap, ztmp.ap, gg.ap, ALU.mult), [ztmp, gg], [zo])
                                dstz = Z1 if order == 0 else Z2
                                S.dma("sp", rows3(dstz, sb, b, 0, 512)[:, t2a:t2a + TCH, :], zo.ap, zo.ctr, reads=[zo])
                        S.phase_end()
            S.phase_end()

    if "ffn" in stages:
        with contextlib.ExitStack() as es:
            S.phase_begin()
            fb = ffn_bufs(es)
            ring = [alloc(es, f"ringb{i}", [128, 4096], BF16) for i in range(6)]
            xTs = [(alloc(es, f"xT4_{i}", [128, 8, TT], F32), [Buf(f"xT4{i}_{k}") for k in range(8)]) for i in range(2)]
            hnT = alloc(es, "hnT4", [128, 8, TT], BF16, False)
            hnTb = [Buf(f"hnT4_{k}") for k in range(8)]
            mixA = alloc(es, "mixA", [128, 4, 512], BF16)
            z2t = alloc(es, "z2t", [128, 4, 512], F32)
            z2n = alloc(es, "z2n", [128, 4, 512], BF16, False)
            z2nb = [Buf(f"z2n{k}") for k in range(4)]
            ghy = alloc(es, "ghy", [128, 512], F32)
            ss4 = alloc(es, "ss4", [128, 4], F32, False)
            junk4 = alloc(es, "junk4", [128, 512], F32, False)
            mixT = alloc(es, "mixT", [128, 8, TT], BF16, False)
            mixTb = [Buf(f"mixT{k}") for k in range(8)]
            outT = fb["tmp"]
            outTb = fb["tmpb"]
            ytok = [alloc(es, f"ytok{i}", [128, D], F32) for i in range(2)]
            S.dma("sp", ghy.ap, bcast_rows(hnorm), ghy.ctr, writes=[ghy])
            if "attn" not in stages or "hyena" not in stages:
                S.ins("pool", MSET(z2t.ap, 0.0), (), [z2t])
                S.ins("pool", MSET(mixA.ap, 0.0), (), [mixA])
                for ti in range(NTILES):
                    tok0 = ti * TT
                    if "attn" not in stages:
                        S.dma("sp", MIXA[tok0:tok0 + TT, :].rearrange("(m p) c -> p m c", p=128), mixA.ap, mixA.ctr, reads=[mixA])
                    if "hyena" not in stages:
                        S.dma("sp", Z2[tok0:tok0 + TT, :].rearrange("(m p) c -> p m c", p=128), z2t.ap, z2t.ctr, reads=[z2t])
                S.barrier()
            for ti in range(NTILES):
                tok0 = ti * TT
                s = tile_seq(tok0)
                xT, xTb = xTs[ti % 2]
                S.dma("sp", xT.ap, X1[:, :, tok0:tok0 + TT].rearrange("k p t -> p k t"), xT.ctr, writes=xTb)
                S.dma("sp", mixA.ap, MIXA[tok0:tok0 + TT, :].rearrange("(m p) c -> p m c", p=128), mixA.ctr, writes=[mixA])
                S.dma("sp", z2t.ap, Z2[tok0:tok0 + TT, :].rearrange("(m p) c -> p m c", p=128), z2t.ctr, writes=[z2t])
                for st in range(4):
                    S.ins("dve", STT(junk4.ap, z2t.ap[:, st, :], 1.0, z2t.ap[:, st, :], ALU.mult, ALU.mult, accum_out=ss4.ap[:, st:st + 1]),
                          [z2t], [junk4, ss4])
                S.ins("act", ACT(ss4.ap, ss4.ap, AF.Sqrt, bias=epst.ap[:, 0:1], scale=1.0 / 512), [ss4, epst], [ss4])
                S.ins("dve", RECIP(ss4.ap, ss4.ap), [ss4], [ss4])
                for st in range(4):
                    S.ins("dve", STT(z2n.ap[:, st, :], z2t.ap[:, st, :], ss4.ap[:, st:st + 1], ghy.ap, ALU.mult, ALU.mult),
                          [z2t, ss4, ghy], [z2nb[st]])
                for fc in range(8):
                    pb = nextbank()
                    pbv = pb.ap.bitcast(BF16)
                    for st in range(4):
                        if fc < 4:
                            S.ins("pe", TR(pbv[:, st * 128:(st + 1) * 128], mixA.ap[:, st, fc * 128:(fc + 1) * 128], identb.ap),
                                  [mixA, identb], [pb])
                        else:
                            S.ins("pe", TR(pbv[:, st * 128:(st + 1) * 128], z2n.ap[:, st, (fc - 4) * 128:(fc - 3) * 128], identb.ap),
                                  [z2nb[st], identb], [pb])
                    evac(mixT.ap[:, fc, :], pbv[:, 0:TT], [pb], [mixTb[fc]])
                for oc in range(8):
                    sw = ring_load(ring, Wl["wout"][oc], 1024)
                    swv = sw.ap[:, 0:1024].rearrange("p (k c) -> p k c", c=128)
                    py = nextbank()
                    for kc in range(8):
                        S.ins("pe", MM(py.ap, swv[:, kc, :], mixT.ap[:, kc, :], kc == 0, kc == 7), [sw, mixTb[kc]], [py])
                    S.ins("dve", STT(xT.ap[:, oc, :], py.ap, par.ap[:, s, 5, oc:oc + 1], xT.ap[:, oc, :], ALU.mult, ALU.add),
                          [py, par, xTb[oc]], [xTb[oc]])
                norm_to(fb, xT, xTb, s, 6, 7, hnT, hnTb)
                ffn(fb, ring, hnT, hnTb, xT, xTb, s, 8, "f2g", "f2u", "f2d")
                norm_to(fb, xT, xTb, s, None, None, outT, outTb)
                for st in range(4):
                    yt = ytok[st % 2]
                    for hf in range(2):
                        pb = nextbank()
                        for k4 in range(4):
                            kc = hf * 4 + k4
                            S.ins("pe", TR(pb.ap[:, k4 * 128:(k4 + 1) * 128], outT.ap[:, kc, st * 128:(st + 1) * 128], ident.ap),
                                  [outTb[kc], ident], [pb])
                        evac(yt.ap[:, hf * 512:(hf + 1) * 512], pb.ap, [pb], [yt])
                    S.dma("sp", yout[tok0 + st * 128: tok0 + (st + 1) * 128, :], yt.ap, yt.ctr, reads=[yt])
            S.phase_end()

    S.emit(nc)
    es_all.close()
    return nc


def prep_shared(inp, LB):
    f = lambda a: np.ascontiguousarray(np.asarray(a, dtype=np.float32))
    sh = {}
    sh["w_ada"] = f(inp["w_ada"][0])
    sh["b_adaT"] = f(inp["b_ada"][0].reshape(72, 128).T)
    nrm = np.stack([inp["ffn1_norm"][0], inp["mix_norm"][0], inp["ffn2_norm"][0], inp["final_norm"]], 0)
    sh["nrm"] = f(nrm.reshape(4, 8, 128).transpose(2, 0, 1))
    sh["f1g"] = f(inp["ffn1_w_gate"][0]); sh["f1u"] = f(inp["ffn1_w_up"][0]); sh["f1d"] = f(inp["ffn1_w_down"][0])
    sh["f2g"] = f(inp["ffn2_w_gate"][0]); sh["f2u"] = f(inp["ffn2_w_up"][0]); sh["f2d"] = f(inp["ffn2_w_down"][0])
    sh["win"] = f(inp["w_in"][0]); sh["wout"] = f(inp["w_out"][0])
    rp = np.zeros((8, 15, 128), np.float32)
    rp[:, :, 48:79] = inp["na_rpb"][0]
    sh["rpbp"] = rp
    sh["convw"] = f(inp["hy_conv_w"][0].reshape(1, 3 * 1536))
    sh["convb"] = f(inp["hy_conv_b"][0].reshape(1, 1536))
    sh["hw1"] = f(inp["hy_w1"][0]); sh["hw2"] = f(inp["hy_w2"][0]); sh["hw3"] = f(inp["hy_w3"][0])
    sh["hwo"] = f(inp["hy_wo"][0])
    sh["hb"] = f(np.stack([inp["hy_b1"][0], inp["hy_b2"][0], inp["hy_b3"][0], inp["hy_sin_freq"][0]], 1))
    sh["skipv"] = f(inp["hy_skip"][0].reshape(1, 1024))
    sh["anorm"] = f(inp["attn_out_norm"][0].reshape(1, 512))
    sh["hnorm"] = f(inp["hy_out_norm"][0].reshape(1, 512))
    sh.update(host_consts(LB))
    return sh


DEBUG_OUT = {}


def run_cores(inp, LB, ncores, stages=("ffn", "attn", "hyena"), debug=False):
    xp = np.asarray(inp["x_prompt"], np.float32)
    xs = np.asarray(inp["x_sample"], np.float32)
    cp = np.asarray(inp["c_prompt"], np.float32)
    cs = np.asarray(inp["c_sample"], np.float32)
    sh = prep_shared(inp, LB)
    in_maps = []
    for i in range(ncores):
        m = dict(sh)
        m["xin"] = np.ascontiguousarray(np.concatenate([xp[2 * i], xp[2 * i + 1], xs[i]], 0))
        c3 = np.stack([cp[2 * i], cp[2 * i + 1], cs[i]], 0)
        m["cT"] = np.ascontiguousarray(c3.reshape(3, 8, 128).transpose(2, 1, 0))
        in_maps.append(m)
    nc = build_program(LB, stages, debug)
    res = run_bass_kernel_spmd(nc, in_maps, core_ids=list(range(ncores)))
    if debug:
        DEBUG_OUT.update(res.results[0])
    yp = np.zeros_like(xp)
    ys = np.zeros_like(xs)
    for i in range(ncores):
        y = res.results[i]["yout"]
        yp[2 * i] = y[0:LB]
        yp[2 * i + 1] = y[LB:2 * LB]
        ys[i] = y[2 * LB:4 * LB]
    return yp, ys


def kernel(**inputs):
    yp, ys = run_cores(inputs, 4096, 8)
    return (yp, ys)
```

```python
import contextlib
import math
import numpy as np
import ml_dtypes
import concourse.bass as bass
import concourse.mybir as mybir
from concourse.bass_utils import run_bass_kernel_spmd

F32 = mybir.dt.float32
BF16 = mybir.dt.bfloat16
ALU = mybir.AluOpType
AF = mybir.ActivationFunctionType

ENGS = ("pe", "act", "dve", "pool", "sp")
D = 1024
DFF = 2816
NJF = DFF // 128
EPS = 1e-6
MAG = 12582912.0
TWO_PI = 2.0 * math.pi


class Counter:
    def __init__(self, name, step, epoch):
        self.name, self.step, self.epoch = name, step, epoch
        self.n = 0
        self.sems = []

    def nxt(self):
        self.n += 1
        return self.n

    def nsems(self):
        return max(1, (self.n + self.epoch - 1) // self.epoch)

    def loc(self, n):
        return (n - 1) // self.epoch, ((n - 1) % self.epoch + 1) * self.step


_CUR = [None]


class Buf:
    __slots__ = ("name", "lw", "rd", "_ctr", "ap")

    def __init__(self, name, ap=None):
        self.name = name
        self.lw = None
        self.rd = {}
        self._ctr = None
        self.ap = ap

    @property
    def ctr(self):
        if self._ctr is None:
            self._ctr = _CUR[0].dma_counter(self.name)
        return self._ctr

    @ctr.setter
    def ctr(self, v):
        self._ctr = v


class Sync:
    def __init__(self):
        self.ops = {e: [] for e in ENGS}
        self.cnt = {e: Counter("c_" + e, 1, 30000) for e in ENGS}
        self.known = {e: {} for e in ENGS}
        self.counters = list(self.cnt.values())
        self.nd = 0
        self.free = []
        self.phase_ctrs = []
        _CUR[0] = self

    def dma_counter(self, name):
        if self.free:
            c = self.free.pop()
        else:
            self.nd += 1
            c = Counter(f"d{self.nd}", 16, 1800)
            self.counters.append(c)
        self.phase_ctrs.append(c)
        return c

    def phase_begin(self):
        self.phase_ctrs = []

    def phase_end(self):
        self.barrier()
        self.free.extend(self.phase_ctrs)
        self.phase_ctrs = []

    def _wait(self, eng, ctr, n):
        k = self.known[eng]
        if k.get(ctr, 0) >= n:
            return
        k[ctr] = n
        self.ops[eng].append(("w", ctr, n))

    def _deps(self, eng, reads, writes):
        deps = {}
        for b in reads:
            if b.lw is not None:
                c, n = b.lw
                if deps.get(c, 0) < n:
                    deps[c] = n
        for b in writes:
            if b.lw is not None:
                c, n = b.lw
                if deps.get(c, 0) < n:
                    deps[c] = n
            for c, n in b.rd.items():
                if deps.get(c, 0) < n:
                    deps[c] = n
        mine = self.cnt[eng]
        for c, n in deps.items():
            if c is mine and eng == "pe":
                continue
            self._wait(eng, c, n)

    def _commit(self, ctr, n, reads, writes):
        for b in reads:
            if b.rd.get(ctr, 0) < n:
                b.rd[ctr] = n
        for b in writes:
            b.lw = (ctr, n)
            b.rd = {}

    def ins(self, eng, fn, reads=(), writes=()):
        self._deps(eng, reads, writes)
        ctr = self.cnt[eng]
        n = ctr.nxt()
        self.ops[eng].append(("i", fn, ctr, n))
        self._commit(ctr, n, reads, writes)

    def dma(self, eng, out_ap, in_ap, ctr, reads=(), writes=()):
        self._deps(eng, reads, writes)
        n = ctr.nxt()
        self.ops[eng].append(("i", lambda e: e.dma_start(out=out_ap, in_=in_ap), ctr, n))
        self._commit(ctr, n, reads, writes)

    def barrier(self):
        for e in ENGS:
            for c in self.counters:
                if c.n > 0:
                    self._wait(e, c, c.n)

    def emit(self, nc):
        for c in self.counters:
            if c.n > 0:
                self._wait("sp", c, c.n)
        with contextlib.ExitStack() as st:
            for c in self.counters:
                c.sems = [st.enter_context(nc.semaphore(f"{c.name}_{i}")) for i in range(c.nsems())]
            block = st.enter_context(nc.Block())

            def run(eng_name):
                def f(e):
                    for op in self.ops[eng_name]:
                        if op[0] == "w":
                            _, c, n = op
                            ei, v = c.loc(n)
                            e.wait_ge(c.sems[ei], v)
                        else:
                            _, fn, c, n = op
                            ei, _v = c.loc(n)
                            fn(e).then_inc(c.sems[ei], c.step)
                return f

            block.tensor(run("pe"))
            block.scalar(run("act"))
            block.vector(run("dve"))
            block.gpsimd(run("pool"))
            block.sync(run("sp"))


def MM(out, lhsT, rhs, start, stop):
    return lambda e: e.matmul(out, lhsT, rhs, start=start, stop=stop)


def TR(out, in_, ident):
    return lambda e: e.transpose(out, in_, ident)


def ACT(out, in_, func, bias=0.0, scale=1.0):
    return lambda e: e.activation(out, in_, func, bias=bias, scale=scale)


def TTO(out, in0, in1, op):
    return lambda e: e.tensor_tensor(out, in0, in1, op=op)


def TS(out, in0, s1, s2, op0, op1=None):
    if op1 is None:
        return lambda e: e.tensor_scalar(out, in0, s1, None, op0=op0)
    return lambda e: e.tensor_scalar(out, in0, s1, s2, op0=op0, op1=op1)


def STT(out, in0, scalar, in1, op0, op1, accum_out=None):
    if accum_out is None:
        return lambda e: e.scalar_tensor_tensor(out, in0, scalar, in1, op0=op0, op1=op1)
    return lambda e: e.scalar_tensor_tensor(out, in0, scalar, in1, op0=op0, op1=op1, accum_out=accum_out)


def CP(out, in_):
    return lambda e: e.tensor_copy(out, in_)


def ACP(out, in_):
    return lambda e: e.copy(out, in_)


def RECIP(out, in_):
    return lambda e: e.reciprocal(out, in_)


def MSET(ap, v):
    return lambda e: e.memset(ap, v)


def bcast_rows(ap, p=128):
    pairs = [list(x) for x in ap.ap]
    return bass.AP(ap.tensor, ap.offset, [[0, p]] + pairs[1:])


def signal_defs(LB):
    m = np.arange(LB)
    jm1 = np.maximum(m - 1, 0)
    sig = [
        (0, m, LB, False),
        (1, jm1, LB, True),
        (0, m, 2 * LB, False),
        (1, jm1, 2 * LB, True),
        (1, LB - 1 - m, 2 * LB, False),
        (1, LB - 1 + m, 2 * LB, True),
        (0, LB + m, 2 * LB, False),
        (0, np.where(m == 0, LB, LB - m), 2 * LB, True),
    ]
    return sig


def host_consts(LB):
    TC = LB // 128
    KC = LB // 128
    N = 2 * LB
    c = {}
    c["ident"] = np.eye(128, dtype=np.float32)
    c["jrev"] = np.ascontiguousarray(np.eye(64, dtype=np.float32)[::-1])
    qc = np.arange(64)[None, :]
    kc = np.arange(64)[:, None]
    st = np.clip(qc - 8, 0, 48)
    cm = ((kc >= st) & (kc < st + 16)).astype(np.float32)
    c["cmask"] = np.concatenate([cm, cm], 0)
    deltas = np.abs(np.linspace(math.log(1e-2) / 0.3, math.log(1e-2) / 1.5, 512, dtype=np.float32)).astype(np.float32)
    c["deltas"] = deltas[None, :]
    sigs = signal_defs(LB)
    zf = np.zeros((8, 17, LB), np.float32)
    negt = np.zeros((128, 8, TC), np.float32)
    f = np.linspace(1e-4, 7.0, 8, dtype=np.float32)[None, :].astype(np.float64)
    for i, (dr, pos, L, mk) in enumerate(sigs):
        t = pos.astype(np.float64) / (L - 1)
        w = 2.0 * math.pi * pos.astype(np.float64)[:, None] / L
        z = np.concatenate([t[:, None], np.cos(f * w), -np.sin(f * w)], -1)
        zf[i] = z.T.astype(np.float32)
        negt[:, i, :] = (-t).astype(np.float32).reshape(TC, 128).T
    c["zfeat"] = zf
    c["negt"] = negt
    N2 = LB // 64
    bf = ml_dtypes.bfloat16
    t1 = np.arange(64, dtype=np.float64)[:, None, None]
    t2 = np.arange(N2, dtype=np.float64)[None, :, None]
    k1 = np.arange(64, dtype=np.float64)[None, None, :]
    th = np.mod((2.0 * math.pi / N) * ((k1 + 0.5) * (N2 * t1 + t2)), 2.0 * math.pi)
    F1 = np.concatenate([np.cos(th), -np.sin(th)], -1)
    c["F1"] = np.ascontiguousarray(F1.reshape(64, N2 * 128)).astype(bf)
    psi = th.transpose(2, 1, 0)
    F2i = (2.0 / N) * np.concatenate([np.cos(psi), -np.sin(psi)], 0)
    c["F2i"] = np.ascontiguousarray(F2i.reshape(128, N2 * 64)).astype(bf)
    a2 = np.arange(N2, dtype=np.float64)
    ph = np.mod((2.0 * math.pi / N2) * np.outer(a2, a2), 2.0 * math.pi)
    C2, S2 = np.cos(ph), np.sin(ph)
    sm = np.stack([np.concatenate([C2, -S2], 1), np.concatenate([S2, C2], 1), np.concatenate([C2, C2], 1),
                   np.concatenate([S2, S2], 1), np.concatenate([-S2, -S2], 1), np.concatenate([-C2, -C2], 1)], 1)
    c["smalls"] = np.ascontiguousarray(sm.reshape(N2, 12 * N2)).astype(bf)
    CC = np.concatenate([C2, C2], 1)
    SS = np.concatenate([S2, S2], 1)
    sm2 = np.stack([np.concatenate([CC, CC], 0), np.concatenate([SS, SS], 0),
                    np.concatenate([-SS, SS], 0), np.concatenate([CC, -CC], 0)], 1)
    c["smalls2"] = np.ascontiguousarray(sm2.reshape(2 * N2, 8 * N2)).astype(bf)
    G1 = np.block([[C2, S2], [-S2, C2]])
    G2 = np.block([[-S2, C2], [-C2, -S2]])
    c["Gm"] = np.ascontiguousarray(np.stack([G1, G2], 1).reshape(2 * N2, 4 * N2)).astype(bf)
    negt2 = np.zeros((64, 8, N2), np.float32)
    for i, (dr, pos, L, mk) in enumerate(sigs):
        tt = pos.astype(np.float64) / (L - 1)
        negt2[:, i, :] = (-tt).astype(np.float32).reshape(64, N2)
    c["negt2"] = negt2
    del c["negt"]
    return c


CONST_SHAPES = None


def build_program(LB, stages=("ffn", "attn", "hyena"), debug=False):
    nc = bass.Bass("TRN2", target_bir_lowering=False)
    S = Sync()
    TC = LB // 128
    KC = LB // 128
    NT = 4 * LB
    TT = 512
    NTILES = NT // TT
    seqs = [(0, 1), (LB, 1), (2 * LB, 2)]
    NSEQ = 3

    def tile_seq(tok):
        return 0 if tok < LB else (1 if tok < 2 * LB else 2)

    def din(name, shape, dt=F32):
        return nc.dram_tensor(name, list(shape), dt, kind="ExternalInput").ap()

    def dscr(name, shape, dt):
        if debug and name in ("s_qk", "s_v", "s_mixa", "s_hyp", "s_hyc", "s_v0", "s_z1", "s_z2", "s_hs2"):
            return nc.dram_tensor(name, list(shape), dt, kind="ExternalOutput").ap()
        return nc.dram_tensor(name, list(shape), dt).ap()

    xin = din("xin", [NT, D])
    cT = din("cT", [128, 8, NSEQ])
    w_ada = din("w_ada", [D, 9 * D])
    b_adaT = din("b_adaT", [128, 72])
    nrm = din("nrm", [128, 4, 8])
    wsrc = {
        "f1g": din("f1g", [D, DFF]), "f1u": din("f1u", [D, DFF]), "f1d": din("f1d", [DFF, D]),
        "f2g": din("f2g", [D, DFF]), "f2u": din("f2u", [D, DFF]), "f2d": din("f2d", [DFF, D]),
        "win": din("win", [D, 3072]), "wout": din("wout", [D, D]),
    }
    rpbp = din("rpbp", [8, 15, 128])
    convw = din("convw", [1, 3 * 1536])
    convb = din("convb", [1, 1536])
    hw1 = din("hw1", [17, 64])
    hw2 = din("hw2", [64, 64])
    hw3 = din("hw3", [64, 64])
    hwo = din("hwo", [64, 2048])
    hb = din("hb", [64, 4])
    skipv = din("skipv", [1, 1024])
    anorm = din("anorm", [1, 512])
    hnorm = din("hnorm", [1, 512])
    ident_d = din("ident", [128, 128])
    jrev_d = din("jrev", [64, 64])
    cmask_d = din("cmask", [128, 64])
    deltas_d = din("deltas", [1, 512])
    zfeat_d = din("zfeat", [8, 17, LB])
    NN2 = LB // 64
    negt2_d = din("negt2", [64, 8, NN2])
    F1_d = din("F1", [64, NN2 * 128], BF16)
    F2i_d = din("F2i", [128, NN2 * 64], BF16)
    smalls_d = din("smalls", [NN2, 12 * NN2], BF16)
    smalls2_d = din("smalls2", [2 * NN2, 8 * NN2], BF16)
    G_d = din("Gm", [2 * NN2, 4 * NN2], BF16)
    yout = nc.dram_tensor("yout", [NT, D], F32, kind="ExternalOutput").ap()

    Wl = {
        "f1g": dscr("s_f1g", [NJF, 128, 8 * 128], BF16), "f1u": dscr("s_f1u", [NJF, 128, 8 * 128], BF16),
        "f1d": dscr("s_f1d", [8, 128, NJF * 128], BF16),
        "f2g": dscr("s_f2g", [NJF, 128, 8 * 128], BF16), "f2u": dscr("s_f2u", [NJF, 128, 8 * 128], BF16),
        "f2d": dscr("s_f2d", [8, 128, NJF * 128], BF16),
        "winl": dscr("s_winl", [8, 128, 8 * 128], BF16), "winr": dscr("s_winr", [4, 128, 8 * 512], BF16),
        "wout": dscr("s_wout", [8, 128, 8 * 128], BF16),
    }
    X1 = dscr("s_x1", [8, 128, NT], F32)
    QK = dscr("s_qk", [8, 128, NT], BF16)
    Vd = dscr("s_v", [NT, 640], BF16)
    HYP = dscr("s_hyp", [NT, 1536], F32)
    HYC = dscr("s_hyc", [NT, 1536], F32)
    V0 = dscr("s_v0", [NT, 512], BF16)
    Z1 = dscr("s_z1", [NT, 512], BF16)
    Z2 = dscr("s_z2", [NT, 512], F32)
    MIXA = dscr("s_mixa", [NT, 512], BF16)
    HS2 = dscr("s_hs2", [4, 2, 2, 2 * NN2, 64, 512], BF16)
    RL1 = dscr("s_rl1", [2, 2, 64, 512], F32)
    B1 = dscr("s_b1", [2, 128, NN2, 512], BF16)
    B2 = dscr("s_b2", [2, 2 * NN2, 64, 512], BF16)

    es_all = contextlib.ExitStack()

    uid = [0]

    def alloc(es, name, shape, dt, dmac=True):
        uid[0] += 1
        name = f"sb{uid[0]}_{name}"
        t = es.enter_context(nc.sbuf_tensor(name, list(shape), dt))
        b = Buf(name, t.ap())
        return b

    P = es_all
    ident = alloc(P, "ident", [128, 128], F32)
    identb = alloc(P, "identb", [128, 128], BF16, False)
    ones = alloc(P, "ones", [128, 128], BF16, False)
    epst = alloc(P, "epst", [128, 1], F32, False)
    par = alloc(P, "par", [128, NSEQ, 9, 8], F32, False)
    gfin = alloc(P, "gfin", [128, 8], F32)
    banks = []
    for i in range(8):
        t = nc.alloc_psum_tensor(f"bank{i}", [128, 512], F32)
        banks.append(Buf(f"bank{i}", t.ap()))
    bank_i = [0]

    def nextbank():
        b = banks[bank_i[0] % 7]
        bank_i[0] += 1
        return b

    ev_i = [0]

    def evac(out_ap, in_ap, reads, writes):
        ev_i[0] += 1
        if ev_i[0] % 2:
            S.ins("act", ACP(out_ap, in_ap), reads, writes)
        else:
            S.ins("dve", CP(out_ap, in_ap), reads, writes)

    S.dma("sp", ident.ap, ident_d, ident.ctr, writes=[ident])
    S.ins("dve", CP(identb.ap, ident.ap), [ident], [identb])
    S.ins("pool", MSET(ones.ap, 1.0), (), [ones])
    S.ins("pool", MSET(epst.ap, EPS), (), [epst])
    S.dma("sp", gfin.ap, nrm[:, 3, :], gfin.ctr, writes=[gfin])

    with contextlib.ExitStack() as es:
        S.phase_begin()
        c_sb = alloc(es, "c_sb", [128, 8, NSEQ], F32)
        sc_sb = alloc(es, "sc_sb", [128, 8, NSEQ], F32, False)
        bada = alloc(es, "bada", [128, 72], F32)
        nrm_sb = alloc(es, "nrm_sb", [128, 4, 8], F32)
        modsb = alloc(es, "modsb", [128, 72, NSEQ], F32, False)
        wslab = [alloc(es, f"wslab{i}", [128, 8, 512], F32) for i in range(2)]
        S.dma("sp", c_sb.ap, cT, c_sb.ctr, writes=[c_sb])
        S.dma("sp", bada.ap, b_adaT, bada.ctr, writes=[bada])
        S.dma("sp", nrm_sb.ap, nrm, nrm_sb.ctr, writes=[nrm_sb])
        S.ins("act", ACT(sc_sb.ap, c_sb.ap, AF.Silu), [c_sb], [sc_sb])
        pm = banks[7]
        for sl in range(18):
            ws = wslab[sl % 2]
            S.dma("sp", ws.ap, w_ada[:, sl * 512:(sl + 1) * 512].rearrange("(k p) c -> p k c", p=128), ws.ctr, writes=[ws])
            for jj in range(4):
                j = sl * 4 + jj
                for kc in range(8):
                    S.ins("pe", MM(pm.ap[:, j * NSEQ:(j + 1) * NSEQ], ws.ap[:, kc, jj * 128:(jj + 1) * 128],
                                   sc_sb.ap[:, kc, :], kc == 0, kc == 7), [ws, sc_sb], [pm])
        pmv = pm.ap[:, 0:72 * NSEQ].rearrange("p (j s) -> p j s", s=NSEQ)
        for s in range(NSEQ):
            S.ins("dve", TTO(modsb.ap[:, :, s], pmv[:, :, s], bada.ap, ALU.add), [pm, bada], [modsb])
        for s in range(NSEQ):
            for sub in range(3):
                sh = modsb.ap[:, (sub * 3 + 0) * 8:(sub * 3 + 0) * 8 + 8, s]
                scl = modsb.ap[:, (sub * 3 + 1) * 8:(sub * 3 + 1) * 8 + 8, s]
                g = modsb.ap[:, (sub * 3 + 2) * 8:(sub * 3 + 2) * 8 + 8, s]
                S.ins("dve", STT(par.ap[:, s, sub * 3 + 0, :], scl, 1.0, nrm_sb.ap[:, sub, :], ALU.add, ALU.mult),
                      [modsb, nrm_sb], [par])
                S.ins("dve", CP(par.ap[:, s, sub * 3 + 1, :], sh), [modsb], [par])
                S.ins("dve", TS(par.ap[:, s, sub * 3 + 2, :], g, 1.0 if sub == 1 else 0.5, None, ALU.mult), [modsb], [par])
        S.phase_end()

    with contextlib.ExitStack() as es:
        S.phase_begin()
        f32s = [alloc(es, f"wc32_{i}", [128, 2816], F32) for i in range(2)]
        b16s = [alloc(es, f"wc16_{i}", [128, 2816], BF16) for i in range(2)]
        it = [0]

        def cast_w(src, K, col0, ncols, dst, TW):
            KCn = K // 128
            CW = 256 if KCn == 8 else 128
            for cs in range(col0, col0 + ncols, CW):
                a = f32s[it[0] % 2]
                b = b16s[it[0] % 2]
                it[0] += 1
                av = a.ap[:, 0:KCn * CW].rearrange("p (k c) -> p k c", c=CW)
                bv = b.ap[:, 0:KCn * CW].rearrange("p (k c) -> p k c", c=CW)
                S.dma("sp", av, src[:, cs:cs + CW].rearrange("(k p) c -> p k c", p=128), a.ctr, writes=[a])
                if it[0] % 2:
                    S.ins("act", ACP(bv, av), [a], [b])
                else:
                    S.ins("dve", CP(bv, av), [a], [b])
                rel = cs - col0
                if TW == 128:
                    for g in range(CW // 128):
                        j = (rel + g * 128) // 128
                        dv = dst[j].rearrange("p (k c) -> p k c", c=128)
                        S.dma("pool", dv, bv[:, :, g * 128:(g + 1) * 128], b.ctr, reads=[b])
                else:
                    g = rel // 512
                    off = rel % 512
                    dv = dst[g].rearrange("p (k c) -> p k c", c=512)
                    S.dma("pool", dv[:, :, off:off + CW], bv, b.ctr, reads=[b])

        if "ffn" in stages:
            for nm in ("f1g", "f1u", "f2g", "f2u"):
                cast_w(wsrc[nm], D, 0, DFF, Wl[nm], 128)
            for nm in ("f1d", "f2d"):
                cast_w(wsrc[nm], DFF, 0, D, Wl[nm], 128)
            cast_w(wsrc["win"], D, 0, 1024, Wl["winl"], 128)
            cast_w(wsrc["win"], D, 1024, 2048, Wl["winr"], 512)
            cast_w(wsrc["wout"], D, 0, D, Wl["wout"], 128)
        S.phase_end()

    N2 = LB // 64
    TCH = min(8, N2)
    KCH = 4
    if "hyena" in stages:
        with contextlib.ExitStack() as es:
            S.phase_begin()
            w1s = alloc(es, "w1s", [17, 64], F32)
            w2s = alloc(es, "w2s", [64, 64], F32)
            w3s = alloc(es, "w3s", [64, 64], F32)
            hbs = alloc(es, "hbs", [64, 4], F32)
            wob = alloc(es, "wob", [64, 2048], BF16, False)
            with contextlib.ExitStack() as esw:
                wo32 = alloc(esw, "wo32", [64, 2048], F32)
                S.dma("sp", wo32.ap, hwo, wo32.ctr, writes=[wo32])
                S.ins("dve", CP(wob.ap, wo32.ap), [wo32], [wob])
                S.barrier()
            dlt = alloc(es, "dlt", [64, 512], F32)
            ngt = alloc(es, "ngt", [64, 8, N2], F32)
            F1s = alloc(es, "F1s", [64, N2 * 128], BF16)
            sm = alloc(es, "sm", [N2, 12 * N2], BF16)
            S.dma("sp", w1s.ap, hw1, w1s.ctr, writes=[w1s])
            S.dma("sp", w2s.ap, hw2, w2s.ctr, writes=[w2s])
            S.dma("sp", w3s.ap, hw3, w3s.ctr, writes=[w3s])
            S.dma("sp", hbs.ap, hb, hbs.ctr, writes=[hbs])
            S.dma("sp", dlt.ap, bcast_rows(deltas_d, 64), dlt.ctr, writes=[dlt])
            S.dma("sp", ngt.ap, negt2_d, ngt.ctr, writes=[ngt])
            S.dma("sp", F1s.ap, F1_d, F1s.ctr, writes=[F1s])
            S.dma("sp", sm.ap, smalls_d, sm.ctr, writes=[sm])
            sm2 = alloc(es, "sm2", [2 * N2, 8 * N2], BF16)
            S.dma("sp", sm2.ap, smalls2_d, sm2.ctr, writes=[sm2])
            sm2v = sm2.ap.rearrange("p (s c) -> p s c", s=4)
            F1v = F1s.ap.rearrange("p (t c) -> p t c", c=128)
            smv = sm.ap.rearrange("p (s c) -> p s c", s=6)
            hbf = alloc(es, "hbf", [64, 3], F32, False)
            for li_ in range(3):
                S.ins("dve", TTO(hbf.ap[:, li_:li_ + 1], hbs.ap[:, li_:li_ + 1], hbs.ap[:, 3:4], ALU.mult), [hbs], [hbf])
            h3T = [alloc(es, f"h3T{i}", [64, LB], BF16, False) for i in range(6)]
            NCH = LB // 512
            GRP = min(4, NCH)
            zt = [alloc(es, f"zt{i}", [17, 512], F32) for i in range(GRP)]
            ta = [alloc(es, f"hta{i}", [64, 512], F32, False) for i in range(GRP)]
            tb = [alloc(es, f"htb{i}", [64, 512], F32, False) for i in range(GRP)]
            lay = [alloc(es, f"lay{i}", [64, GRP * 512], F32, False) for i in range(2)]
            layb = [[Buf(f"lay{i}_{g}") for g in range(GRP)] for i in range(2)]
            dec = [alloc(es, f"dec{i}", [64, 512], F32, False) for i in range(3)]
            hb16 = [alloc(es, f"hb16_{i}", [64, 512], BF16, False) for i in range(8)]
            rl1 = alloc(es, "rl1", [128, 512], F32, False)
            habs = [alloc(es, f"habs{i}", [64, 512], BF16, False) for i in range(4)]
            cnt_ha = 0
            stg = [alloc(es, f"fstg{i}", [128, TCH, 512], BF16) for i in range(2)]
            Ach = [alloc(es, f"fA{i}", [2 * N2, KCH, 512], BF16) for i in range(4)]
            Achb = [[Buf(f"fAh{i}_{k}") for k in range(2)] for i in range(4)]
            hst = [alloc(es, f"fhst{i}", [2 * N2, KCH, 512], BF16) for i in range(4)]
            sigs = signal_defs(LB)
            cnt = {"z": 0, "d": 0, "h": 0, "s": 0, "a": 0, "t": 0, "ha": 0}

            def hidden(sig, slot):
                for g0 in range(0, NCH, GRP):
                    chunks = list(range(g0, min(NCH, g0 + GRP)))
                    for gi, c in enumerate(chunks):
                        S.dma("sp", zt[gi].ap, zfeat_d[sig, :, c * 512:(c + 1) * 512], zt[gi].ctr, writes=[zt[gi]])
                    for li, wsb in enumerate((w1s, w2s, w3s)):
                        pbs = []
                        for gi, c in enumerate(chunks):
                            pb = nextbank()
                            if li == 0:
                                cur, curb = zt[gi].ap, zt[gi]
                            else:
                                cur, curb = lay[li - 1].ap[:, gi * 512:(gi + 1) * 512], layb[li - 1][gi]
                            S.ins("pe", MM(pb.ap[0:64, :], wsb.ap, cur, True, True), [wsb, curb], [pb])
                            pbs.append(pb)
                        for gi, c in enumerate(chunks):
                            S.ins("act", ACT(ta[gi].ap, pbs[gi].ap[0:64, :], AF.Identity, bias=hbf.ap[:, li:li + 1], scale=hbs.ap[:, 3:4]),
                                  [pbs[gi], hbs, hbf], [ta[gi]])
                        for gi, c in enumerate(chunks):
                            S.ins("dve", TS(tb[gi].ap, ta[gi].ap, 1.0 / TWO_PI, MAG, ALU.mult, ALU.add), [ta[gi]], [tb[gi]])
                        for gi, c in enumerate(chunks):
                            S.ins("dve", TS(tb[gi].ap, tb[gi].ap, MAG, None, ALU.subtract), [tb[gi]], [tb[gi]])
                        for gi, c in enumerate(chunks):
                            S.ins("dve", STT(ta[gi].ap, tb[gi].ap, -TWO_PI, ta[gi].ap, ALU.mult, ALU.add), [ta[gi], tb[gi]], [ta[gi]])
                        for gi, c in enumerate(chunks):
                            S.ins("dve", TS(ta[gi].ap, ta[gi].ap, math.pi, -math.pi, ALU.min, ALU.max), [ta[gi]], [ta[gi]])
                        for gi, c in enumerate(chunks):
                            if li < 2:
                                S.ins("act", ACT(lay[li].ap[:, gi * 512:(gi + 1) * 512], ta[gi].ap, AF.Sin), [ta[gi]], [layb[li][gi]])
                            else:
                                S.ins("act", ACT(h3T[slot].ap[:, c * 512:(c + 1) * 512], ta[gi].ap, AF.Sin), [ta[gi]], [h3T[slot]])

            def genA(sig, slot, order, t2, mask_after):
                dr = sigs[sig][0]
                off = (dr * 2 + order) * 512
                pb = nextbank()
                hv = h3T[slot].ap.rearrange("p (a b) -> p a b", b=N2)[:, :, t2]
                S.ins("pe", MM(pb.ap[0:64, :], hv, wob.ap[:, off:off + 512], True, True), [h3T[slot], wob], [pb])
                d = dec[cnt["d"] % 3]
                cnt["d"] += 1
                S.ins("act", ACT(d.ap, dlt.ap, AF.Exp, scale=ngt.ap[:, sig, t2:t2 + 1]), [dlt, ngt], [d])
                h = hb16[cnt["s"] % 8]
                cnt["s"] += 1
                S.ins("dve", TTO(h.ap, pb.ap[0:64, :], d.ap, ALU.mult), [pb, d], [h])
                return h

            def pipeline(n, stages, lag):
                st = [dict() for _ in range(n)]
                ns = len(stages)
                for step in range(n + (ns - 1) * lag):
                    for si in reversed(range(ns)):
                        k = step - si * lag
                        if 0 <= k < n:
                            stages[si](k, st[k])

            LAG = 2
            for fs in range(2):
                if fs == 0:
                    sl = {0: 0, 1: 1}
                    l1sigs = [(0, False), (1, True)]
                    pieces = [(0, 0, 1)]
                else:
                    sl = {2: 0, 3: 1, 4: 2, 5: 3, 6: 4, 7: 5}
                    l1sigs = [(2, False), (6, False), (3, True), (5, False)]
                    pieces = [(1, 2, 3), (2, 4, 5), (3, 6, 7)]
                for sg, slot in sl.items():
                    hidden(sg, slot)
                for order in range(2):
                    l1b = banks[7]
                    l1set = {sg: mk for (sg, mk) in l1sigs}
                    l1order = [sg for (pidx_, sa_, sb_) in pieces for sg in (sa_, sb_) if sg in l1set]
                    l1first, l1last = l1order[0], l1order[-1]
                    for (pidx, sa, sbb) in pieces:
                        units = [(ab, sg, msk, t2) for ab, (sg, msk) in enumerate(((sa, sigs[sa][3]), (sbb, True))) for t2 in range(N2)]

                        def pA(k, stt, units=units, order=order):
                            ab, sg, msk, t2 = units[k]
                            h = genA(sg, sl[sg], order, t2, False)
                            late_mask = (sg in l1set) and (not l1set[sg]) and msk
                            if msk and t2 == 0 and not late_mask:
                                S.ins("dve", MSET(h.ap[0:1, :], 0.0), (), [h])
                            stt["h"] = h
                            stt["late"] = late_mask and t2 == 0

                        def pB(k, stt, units=units):
                            ab, sg, msk, t2 = units[k]
                            xb = stt["h"]
                            if sg in l1set:
                                ha = habs[cnt["ha"] % 4]
                                cnt["ha"] += 1
                                S.ins("dve", STT(ha.ap, xb.ap, -1.0, xb.ap, ALU.mult, ALU.max), [xb], [ha])
                                stt["ha"] = ha
                            if stt["late"]:
                                S.ins("dve", MSET(xb.ap[0:1, :], 0.0), (), [xb])
                            stt["xb"] = xb

                        def pC(k, stt, units=units):
                            ab, sg, msk, t2 = units[k]
                            j = t2 % TCH
                            if j == 0:
                                stt["st"] = stg[cnt["t"] % 2]
                                cnt["t"] += 1
                                cnt["cur_st"] = stt["st"]
                            st_ = cnt["cur_st"]
                            pb = nextbank()
                            S.ins("pe", MM(pb.ap, F1v[:, t2, :], stt["xb"].ap, True, True), [F1s, stt["xb"]], [pb])
                            S.ins("act", ACP(st_.ap[:, j, :], pb.ap), [pb], [st_])
                            if j == TCH - 1:
                                t2a = t2 - j
                                S.dma("pool", B1[ab, :, t2a:t2a + TCH, :], st_.ap, st_.ctr, reads=[st_])
                            if sg in l1set:
                                S.ins("pe", MM(l1b.ap, ones.ap[0:64, :], stt["ha"].ap, sg == l1first and t2 == 0, sg == l1last and t2 == N2 - 1),
                                      [ones, stt["ha"]], [l1b])

                        pipeline(len(units), [pA, pB, pC], LAG)
                        S.barrier()
                        for k1a in range(0, 64, KCH):
                            bufs = {}
                            for c in range(2):
                                ai = cnt["a"] % 4
                                cnt["a"] += 1
                                bf = Ach[ai]
                                for ab in range(2):
                                    hbuf = Achb[ai][ab]
                                    S.dma("sp", bf.ap[ab * N2:(ab + 1) * N2, :, :],
                                          B1[ab, c * 64 + k1a: c * 64 + k1a + KCH, :, :].rearrange("k t c -> t k c"),
                                          hbuf.ctr, writes=[hbuf])
                                bufs[c] = (bf, Achb[ai])
                            hA = hst[(k1a // KCH) % 2 * 2]
                            hB = hst[(k1a // KCH) % 2 * 2 + 1]
                            for j in range(KCH):
                                pA_ = nextbank()
                                for qi, (si, c) in enumerate(((0, 0), (1, 1))):
                                    bf, hbs_ = bufs[c]
                                    S.ins("pe", MM(pA_.ap[0:2 * N2, :], sm2v[:, si, :], bf.ap[:, j, :], qi == 0, qi == 1), [sm2] + hbs_, [pA_])
                                S.ins("act", ACP(hA.ap[:, j, :], pA_.ap[0:2 * N2, :]), [pA_], [hA])
                                pB_ = nextbank()
                                for qi, (si, c) in enumerate(((2, 0), (3, 1))):
                                    bf, hbs_ = bufs[c]
                                    S.ins("pe", MM(pB_.ap[0:2 * N2, :], sm2v[:, si, :], bf.ap[:, j, :], qi == 0, qi == 1), [sm2] + hbs_, [pB_])
                                S.ins("act", ACP(hB.ap[:, j, :], pB_.ap[0:2 * N2, :]), [pB_], [hB])
                            S.dma("pool", HS2[pidx, order, 0, :, k1a:k1a + KCH, :], hA.ap, hA.ctr, reads=[hA])
                            S.dma("pool", HS2[pidx, order, 1, :, k1a:k1a + KCH, :], hB.ap, hB.ctr, reads=[hB])
                        S.barrier()
                    S.ins("dve", RECIP(rl1.ap, l1b.ap), [l1b], [rl1])
                    S.dma("sp", RL1[fs, order], rl1.ap[0:64, :], rl1.ctr, reads=[rl1])
                    S.barrier()
            S.phase_end()

    def norm_to(es_b, xT, xTb, s, aidx, bidx, outT, outTb):
        sq, rstd, tmp, tmpb = es_b["sq"], es_b["rstd"], es_b["tmp"], es_b["tmpb"]
        for kc in range(8):
            S.ins("dve", TTO(sq.ap[:, kc, :], xT.ap[:, kc, :], xT.ap[:, kc, :], ALU.mult), [xTb[kc]], [es_b["sqb"][kc]])
        pb = nextbank()
        for kc in range(8):
            S.ins("pe", MM(pb.ap, ones.ap, sq.ap[:, kc, :], kc == 0, kc == 7), [ones, es_b["sqb"][kc]], [pb])
        S.ins("act", ACT(rstd.ap, pb.ap, AF.Sqrt, bias=epst.ap[:, 0:1], scale=1.0 / D), [pb, epst], [rstd])
        S.ins("dve", RECIP(rstd.ap, rstd.ap), [rstd], [rstd])
        for kc in range(8):
            if aidx is None:
                sc_ap = gfin.ap[:, kc:kc + 1]
                S.ins("dve", STT(outT.ap[:, kc, :], xT.ap[:, kc, :], sc_ap, rstd.ap, ALU.mult, ALU.mult),
                      [xTb[kc], rstd, gfin], [outTb[kc]])
            else:
                S.ins("dve", STT(tmp.ap[:, kc, :], xT.ap[:, kc, :], par.ap[:, s, aidx, kc:kc + 1], rstd.ap, ALU.mult, ALU.mult),
                      [xTb[kc], rstd, par], [tmpb[kc]])
                S.ins("act", ACT(outT.ap[:, kc, :], tmp.ap[:, kc, :], AF.Identity, bias=par.ap[:, s, bidx, kc:kc + 1]),
                      [tmpb[kc], par], [outTb[kc]])

    ring_i = [0]

    def ring_load(ring, src2d, width):
        sl = ring[ring_i[0] % len(ring)]
        ring_i[0] += 1
        S.dma("sp", sl.ap[:, 0:width], src2d, sl.ctr, writes=[sl])
        return sl

    def ffn(es_b, ring, hnT, hnTb, xT, xTb, s, gidx, wg, wu, wd):
        aT, aTb, sgt = es_b["aT"], es_b["aTb"], es_b["sgt"]
        for j in range(NJF):
            sg = ring_load(ring, Wl[wg][j], 1024)
            su = ring_load(ring, Wl[wu][j], 1024)
            sgv = sg.ap[:, 0:1024].rearrange("p (k c) -> p k c", c=128)
            suv = su.ap[:, 0:1024].rearrange("p (k c) -> p k c", c=128)
            pg = nextbank()
            pu = nextbank()
            for kc in range(8):
                S.ins("pe", MM(pg.ap, sgv[:, kc, :], hnT.ap[:, kc, :], kc == 0, kc == 7), [sg, hnTb[kc]], [pg])
            for kc in range(8):
                S.ins("pe", MM(pu.ap, suv[:, kc, :], hnT.ap[:, kc, :], kc == 0, kc == 7), [su, hnTb[kc]], [pu])
            sl = sgt[j % 2]
            S.ins("act", ACT(sl.ap, pg.ap, AF.Silu), [pg], [sl])
            S.ins("dve", TTO(aT.ap[:, j, :], sl.ap, pu.ap, ALU.mult), [sl, pu], [aTb[j]])
        for oc in range(8):
            sd = ring_load(ring, Wl[wd][oc], NJF * 128)
            sdv = sd.ap[:, 0:NJF * 128].rearrange("p (k c) -> p k c", c=128)
            py = nextbank()
            for j in range(NJF):
                S.ins("pe", MM(py.ap, sdv[:, j, :], aT.ap[:, j, :], j == 0, j == NJF - 1), [sd, aTb[j]], [py])
            S.ins("dve", STT(xT.ap[:, oc, :], py.ap, par.ap[:, s, gidx, oc:oc + 1], xT.ap[:, oc, :], ALU.mult, ALU.add),
                  [py, par, xTb[oc]], [xTb[oc]])

    def ffn_bufs(es):
        b = {}
        b["sq"] = alloc(es, "sq", [128, 8, TT], BF16, False)
        b["sqb"] = [Buf(f"sq{k}") for k in range(8)]
        b["rstd"] = alloc(es, "rstd", [128, TT], F32, False)
        b["tmp"] = alloc(es, "tmp", [128, 8, TT], F32, False)
        b["tmpb"] = [Buf(f"tmp{k}") for k in range(8)]
        b["aT"] = alloc(es, "aT", [128, NJF, TT], BF16, False)
        b["aTb"] = [Buf(f"aT{k}") for k in range(NJF)]
        b["sgt"] = [alloc(es, f"sgt{i}", [128, TT], F32, False) for i in range(2)]
        return b

    if "ffn" in stages:
        with contextlib.ExitStack() as es:
            S.phase_begin()
            fb = ffn_bufs(es)
            ring = [alloc(es, f"ring{i}", [128, 4096], BF16) for i in range(6)]
            xtok = [alloc(es, f"xtok{i}", [128, D], F32) for i in range(4)]
            xTs = [(alloc(es, f"xT_{i}", [128, 8, TT], F32), [Buf(f"xT{i}_{k}") for k in range(8)]) for i in range(2)]
            hnT = alloc(es, "hnT", [128, 8, TT], BF16, False)
            hnTb = [Buf(f"hnT{k}") for k in range(8)]
            qk = alloc(es, "qk", [128, 8, TT], BF16)
            qkb = [Buf(f"qk{k}") for k in range(8)]
            vst = [alloc(es, f"vst{i}", [128, 8, 80], BF16) for i in range(2)]
            hstg = [alloc(es, f"hstg{i}", [128, 512], F32) for i in range(3)]
            for v in vst:
                S.ins("pool", MSET(v.ap, 1.0), (), [v])
            hq = 0
            for ti in range(NTILES):
                tok0 = ti * TT
                s = tile_seq(tok0)
                xT, xTb = xTs[ti % 2]
                for st in range(4):
                    S.dma("sp", xtok[st].ap, xin[tok0 + st * 128: tok0 + (st + 1) * 128, :], xtok[st].ctr, writes=[xtok[st]])
                for kc in range(8):
                    pb = nextbank()
                    for st in range(4):
                        S.ins("pe", TR(pb.ap[:, st * 128:(st + 1) * 128], xtok[st].ap[:, kc * 128:(kc + 1) * 128], ident.ap),
                              [xtok[st], ident], [pb])
                    evac(xT.ap[:, kc, :], pb.ap, [pb], [xTb[kc]])
                norm_to(fb, xT, xTb, s, 0, 1, hnT, hnTb)
                ffn(fb, ring, hnT, hnTb, xT, xTb, s, 2, "f1g", "f1u", "f1d")
                S.dma("pool", X1[:, :, tok0:tok0 + TT].rearrange("k p t -> p k t"), xT.ap, xT.ctr, reads=xTb)
                norm_to(fb, xT, xTb, s, 3, 4, hnT, hnTb)
                for j in range(8):
                    sw = ring_load(ring, Wl["winl"][j], 1024)
                    swv = sw.ap[:, 0:1024].rearrange("p (k c) -> p k c", c=128)
                    pb = nextbank()
                    for kc in range(8):
                        S.ins("pe", MM(pb.ap, swv[:, kc, :], hnT.ap[:, kc, :], kc == 0, kc == 7), [sw, hnTb[kc]], [pb])
                    evac(qk.ap[:, j, :], pb.ap, [pb], [qkb[j]])
                S.dma("pool", QK[:, :, tok0:tok0 + TT].rearrange("k p t -> p k t"), qk.ap, qk.ctr, reads=qkb)
                for g in range(4):
                    sw = ring_load(ring, Wl["winr"][g], 4096)
                    swv = sw.ap.rearrange("p (k c) -> p k c", c=512)
                    for st in range(4):
                        pb = nextbank()
                        for kc in range(8):
                            S.ins("pe", MM(pb.ap, hnT.ap[:, kc, st * 128:(st + 1) * 128], swv[:, kc, :], kc == 0, kc == 7),
                                  [sw, hnTb[kc]], [pb])
                        tk = tok0 + st * 128
                        if g == 0:
                            v = vst[st % 2]
                            evac(v.ap[:, :, 0:64], pb.ap.rearrange("p (h d) -> p h d", d=64), [pb], [v])
                            S.dma("pool", Vd[tk:tk + 128, :], v.ap.rearrange("p h d -> p (h d)"), v.ctr, reads=[v])
                        else:
                            h = hstg[hq % 3]
                            hq += 1
                            evac(h.ap, pb.ap, [pb], [h])
                            S.dma("pool", HYP[tk:tk + 128, (g - 1) * 512:g * 512], h.ap, h.ctr, reads=[h])
            S.phase_end()

    if "attn" in stages:
        with contextlib.ExitStack() as es:
            S.phase_begin()
            Tpp = alloc(es, "Tpp", [64, 8, 15, 64], F32)
            jrev = alloc(es, "jrev", [64, 64], F32)
            cmask = alloc(es, "cmask", [128, 64], F32)
            T2 = alloc(es, "T2", [128, 8 * 14, 64], F32, False)
            gat = alloc(es, "gat", [128, 512], F32)
            S.dma("sp", jrev.ap, jrev_d, jrev.ctr, writes=[jrev])
            S.dma("sp", cmask.ap, cmask_d, cmask.ctr, writes=[cmask])
            S.dma("sp", gat.ap, bcast_rows(anorm), gat.ctr, writes=[gat])
            for h in range(8):
                src = bass.AP(rpbp.tensor, rpbp.offset + h * 15 * 128, [[1, 64], [128, 15], [1, 64]])
                S.dma("sp", Tpp.ap[:, h, :, :], src, Tpp.ctr, writes=[Tpp])
            import os
            ACUT = int(os.environ.get("ACUT", "9"))
            if ACUT >= -1:
                S.ins("act", ACT(Tpp.ap, Tpp.ap, AF.Exp), [Tpp], [Tpp])
            for bk in (range(14) if ACUT >= 0 else []):
                pb = nextbank()
                for sl in range(8):
                    fl = bk * 8 + sl
                    h, d = fl // 14, fl % 14
                    for half in range(2):
                        S.ins("pe", MM(pb.ap[64 * half:64 * half + 64, sl * 64:(sl + 1) * 64], Tpp.ap[:, h, d + half, :], jrev.ap, True, True),
                              [Tpp, jrev], [pb])
                if ACUT >= 1:
                    S.ins("dve", TTO(T2.ap[:, bk * 8:(bk + 1) * 8, :], pb.ap.rearrange("p (s c) -> p s c", c=64),
                                cmask.ap.unsqueeze(1).to_broadcast([128, 8, 64]), ALU.mult), [pb, cmask], [T2])
            T2v = T2.ap.rearrange("p (h d) c -> p h d c", d=14)
            if debug:
                dbgT2 = nc.dram_tensor("dbgT2", [128, 112 * 64], F32, kind="ExternalOutput").ap()
                dbgE = nc.dram_tensor("dbgE", [128, 512], F32, kind="ExternalOutput").ap()
                dbgP = nc.dram_tensor("dbgP", [128, 512], BF16, kind="ExternalOutput").ap()
                dbgO = nc.dram_tensor("dbgO", [128, 512], F32, kind="ExternalOutput").ap()
                dbgA = nc.dram_tensor("dbgA", [128, 512], F32, kind="ExternalOutput").ap()
                dctr = S.dma_counter("dbg")
                S.dma("sp", dbgT2, T2.ap.rearrange("p a c -> p (a c)"), dctr, reads=[T2])
            QB = min(32, LB // 64)
            KMAX = QB + 8
            NE = (KMAX + 2) // 2
            qT = alloc(es, "qT", [128, 4, QB * 64], BF16)
            kT = alloc(es, "kT", [128, 4, (KMAX + 1) * 64], BF16)
            Ve = alloc(es, "Ve", [128, NE, 640], BF16)
            Vo = alloc(es, "Vo", [128, NE, 640], BF16)
            Est = [alloc(es, f"Est{i}", [128, 512], F32, False) for i in range(4)]
            Pst = [alloc(es, f"Pst{i}", [128, 512], BF16, False) for i in range(4)]
            rec = alloc(es, "rec", [128, 8], F32, False)
            att = alloc(es, "att", [128, 512], F32, False)
            junk = alloc(es, "junk", [128, 512], F32, False)
            ssq = alloc(es, "ssq", [128, 1], F32, False)
            ans = [alloc(es, f"ans{i}", [128, 512], BF16) for i in range(2)]
            pq = 0
            for (sb, nb) in (seqs if ACUT >= 2 else []):
                rows = nb * LB // 64
                for qb0 in range(0, rows, QB):
                    r_lo, r_hi = qb0, qb0 + QB
                    klo = min(max(r_lo - 4, 0), rows - 8)
                    khi = min(max(r_hi - 1 - 4, 0), rows - 8) + 8
                    klo_e = klo - (klo % 2)
                    nk = khi - klo_e
                    ne = (nk + 1) // 2
                    no = (nk - 1) // 2
                    S.dma("sp", qT.ap, QK[0:4, :, sb + r_lo * 64: sb + r_hi * 64].rearrange("k p t -> p k t"), qT.ctr, writes=[qT])
                    S.dma("sp", kT.ap[:, :, 0:nk * 64], QK[4:8, :, sb + klo_e * 64: sb + (klo_e + nk) * 64].rearrange("k p t -> p k t"),
                          kT.ctr, writes=[kT])
                    t0 = sb + klo_e * 64
                    S.dma("sp", Ve.ap[:, 0:ne, :], Vd[t0:t0 + ne * 128, :].rearrange("(m p) c -> p m c", p=128), Ve.ctr, writes=[Ve])
                    if no > 0:
                        S.dma("sp", Vo.ap[:, 0:no, :], Vd[t0 + 64:t0 + 64 + no * 128, :].rearrange("(m p) c -> p m c", p=128),
                              Vo.ctr, writes=[Vo])
                    Sset = [(banks[0], banks[1]), (banks[2], banks[3])]
                    Oset = [(banks[4], banks[5]), (banks[6], banks[7])]
                    units = [(rr, hq) for rr in range(r_lo, r_hi) for hq in range(2)]

                    def rowinfo(rr):
                        r0 = min(max(rr - 4, 0), rows - 8)
                        d0 = r0 - rr + 7
                        pr = (r0 - klo_e) % 2
                        m0 = (r0 - klo_e - pr) // 2
                        return r0, d0, (Vo if pr else Ve), m0, (r0 - klo_e) * 64, (rr - r_lo) * 64

                    def attA(u):
                        rr, hq = units[u]
                        r0, d0, Vt, m0, kofs, qofs = rowinfo(rr)
                        sbs = Sset[u % 2]
                        for hpi in range(2):
                            hp = 2 * hq + hpi
                            for i in range(4):
                                for hh in range(2):
                                    p0 = 64 * hh
                                    S.ins("pe", MM(sbs[hh].ap[:, (hpi * 4 + i) * 64:(hpi * 4 + i + 1) * 64],
                                                   kT.ap[p0:p0 + 64, hp, kofs + 128 * i: kofs + 128 * (i + 1)],
                                                   qT.ap[p0:p0 + 64, hp, qofs:qofs + 64], True, True), [kT, qT], [sbs[hh]])

                    def attB(u):
                        nonlocal pq
                        rr, hq = units[u]
                        r0, d0, Vt, m0, kofs, qofs = rowinfo(rr)
                        sbs = Sset[u % 2]
                        ob = Oset[rr % 2]
                        for hh in range(2):
                            E = Est[pq % 4]
                            Pb = Pst[pq % 4]
                            pq += 1
                            h0 = 4 * hq + hh
                            S.ins("act", ACT(E.ap, sbs[hh].ap, AF.Exp, scale=0.125), [sbs[hh]], [E])
                            S.ins("dve", TTO(Pb.ap.rearrange("p (h i c) -> p h i c", h=2, i=4),
                                            E.ap.rearrange("p (h i c) -> p h i c", h=2, i=4),
                                            T2v[:, h0:h0 + 3:2, d0:d0 + 7:2, :], ALU.mult), [E, T2], [Pb])
                            for hpi in range(2):
                                h = 4 * hq + 2 * hpi + hh
                                for i in range(4):
                                    S.ins("pe", MM(ob[hq].ap[0:64, (h % 4) * 80:(h % 4) * 80 + 66],
                                                   Pb.ap[:, (hpi * 4 + i) * 64:(hpi * 4 + i + 1) * 64],
                                                   Vt.ap[:, m0 + i, h * 80:h * 80 + 66], i == 0, i == 3), [Pb, Vt], [ob[hq]])
                        if hq == 1:
                            r = rr
                            for bk in range(2):
                                obv = ob[bk].ap[0:64, 0:320].rearrange("p (h d) -> p h d", d=80)
                                S.ins("dve", RECIP(rec.ap[0:64, bk * 4:bk * 4 + 4], obv[:, :, 64]), [ob[bk]], [rec])
                                S.ins("dve", TTO(att.ap[0:64, bk * 256:(bk + 1) * 256].rearrange("p (h d) -> p h d", d=64), obv[:, :, 0:64],
                                                rec.ap[0:64, bk * 4:bk * 4 + 4].unsqueeze(2).to_broadcast([64, 4, 64]), ALU.mult),
                                      [ob[bk], rec], [att])
                            S.ins("dve", STT(junk.ap[0:64, :], att.ap[0:64, :], 1.0, att.ap[0:64, :], ALU.mult, ALU.mult, accum_out=ssq.ap[0:64, :]), [att], [junk, ssq])
                            S.ins("act", ACT(ssq.ap[0:64, :], ssq.ap[0:64, :], AF.Sqrt, bias=epst.ap[0:64, 0:1], scale=1.0 / 512), [ssq, epst], [ssq])
                            S.ins("dve", RECIP(ssq.ap[0:64, :], ssq.ap[0:64, :]), [ssq], [ssq])
                            an = ans[r % 2]
                            S.ins("dve", STT(an.ap[0:64, :], att.ap[0:64, :], ssq.ap[0:64, 0:1], gat.ap[0:64, :], ALU.mult, ALU.mult), [att, ssq, gat], [an])
                            S.dma("pool", MIXA[sb + r * 64: sb + r * 64 + 64, :], an.ap[0:64, :], an.ctr, reads=[an])

                    attA(0)
                    for u in range(len(units)):
                        if u + 1 < len(units):
                            attA(u + 1)
                        attB(u)

            S.phase_end()

    if "hyena" in stages:
        with contextlib.ExitStack() as es:
            S.phase_begin()
            cw = alloc(es, "cw", [128, 3, 1536], F32)
            cb = alloc(es, "cb", [128, 1536], F32)
            S.dma("sp", cw.ap, bcast_rows(convw).rearrange("p (i c) -> p i c", i=3), cw.ctr, writes=[cw])
            S.dma("sp", cb.ap, bcast_rows(convb), cb.ctr, writes=[cb])
            G = 4
            NB3 = 2
            At = [alloc(es, f"cA{i}", [128, G, 512], F32) for i in range(NB3)]
            Bt = [alloc(es, f"cB{i}", [128, G, 512], F32) for i in range(NB3)]
            Ct = [alloc(es, f"cC{i}", [128, G, 512], F32) for i in range(NB3)]
            Ot = [alloc(es, f"cO{i}", [128, G, 512], F32) for i in range(NB3)]
            Ob = [alloc(es, f"cOb{i}", [128, G, 512], BF16) for i in range(NB3)]
            dq = ["sp", "act"]
            q = 0
            for (sb, nb) in seqs:
                ntl = nb * TC
                for part in range(3):
                    c0 = part * 512
                    for tl in range(0, ntl, G):
                        tok = sb + tl * 128
                        a, b, c, o, ob_ = At[q % NB3], Bt[q % NB3], Ct[q % NB3], Ot[q % NB3], Ob[q % NB3]
                        eng = "pool" if q % 3 == 2 else "dve"
                        dqe = "sp"
                        q += 1

                        def rows(r0, n):
                            return HYP[r0:r0 + n * 128, c0:c0 + 512].rearrange("(m p) c -> p m c", p=128)
                        if tl == 0:
                            S.ins(eng, MSET(a.ap[0:1, 0, :], 0.0), (), [a])
                            S.dma(dqe, a.ap[1:128, 0, :], HYP[tok:tok + 127, c0:c0 + 512], a.ctr, writes=[a])
                            S.dma(dqe, a.ap[:, 1:G, :], rows(tok + 127, G - 1), a.ctr, writes=[a])
                        else:
                            S.dma(dqe, a.ap, rows(tok - 1, G), a.ctr, writes=[a])
                        S.dma(dqe, b.ap, rows(tok, G), b.ctr, writes=[b])
                        if tl + G >= ntl:
                            S.ins(eng, MSET(c.ap[:, G - 1, :], 0.0), (), [c])
                            S.dma(dqe, c.ap[:, 0:G - 1, :], rows(tok + 1, G - 1), c.ctr, writes=[c])
                            lt = tok + (G - 1) * 128
                            S.dma(dqe, c.ap[0:127, G - 1, :], HYP[lt + 1:lt + 128, c0:c0 + 512], c.ctr, writes=[c])
                        else:
                            S.dma(dqe, c.ap, rows(tok + 1, G), c.ctr, writes=[c])
                        w0 = cw.ap[:, 0, c0:c0 + 512].unsqueeze(1).to_broadcast([128, G, 512])
                        w1 = cw.ap[:, 1, c0:c0 + 512].unsqueeze(1).to_broadcast([128, G, 512])
                        w2 = cw.ap[:, 2, c0:c0 + 512].unsqueeze(1).to_broadcast([128, G, 512])
                        bb = cb.ap[:, c0:c0 + 512].unsqueeze(1).to_broadcast([128, G, 512])
                        S.ins(eng, TTO(a.ap, a.ap, w0, ALU.mult), [a, cw], [a])
                        S.ins(eng, TTO(b.ap, b.ap, w1, ALU.mult), [b, cw], [b])
                        S.ins(eng, TTO(c.ap, c.ap, w2, ALU.mult), [c, cw], [c])
                        S.ins(eng, TTO(a.ap, a.ap, b.ap, ALU.add), [a, b], [a])
                        S.ins(eng, TTO(c.ap, c.ap, bb, ALU.add), [c, cb], [c])
                        if part == 0:
                            S.ins(eng, TTO(ob_.ap, a.ap, c.ap, ALU.add), [a, c], [ob_])
                            S.dma("act", V0[tok:tok + G * 128, :].rearrange("(m p) c -> p m c", p=128), ob_.ap, ob_.ctr, reads=[ob_])
                        else:
                            S.ins(eng, TTO(o.ap, a.ap, c.ap, ALU.add), [a, c], [o])
                            S.dma("act", HYC[tok:tok + G * 128, c0:c0 + 512].rearrange("(m p) c -> p m c", p=128), o.ap, o.ctr, reads=[o])
            S.phase_end()

        with contextlib.ExitStack() as es:
            S.phase_begin()
            skb = alloc(es, "skb", [64, 1024], F32)
            F1s = alloc(es, "F1s3", [64, N2 * 128], BF16)
            F2s = alloc(es, "F2s3", [128, N2 * 64], BF16)
            sm = alloc(es, "sm3", [N2, 12 * N2], BF16)
            Gs = alloc(es, "Gs3", [2 * N2, 4 * N2], BF16)
            S.dma("sp", skb.ap, bcast_rows(skipv, 64), skb.ctr, writes=[skb])
            S.dma("sp", F1s.ap, F1_d, F1s.ctr, writes=[F1s])
            S.dma("sp", F2s.ap, F2i_d, F2s.ctr, writes=[F2s])
            S.dma("sp", sm.ap, smalls_d, sm.ctr, writes=[sm])
            S.dma("sp", Gs.ap, G_d, Gs.ctr, writes=[Gs])
            F1v = F1s.ap.rearrange("p (t c) -> p t c", c=128)
            F2v = F2s.ap.rearrange("p (t c) -> p t c", c=64)
            smv = sm.ap.rearrange("p (s c) -> p s c", s=6)
            Gv = Gs.ap.rearrange("p (s c) -> p s c", s=2)
            S.barrier()

            def rows3(dr, sb, b, c0, c1):
                return dr[sb + b * LB: sb + (b + 1) * LB, c0:c1].rearrange("(a t) c -> a t c", t=N2)

            for (sb, nb) in seqs:
                for order in range(2):
                    src = V0 if order == 0 else Z1
                    with contextlib.ExitStack() as e1:
                        S.phase_begin()
                        uch = [alloc(e1, f"uch{i}", [64, TCH, 512], BF16) for i in range(3)]
                        stg = [alloc(e1, f"stg{i}", [128, TCH, 512], BF16) for i in range(2)]
                        q = 0
                        for b in range(nb):
                            for t2a in range(0, N2, TCH):
                                u = uch[q % 3]
                                st_ = stg[q % 2]
                                q += 1
                                S.dma("sp", u.ap, rows3(src, sb, b, 0, 512)[:, t2a:t2a + TCH, :], u.ctr, writes=[u])
                                for j in range(TCH):
                                    pb = nextbank()
                                    S.ins("pe", MM(pb.ap, F1v[:, t2a + j, :], u.ap[:, j, :], True, True), [F1s, u], [pb])
                                    S.ins("act", ACP(st_.ap[:, j, :], pb.ap), [pb], [st_])
                                S.dma("pool", B1[b, :, t2a:t2a + TCH, :], st_.ap, st_.ctr, reads=[st_])
                        S.phase_end()
                    with contextlib.ExitStack() as e2:
                        S.phase_begin()
                        Ach = [alloc(e2, f"A{i}", [N2, KCH, 512], BF16) for i in range(8)]
                        Hch = [alloc(e2, f"H{i}", [2 * N2, KCH, 512], BF16) for i in range(12)]
                        T12 = [alloc(e2, f"T12_{i}", [2 * N2, 512], BF16, False) for i in range(16)]
                        tmpf = [alloc(e2, f"tmpf{i}", [2 * N2, 512], F32, False) for i in range(8)]
                        stgB = [alloc(e2, f"stgB{i}", [2 * N2, KCH, 512], BF16) for i in range(4)]
                        cq = {"a": 0, "h": 0, "t": 0, "f": 0, "s": 0}
                        pids = [0] if nb == 1 else [1, 2, 3]
                        if nb == 1:
                            terms = [[(0, 0)]]
                        else:
                            terms = [[(0, 1), (1, 2)], [(0, 3), (1, 1)]]
                        Uset = [(banks[0], banks[1]), (banks[2], banks[3])]
                        Cbanks = [banks[4], banks[5], banks[6]]
                        chst = {}
                        ust = [dict() for _ in range(64)]

                        def s2A(k):
                            ci, j = k // KCH, k % KCH
                            k1a = ci * KCH
                            if j == 0:
                                Ab = {}
                                for b in range(nb):
                                    for c in range(2):
                                        bf = Ach[cq["a"] % 8]
                                        cq["a"] += 1
                                        S.dma("sp", bf.ap, B1[b, c * 64 + k1a: c * 64 + k1a + KCH, :, :].rearrange("k t c -> t k c"),
                                              bf.ctr, writes=[bf])
                                        Ab[b, c] = bf
                                Hb = {}
                                for pidx in pids:
                                    for ab in range(2):
                                        bf = Hch[cq["h"] % 12]
                                        cq["h"] += 1
                                        S.dma("pool", bf.ap, HS2[pidx, order, ab, :, k1a:k1a + KCH, :], bf.ctr, writes=[bf])
                                        Hb[pidx, ab] = bf
                                sB = []
                                for ob in range(nb):
                                    sB.append(stgB[cq["s"] % 4])
                                    cq["s"] += 1
                                chst[ci] = (Ab, Hb, sB)
                            Ab, Hb, sB = chst[ci]
                            pU = []
                            for b in range(nb):
                                pu = Uset[k % 2][b]
                                S.ins("pe", MM(pu.ap[0:2 * N2, :], smv[:, 0, :], Ab[b, 0].ap[:, j, :], True, False), [sm, Ab[b, 0]], [pu])
                                S.ins("pe", MM(pu.ap[0:2 * N2, :], smv[:, 1, :], Ab[b, 1].ap[:, j, :], False, True), [sm, Ab[b, 1]], [pu])
                                pU.append(pu)
                            ust[k]["pU"] = pU

                        def s2B(k):
                            ci, j = k // KCH, k % KCH
                            Ab, Hb, sB = chst[ci]
                            pU = ust[k]["pU"]
                            allT = []
                            for ob in range(nb):
                                Ts = []
                                for ab in range(2):
                                    T = T12[cq["t"] % 16]
                                    cq["t"] += 1
                                    tl = terms[ob]
                                    if len(tl) == 1:
                                        (ub, pidx) = tl[0]
                                        S.ins("dve", TTO(T.ap, pU[ub].ap[0:2 * N2, :], Hb[pidx, ab].ap[:, j, :], ALU.mult),
                                              [pU[ub], Hb[pidx, ab]], [T])
                                    else:
                                        f0 = tmpf[cq["f"] % 8]
                                        f1 = tmpf[(cq["f"] + 1) % 8]
                                        cq["f"] += 2
                                        (u0, p0_), (u1, p1_) = tl
                                        S.ins("dve", TTO(f0.ap, pU[u0].ap[0:2 * N2, :], Hb[p0_, ab].ap[:, j, :], ALU.mult),
                                              [pU[u0], Hb[p0_, ab]], [f0])
                                        S.ins("dve", TTO(f1.ap, pU[u1].ap[0:2 * N2, :], Hb[p1_, ab].ap[:, j, :], ALU.mult),
                                              [pU[u1], Hb[p1_, ab]], [f1])
                                        S.ins("dve", TTO(T.ap, f0.ap, f1.ap, ALU.add), [f0, f1], [T])
                                    Ts.append(T)
                                allT.append(Ts)
                            ust[k]["T"] = allT

                        def s2C(k):
                            ci, j = k // KCH, k % KCH
                            k1a = ci * KCH
                            Ab, Hb, sB = chst[ci]
                            for ob in range(nb):
                                Ts = ust[k]["T"][ob]
                                pbk = Cbanks[cq["c"] % 3]
                                cq["c"] += 1
                                S.ins("pe", MM(pbk.ap[0:2 * N2, :], Gv[:, 0, :], Ts[0].ap, True, False), [Gs, Ts[0]], [pbk])
                                S.ins("pe", MM(pbk.ap[0:2 * N2, :], Gv[:, 1, :], Ts[1].ap, False, True), [Gs, Ts[1]], [pbk])
                                S.ins("act", ACP(sB[ob].ap[:, j, :], pbk.ap[0:2 * N2, :]), [pbk], [sB[ob]])
                            if j == KCH - 1:
                                for ob in range(nb):
                                    S.dma("pool", B2[ob, :, k1a:k1a + KCH, :], sB[ob].ap, sB[ob].ctr, reads=[sB[ob]])

                        cq["c"] = 0
                        for step in range(64 + 2):
                            if 0 <= step - 2 < 64:
                                s2C(step - 2)
                            if 0 <= step - 1 < 64:
                                s2B(step - 1)
                            if step < 64:
                                s2A(step)

                        S.phase_end()
                    with contextlib.ExitStack() as e3:
                        S.phase_begin()
                        Bc = [alloc(e3, f"Bc{i}", [128, TCH, 512], BF16) for i in range(2)]
                        Bh = [[Buf(f"bch{i}_{k}") for k in range(2)] for i in range(2)]
                        vch = [alloc(e3, f"vch{i}", [64, TCH, 512], BF16) for i in range(2)]
                        gch = [alloc(e3, f"gch{i}", [64, TCH, 512], F32) for i in range(2)]
                        rlt = alloc(e3, "rlt", [64, 512], F32)
                        srt = alloc(e3, "srt", [64, 512], F32, False)
                        S.dma("sp", rlt.ap, RL1[0 if nb == 1 else 1, order], rlt.ctr, writes=[rlt])
                        S.ins("dve", RECIP(srt.ap, rlt.ap), [rlt], [srt])
                        S.ins("dve", TTO(srt.ap, srt.ap, skb.ap[:, order * 512:(order + 1) * 512], ALU.mult), [srt, skb], [srt])
                        vsk = [alloc(e3, f"vsk{i}", [64, TCH, 512], F32, False) for i in range(2)]
                        zof = [alloc(e3, f"zof{i}", [64, TCH, 512], F32) for i in range(2)]
                        zob = [alloc(e3, f"zob{i}", [64, TCH, 512], BF16) for i in range(2)] if order == 0 else None
                        q = 0
                        zq = 0
                        gc0 = 512 * (1 + order)
                        for b in range(nb):
                            for t2a in range(0, N2, TCH):
                                bc = Bc[q % 2]
                                hb0, hb1 = Bh[q % 2]
                                vv_ = vch[q % 2]
                                gg = gch[q % 2]
                                zo = (zob if order == 0 else zof)[q % 2]
                                ztmp = zof[q % 2]
                                vs_ = vsk[q % 2]
                                q += 1
                                S.dma("sp", bc.ap[0:64, :, :], B2[b, t2a:t2a + TCH, :, :].rearrange("t k c -> k t c"), hb0.ctr,
                                      reads=(), writes=[hb0])
                                S.dma("sp", bc.ap[64:128, :, :], B2[b, N2 + t2a:N2 + t2a + TCH, :, :].rearrange("t k c -> k t c"), hb1.ctr,
                                      reads=(), writes=[hb1])
                                S.dma("sp", vv_.ap, rows3(src, sb, b, 0, 512)[:, t2a:t2a + TCH, :], vv_.ctr, writes=[vv_])
                                S.dma("pool", gg.ap, rows3(HYC, sb, b, gc0, gc0 + 512)[:, t2a:t2a + TCH, :], gg.ctr, writes=[gg])
                                srbc = srt.ap.unsqueeze(1).to_broadcast([64, TCH, 512])
                                rlbc = rlt.ap.unsqueeze(1).to_broadcast([64, TCH, 512])
                                S.ins("dve", TTO(vs_.ap, vv_.ap, srbc, ALU.mult), [vv_, srt], [vs_])
                                S.ins("dve", TTO(gg.ap, gg.ap, rlbc, ALU.mult), [gg, rlt], [gg])
                                for j in range(TCH):
                                    py = nextbank()
                                    S.ins("pe", MM(py.ap[0:64, :], F2v[:, t2a + j, :], bc.ap[:, j, :], True, True), [F2s, hb0, hb1], [py])
                                    S.ins("dve", TTO(ztmp.ap[:, j, :], py.ap[0:64, :], vs_.ap[:, j, :], ALU.add), [py, vs_], [ztmp])
                                S.ins("dve", TTO(zo.ap, ztmp.ap, gg.ap, ALU.mult), [ztmp, gg], [zo])
                                dstz = Z1 if order == 0 else Z2
                                S.dma("sp", rows3(dstz, sb, b, 0, 512)[:, t2a:t2a + TCH, :], zo.ap, zo.ctr, reads=[zo])
                        S.phase_end()
            S.phase_end()

    if "ffn" in stages:
        with contextlib.ExitStack() as es:
            S.phase_begin()
            fb = ffn_bufs(es)
            ring = [alloc(es, f"ringb{i}", [128, 4096], BF16) for i in range(6)]
            xTs = [(alloc(es, f"xT4_{i}", [128, 8, TT], F32), [Buf(f"xT4{i}_{k}") for k in range(8)]) for i in range(2)]
            hnT = alloc(es, "hnT4", [128, 8, TT], BF16, False)
            hnTb = [Buf(f"hnT4_{k}") for k in range(8)]
            mixA = alloc(es, "mixA", [128, 4, 512], BF16)
            z2t = alloc(es, "z2t", [128, 4, 512], F32)
            z2n = alloc(es, "z2n", [128, 4, 512], BF16, False)
            z2nb = [Buf(f"z2n{k}") for k in range(4)]
            ghy = alloc(es, "ghy", [128, 512], F32)
            ss4 = alloc(es, "ss4", [128, 4], F32, False)
            junk4 = alloc(es, "junk4", [128, 512], F32, False)
            mixT = alloc(es, "mixT", [128, 8, TT], BF16, False)
            mixTb = [Buf(f"mixT{k}") for k in range(8)]
            outT = fb["tmp"]
            outTb = fb["tmpb"]
            ytok = [alloc(es, f"ytok{i}", [128, D], F32) for i in range(2)]
            S.dma("sp", ghy.ap, bcast_rows(hnorm), ghy.ctr, writes=[ghy])
            if "attn" not in stages or "hyena" not in stages:
                S.ins("pool", MSET(z2t.ap, 0.0), (), [z2t])
                S.ins("pool", MSET(mixA.ap, 0.0), (), [mixA])
                for ti in range(NTILES):
                    tok0 = ti * TT
                    if "attn" not in stages:
                        S.dma("sp", MIXA[tok0:tok0 + TT, :].rearrange("(m p) c -> p m c", p=128), mixA.ap, mixA.ctr, reads=[mixA])
                    if "hyena" not in stages:
                        S.dma("sp", Z2[tok0:tok0 + TT, :].rearrange("(m p) c -> p m c", p=128), z2t.ap, z2t.ctr, reads=[z2t])
                S.barrier()
            for ti in range(NTILES):
                tok0 = ti * TT
                s = tile_seq(tok0)
                xT, xTb = xTs[ti % 2]
                S.dma("sp", xT.ap, X1[:, :, tok0:tok0 + TT].rearrange("k p t -> p k t"), xT.ctr, writes=xTb)
                S.dma("sp", mixA.ap, MIXA[tok0:tok0 + TT, :].rearrange("(m p) c -> p m c", p=128), mixA.ctr, writes=[mixA])
                S.dma("sp", z2t.ap, Z2[tok0:tok0 + TT, :].rearrange("(m p) c -> p m c", p=128), z2t.ctr, writes=[z2t])
                for st in range(4):
                    S.ins("dve", STT(junk4.ap, z2t.ap[:, st, :], 1.0, z2t.ap[:, st, :], ALU.mult, ALU.mult, accum_out=ss4.ap[:, st:st + 1]),
                          [z2t], [junk4, ss4])
                S.ins("act", ACT(ss4.ap, ss4.ap, AF.Sqrt, bias=epst.ap[:, 0:1], scale=1.0 / 512), [ss4, epst], [ss4])
                S.ins("dve", RECIP(ss4.ap, ss4.ap), [ss4], [ss4])
                for st in range(4):
                    S.ins("dve", STT(z2n.ap[:, st, :], z2t.ap[:, st, :], ss4.ap[:, st:st + 1], ghy.ap, ALU.mult, ALU.mult),
                          [z2t, ss4, ghy], [z2nb[st]])
                for fc in range(8):
                    pb = nextbank()
                    pbv = pb.ap.bitcast(BF16)
                    for st in range(4):
                        if fc < 4:
                            S.ins("pe", TR(pbv[:, st * 128:(st + 1) * 128], mixA.ap[:, st, fc * 128:(fc + 1) * 128], identb.ap),
                                  [mixA, identb], [pb])
                        else:
                            S.ins("pe", TR(pbv[:, st * 128:(st + 1) * 128], z2n.ap[:, st, (fc - 4) * 128:(fc - 3) * 128], identb.ap),
                                  [z2nb[st], identb], [pb])
                    evac(mixT.ap[:, fc, :], pbv[:, 0:TT], [pb], [mixTb[fc]])
                for oc in range(8):
                    sw = ring_load(ring, Wl["wout"][oc], 1024)
                    swv = sw.ap[:, 0:1024].rearrange("p (k c) -> p k c", c=128)
                    py = nextbank()
                    for kc in range(8):
                        S.ins("pe", MM(py.ap, swv[:, kc, :], mixT.ap[:, kc, :], kc == 0, kc == 7), [sw, mixTb[kc]], [py])
                    S.ins("dve", STT(xT.ap[:, oc, :], py.ap, par.ap[:, s, 5, oc:oc + 1], xT.ap[:, oc, :], ALU.mult, ALU.add),
                          [py, par, xTb[oc]], [xTb[oc]])
                norm_to(fb, xT, xTb, s, 6, 7, hnT, hnTb)
                ffn(fb, ring, hnT, hnTb, xT, xTb, s, 8, "f2g", "f2u", "f2d")
                norm_to(fb, xT, xTb, s, None, None, outT, outTb)
                for st in range(4):
                    yt = ytok[st % 2]
                    for hf in range(2):
                        pb = nextbank()
                        for k4 in range(4):
                            kc = hf * 4 + k4
                            S.ins("pe", TR(pb.ap[:, k4 * 128:(k4 + 1) * 128], outT.ap[:, kc, st * 128:(st + 1) * 128], ident.ap),
                                  [outTb[kc], ident], [pb])
                        evac(yt.ap[:, hf * 512:(hf + 1) * 512], pb.ap, [pb], [yt])
                    S.dma("sp", yout[tok0 + st * 128: tok0 + (st + 1) * 128, :], yt.ap, yt.ctr, reads=[yt])
            S.phase_end()

    S.emit(nc)
    es_all.close()
    return nc


def prep_shared(inp, LB):
    f = lambda a: np.ascontiguousarray(np.asarray(a, dtype=np.float32))
    sh = {}
    sh["w_ada"] = f(inp["w_ada"][0])
    sh["b_adaT"] = f(inp["b_ada"][0].reshape(72, 128).T)
    nrm = np.stack([inp["ffn1_norm"][0], inp["mix_norm"][0], inp["ffn2_norm"][0], inp["final_norm"]], 0)
    sh["nrm"] = f(nrm.reshape(4, 8, 128).transpose(2, 0, 1))
    sh["f1g"] = f(inp["ffn1_w_gate"][0]); sh["f1u"] = f(inp["ffn1_w_up"][0]); sh["f1d"] = f(inp["ffn1_w_down"][0])
    sh["f2g"] = f(inp["ffn2_w_gate"][0]); sh["f2u"] = f(inp["ffn2_w_up"][0]); sh["f2d"] = f(inp["ffn2_w_down"][0])
    sh["win"] = f(inp["w_in"][0]); sh["wout"] = f(inp["w_out"][0])
    rp = np.zeros((8, 15, 128), np.float32)
    rp[:, :, 48:79] = inp["na_rpb"][0]
    sh["rpbp"] = rp
    sh["convw"] = f(inp["hy_conv_w"][0].reshape(1, 3 * 1536))
    sh["convb"] = f(inp["hy_conv_b"][0].reshape(1, 1536))
    sh["hw1"] = f(inp["hy_w1"][0]); sh["hw2"] = f(inp["hy_w2"][0]); sh["hw3"] = f(inp["hy_w3"][0])
    sh["hwo"] = f(inp["hy_wo"][0])
    sh["hb"] = f(np.stack([inp["hy_b1"][0], inp["hy_b2"][0], inp["hy_b3"][0], inp["hy_sin_freq"][0]], 1))
    sh["skipv"] = f(inp["hy_skip"][0].reshape(1, 1024))
    sh["anorm"] = f(inp["attn_out_norm"][0].reshape(1, 512))
    sh["hnorm"] = f(inp["hy_out_norm"][0].reshape(1, 512))
    sh.update(host_consts(LB))
    return sh


DEBUG_OUT = {}


def run_cores(inp, LB, ncores, stages=("ffn", "attn", "hyena"), debug=False):
    xp = np.asarray(inp["x_prompt"], np.float32)
    xs = np.asarray(inp["x_sample"], np.float32)
    cp = np.asarray(inp["c_prompt"], np.float32)
    cs = np.asarray(inp["c_sample"], np.float32)
    sh = prep_shared(inp, LB)
    in_maps = []
    for i in range(ncores):
        m = dict(sh)
        m["xin"] = np.ascontiguousarray(np.concatenate([xp[2 * i], xp[2 * i + 1], xs[i]], 0))
        c3 = np.stack([cp[2 * i], cp[2 * i + 1], cs[i]], 0)
        m["cT"] = np.ascontiguousarray(c3.reshape(3, 8, 128).transpose(2, 1, 0))
        in_maps.append(m)
    nc = build_program(LB, stages, debug)
    res = run_bass_kernel_spmd(nc, in_maps, core_ids=list(range(ncores)))
    if debug:
        DEBUG_OUT.update(res.results[0])
    yp = np.zeros_like(xp)
    ys = np.zeros_like(xs)
    for i in range(ncores):
        y = res.results[i]["yout"]
        yp[2 * i] = y[0:LB]
        yp[2 * i + 1] = y[LB:2 * LB]
        ys[i] = y[2 * LB:4 * LB]
    return yp, ys


def kernel(**inputs):
    yp, ys = run_cores(inputs, 4096, 8)
    return (yp, ys)
```
